# Optimizing a Trainium2 kernel written in Bass

```python
import math
import jax, jax.numpy as jnp
from jax import lax
import numpy as np

D_MODEL = 1024
BATCH = 16
SEQ = 4096
DEPTH = 2
DEC_BATCH = 32
DEC_SEQ = 16
PAST_LEN = 1024

CHUNK = 64
D_MIX = 2 * D_MODEL
HEAD_DIM = 64
D_A = 3 * D_MIX // 8
H_A = D_A // HEAD_DIM
R_W = 64
R_A = 64
GN_EPS = 64e-5
D_B = 3 * D_MIX // 8
H_B = D_B // HEAD_DIM
N_GROUPS = 2
HPG = H_B // N_GROUPS
D_STATE = 128
CONV_W = 4
CONV_DIM = D_B + 2 * N_GROUPS * D_STATE
D_C = D_MIX - D_A - D_B
H_C = D_C // HEAD_DIM
SB_BLOCK = 128
W_SHIFT = 3 * D_A + R_W + R_A
O1 = W_SHIFT
O2 = O1 + D_A
O3 = O2 + D_B
O4 = O3 + CONV_DIM
O5 = O4 + H_B
O6 = O5 + D_C
O7 = O6 + D_C
O8 = O7 + D_C
N_IN = O8 + D_C
IN_SPLITS = (O1, O2, O3, O4, O5, O6, O7, O8)

kernel_name = 'hybrid_rwkv7_mamba2_stickbreaking_stream_step'


def rms_norm(x, w, eps=1e-6):
    xf = x.astype(jnp.float32)
    y = xf * lax.rsqrt(jnp.mean(xf * xf, axis=-1, keepdims=True) + eps)
    return (y * w.astype(jnp.float32)).astype(x.dtype)


def token_shift(u, prev, mu):
    u_prev = jnp.concatenate([prev.astype(u.dtype), u[:, :-1]], axis=1)
    return u + (u_prev - u) * mu, u[:, -1:]


def rwkv7_branch(ua, gate, S0, shift0, p):
    B, T, _ = ua.shape
    f32 = jnp.float32
    us, new_shift = token_shift(ua, shift0, p['rwkv_mu'])
    r, k, v, w_lo, a_lo = jnp.split(us, [D_A, 2 * D_A, 3 * D_A, 3 * D_A + R_W], axis=-1)
    w_log = -jax.nn.softplus(-(p['rwkv_w0'] + jnp.tanh(w_lo) @ p['rwkv_w2']).astype(f32)) - 0.5
    decay = jnp.exp(-jnp.exp(w_log))
    a = jax.nn.sigmoid((p['rwkv_a0'] + a_lo @ p['rwkv_a2']).astype(f32))
    heads = lambda t: t.astype(f32).reshape(B, T, H_A, HEAD_DIM)
    r, k, v, decay, a = heads(r), heads(k), heads(v), heads(decay), heads(a)
    kk = k * p['rwkv_k_k'].astype(f32).reshape(H_A, HEAD_DIM)
    kk = kk / jnp.maximum(jnp.sqrt(jnp.sum(kk * kk, axis=-1, keepdims=True)), 1e-12)
    k = k * (1.0 + (a - 1.0) * p['rwkv_k_a'].astype(f32).reshape(H_A, HEAD_DIM))

    def step(S, inp):
        r_t, w_t, k_t, v_t, rm_t, ad_t = inp
        sa = jnp.einsum('bhij,bhj->bhi', S, rm_t)
        S = S * w_t[:, :, None, :] + sa[..., None] * ad_t[:, :, None, :] + v_t[..., None] * k_t[:, :, None, :]
        return S, jnp.einsum('bhij,bhj->bhi', S, r_t)

    seq = tuple(t.swapaxes(0, 1) for t in (r, decay, k, v, -kk, kk * a))
    S_T, y = lax.scan(step, S0.astype(f32), seq)
    y = y.swapaxes(0, 1)
    mean = jnp.mean(y, axis=-1, keepdims=True)
    var = jnp.mean(jnp.square(y - mean), axis=-1, keepdims=True)
    y = (y - mean) * lax.rsqrt(var + GN_EPS) * p['rwkv_ln_w'].astype(f32).reshape(H_A, HEAD_DIM) \
        + p['rwkv_ln_b'].astype(f32).reshape(H_A, HEAD_DIM)
    y = y + jnp.sum(r * k * p['rwkv_r_k'].astype(f32), axis=-1, keepdims=True) * v
    out = y.reshape(B, T, D_A) * jax.nn.silu(gate.astype(f32))
    return out.astype(ua.dtype), S_T.astype(S0.dtype), new_shift.astype(shift0.dtype)


def ssd_chunked(x, dt, A, Bm, Cm, S0):
    Bsz, T = x.shape[:2]
    Q = math.gcd(T, CHUNK)
    nC = T // Q
    ch = lambda t: t.reshape(Bsz, nC, Q, *t.shape[2:])
    x, dt, Bm, Cm = ch(x), ch(dt), ch(Bm), ch(Cm)
    xdt = x * dt[..., None]
    a_cs = jnp.cumsum(dt * A, axis=2).transpose(0, 1, 3, 4, 2)
    tril = jnp.tril(jnp.ones((Q, Q), dtype=bool))
    seg = jnp.exp(jnp.where(tril, a_cs[..., :, None] - a_cs[..., None, :], -jnp.inf))
    scores = jnp.einsum('bcqgn,bcsgn->bcgqs', Cm, Bm)[:, :, :, None] * seg
    y_diag = jnp.einsum('bcgmqs,bcsgmp->bcqgmp', scores, xdt)
    decay_to_end = jnp.exp(a_cs[..., -1:] - a_cs)
    chunk_states = jnp.einsum('bcsgn,bcgms,bcsgmp->bcgmpn', Bm, decay_to_end, xdt)
    chunk_decay = jnp.exp(a_cs[..., -1])

    def step(S, inp):
        st, dec = inp
        return S * dec[..., None, None] + st, S

    S_T, S_in = lax.scan(step, S0, (chunk_states.swapaxes(0, 1), chunk_decay.swapaxes(0, 1)))
    S_in = S_in.swapaxes(0, 1)
    y_off = jnp.einsum('bcqgn,bcgmpn,bcgmq->bcqgmp', Cm, S_in, jnp.exp(a_cs))
    return (y_diag + y_off).reshape(Bsz, T, *x.shape[3:]), S_T


def mamba2_branch(z, xbc, dt_raw, S0, conv0, p):
    B, T, _ = xbc.shape
    f32 = jnp.float32
    xpad = jnp.concatenate([conv0.astype(xbc.dtype), xbc], axis=1)
    new_conv = xpad[:, T:]
    cw = p['ssm_conv_w'].astype(f32)
    conv = sum((xpad[:, i:i + T].astype(f32) * cw[i] for i in range(CONV_W)), start=p['ssm_conv_b'].astype(f32))
    xbc = jax.nn.silu(conv)
    xs, Bm, Cm = jnp.split(xbc, [D_B, D_B + N_GROUPS * D_STATE], axis=-1)
    xs = xs.reshape(B, T, N_GROUPS, HPG, HEAD_DIM)
    Bm = Bm.reshape(B, T, N_GROUPS, D_STATE)
    Cm = Cm.reshape(B, T, N_GROUPS, D_STATE)
    dt = jax.nn.softplus(dt_raw.astype(f32) + p['ssm_dt_bias'].astype(f32)).reshape(B, T, N_GROUPS, HPG)
    A = -jnp.exp(p['ssm_A_log'].astype(f32)).reshape(N_GROUPS, HPG)
    S0g = S0.astype(f32).reshape(B, N_GROUPS, HPG, HEAD_DIM, D_STATE)
    y, S_T = ssd_chunked(xs, dt, A, Bm, Cm, S0g)
    y = y + p['ssm_D'].astype(f32).reshape(N_GROUPS, HPG, 1) * xs
    y = y.reshape(B, T, D_B) * jax.nn.silu(z.astype(f32))
    y = rms_norm(y.reshape(B, T, N_GROUPS, D_B // N_GROUPS),
                 p['ssm_norm_w'].reshape(N_GROUPS, D_B // N_GROUPS), 1e-5).reshape(B, T, D_B)
    return y.astype(z.dtype), S_T.reshape(B, H_B, HEAD_DIM, D_STATE).astype(S0.dtype), new_conv.astype(conv0.dtype)


def stick_breaking_block(q, k, v, q_start):
    f32 = jnp.float32
    z = jnp.einsum('bhqd,bhkd->bhqk', q.astype(f32), k.astype(f32)) * (HEAD_DIM ** -0.5)
    q_pos = q_start + jnp.arange(q.shape[2])
    k_pos = jnp.arange(k.shape[2])
    causal = k_pos[None, :] < q_pos[:, None]
    log_keep = jnp.where(causal, jax.nn.log_sigmoid(-z), 0.0)
    log_tail = lax.cumsum(log_keep, axis=3, reverse=True) - log_keep
    att = jnp.where(causal, jnp.exp(jax.nn.log_sigmoid(z) + log_tail), 0.0)
    return jnp.einsum('bhqk,bhkd->bhqd', att, v.astype(f32))


def stick_breaking_branch(q, k, v, gate, k_cache, v_cache, p):
    B, T, _ = q.shape
    heads = lambda t: t.reshape(B, T, H_C, HEAD_DIM).transpose(0, 2, 1, 3)
    q = rms_norm(heads(q), p['sb_q_norm_w'])
    k = rms_norm(heads(k), p['sb_k_norm_w'])
    v = heads(v)
    P = k_cache.shape[2]
    k_all = jnp.concatenate([k_cache.astype(k.dtype), k], axis=2)
    v_all = jnp.concatenate([v_cache.astype(v.dtype), v], axis=2)
    outs = []
    for start in range(0, T, SB_BLOCK):
        end = min(T, start + SB_BLOCK)
        outs.append(stick_breaking_block(q[:, :, start:end], k_all[:, :, :P + end], v_all[:, :, :P + end], P + start))
    o = jnp.concatenate(outs, axis=2).transpose(0, 2, 1, 3).reshape(B, T, D_C)
    o = o * jax.nn.silu(gate.astype(jnp.float32))
    return o.astype(gate.dtype), k, v


def hybrid_layer(x, p, S_rwkv, shift, S_ssm, conv_buf, k_cache, v_cache):
    h = rms_norm(x, p['norm_w'])
    u = h @ p['w_in']
    ua, ga, zb, xbc, dtb, qc, kc, vc, gc = jnp.split(u, IN_SPLITS, axis=-1)
    oa, S_rwkv, shift = rwkv7_branch(ua, ga, S_rwkv, shift, p)
    ob, S_ssm, conv_buf = mamba2_branch(zb, xbc, dtb, S_ssm, conv_buf, p)
    oc, k_new, v_new = stick_breaking_branch(qc, kc, vc, gc, k_cache, v_cache, p)
    y = x + (jnp.concatenate([oa, ob, oc], axis=-1) @ p['w_out']).astype(x.dtype)
    return y, (S_rwkv, shift, S_ssm, conv_buf, k_new, v_new)


def setup_inputs(seed: int = 0) -> dict:
    key = jax.random.key(seed)
    ks = iter(jax.random.split(key, 40))
    f32 = jnp.float32
    nrm = lambda shape, s: s * jax.random.normal(next(ks), shape, f32)
    uni = lambda shape, lo, hi: jax.random.uniform(next(ks), shape, f32, minval=lo, maxval=hi)
    x_prompt = nrm((BATCH, SEQ, D_MODEL), 1.0)
    x_sample = nrm((DEC_BATCH, DEC_SEQ, D_MODEL), 1.0)
    state_rwkv = nrm((DEPTH, DEC_BATCH, H_A, HEAD_DIM, HEAD_DIM), 0.5)
    state_rwkv_shift = nrm((DEPTH, DEC_BATCH, 1, W_SHIFT), 1.0)
    state_ssm = nrm((DEPTH, DEC_BATCH, H_B, HEAD_DIM, D_STATE), 0.1)
    state_conv = nrm((DEPTH, DEC_BATCH, CONV_W - 1, CONV_DIM), 1.0)
    cache_sb_k = nrm((DEPTH, DEC_BATCH, H_C, PAST_LEN, HEAD_DIM), 1.0)
    cache_sb_v = nrm((DEPTH, DEC_BATCH, H_C, PAST_LEN, HEAD_DIM), 1.0)
    norm_w = 1.0 + nrm((DEPTH, D_MODEL), 0.02)
    w_in = nrm((DEPTH, D_MODEL, N_IN), D_MODEL ** -0.5)
    w_out = nrm((DEPTH, D_MIX, D_MODEL), 0.5 * D_MIX ** -0.5)
    rwkv_mu = uni((DEPTH, W_SHIFT), 0.0, 1.0)
    rwkv_w0 = uni((DEPTH, D_A), -4.0, 1.0)
    rwkv_w2 = nrm((DEPTH, R_W, D_A), 0.5 * R_W ** -0.5)
    rwkv_a0 = nrm((DEPTH, D_A), 0.5)
    rwkv_a2 = nrm((DEPTH, R_A, D_A), 0.5 * R_A ** -0.5)
    rwkv_k_k = 0.85 + nrm((DEPTH, D_A), 0.02)
    rwkv_k_a = 1.0 + nrm((DEPTH, D_A), 0.02)
    rwkv_r_k = nrm((DEPTH, H_A, HEAD_DIM), 0.1)
    rwkv_ln_w = 1.0 + nrm((DEPTH, D_A), 0.02)
    rwkv_ln_b = nrm((DEPTH, D_A), 0.02)
    ssm_conv_w = nrm((DEPTH, CONV_W, CONV_DIM), CONV_W ** -0.5)
    ssm_conv_b = nrm((DEPTH, CONV_DIM), 0.02)
    dt0 = jnp.exp(uni((DEPTH, H_B), math.log(1e-3), math.log(1e-1)))
    ssm_dt_bias = dt0 + jnp.log(-jnp.expm1(-dt0))
    ssm_A_log = jnp.log(uni((DEPTH, H_B), 1.0, 16.0))
    ssm_D = 1.0 + nrm((DEPTH, H_B), 0.1)
    ssm_norm_w = 1.0 + nrm((DEPTH, D_B), 0.02)
    sb_q_norm_w = 1.0 + nrm((DEPTH, HEAD_DIM), 0.02)
    sb_k_norm_w = 1.0 + nrm((DEPTH, HEAD_DIM), 0.02)
    return {'x_prompt': x_prompt, 'x_sample': x_sample,
            'state_rwkv': state_rwkv, 'state_rwkv_shift': state_rwkv_shift,
            'state_ssm': state_ssm, 'state_conv': state_conv,
            'cache_sb_k': cache_sb_k, 'cache_sb_v': cache_sb_v,
            'norm_w': norm_w, 'w_in': w_in, 'w_out': w_out,
            'rwkv_mu': rwkv_mu, 'rwkv_w0': rwkv_w0, 'rwkv_w2': rwkv_w2, 'rwkv_a0': rwkv_a0, 'rwkv_a2': rwkv_a2,
            'rwkv_k_k': rwkv_k_k, 'rwkv_k_a': rwkv_k_a, 'rwkv_r_k': rwkv_r_k,
            'rwkv_ln_w': rwkv_ln_w, 'rwkv_ln_b': rwkv_ln_b,
            'ssm_conv_w': ssm_conv_w, 'ssm_conv_b': ssm_conv_b, 'ssm_dt_bias': ssm_dt_bias,
            'ssm_A_log': ssm_A_log, 'ssm_D': ssm_D, 'ssm_norm_w': ssm_norm_w,
            'sb_q_norm_w': sb_q_norm_w, 'sb_k_norm_w': sb_k_norm_w}


def reference(x_prompt, x_sample, state_rwkv, state_rwkv_shift, state_ssm, state_conv, cache_sb_k, cache_sb_v,
              norm_w, w_in, w_out, rwkv_mu, rwkv_w0, rwkv_w2, rwkv_a0, rwkv_a2, rwkv_k_k, rwkv_k_a, rwkv_r_k,
              rwkv_ln_w, rwkv_ln_b, ssm_conv_w, ssm_conv_b, ssm_dt_bias, ssm_A_log, ssm_D, ssm_norm_w,
              sb_q_norm_w, sb_k_norm_w):
    bp, dtp = x_prompt.shape[0], x_prompt.dtype
    zero_rwkv = jnp.zeros((bp, H_A, HEAD_DIM, HEAD_DIM), dtp)
    zero_shift = jnp.zeros((bp, 1, W_SHIFT), dtp)
    zero_ssm = jnp.zeros((bp, H_B, HEAD_DIM, D_STATE), dtp)
    zero_conv = jnp.zeros((bp, CONV_W - 1, CONV_DIM), dtp)
    empty_kv = jnp.zeros((bp, H_C, 0, HEAD_DIM), dtp)

    yp, ys = x_prompt, x_sample
    new_p = [[] for _ in range(6)]
    new_s = [[] for _ in range(6)]
    for l in range(DEPTH):
        p = {'norm_w': norm_w[l], 'w_in': w_in[l], 'w_out': w_out[l],
             'rwkv_mu': rwkv_mu[l], 'rwkv_w0': rwkv_w0[l], 'rwkv_w2': rwkv_w2[l], 'rwkv_a0': rwkv_a0[l],
             'rwkv_a2': rwkv_a2[l], 'rwkv_k_k': rwkv_k_k[l], 'rwkv_k_a': rwkv_k_a[l], 'rwkv_r_k': rwkv_r_k[l],
             'rwkv_ln_w': rwkv_ln_w[l], 'rwkv_ln_b': rwkv_ln_b[l],
             'ssm_conv_w': ssm_conv_w[l], 'ssm_conv_b': ssm_conv_b[l], 'ssm_dt_bias': ssm_dt_bias[l],
             'ssm_A_log': ssm_A_log[l], 'ssm_D': ssm_D[l], 'ssm_norm_w': ssm_norm_w[l],
             'sb_q_norm_w': sb_q_norm_w[l], 'sb_k_norm_w': sb_k_norm_w[l]}
        yp, st_p = hybrid_layer(yp, p, zero_rwkv, zero_shift, zero_ssm, zero_conv, empty_kv, empty_kv)
        ys, st_s = hybrid_layer(ys, p, state_rwkv[l], state_rwkv_shift[l], state_ssm[l], state_conv[l],
                                cache_sb_k[l], cache_sb_v[l])
        for i in range(6):
            new_p[i].append(st_p[i])
            new_s[i].append(st_s[i])
    p_rwkv, p_shift, p_ssm, p_conv, p_k, p_v = [jnp.stack(t) for t in new_p]
    s_rwkv, s_shift, s_ssm, s_conv, s_k, s_v = [jnp.stack(t) for t in new_s]
    return (yp, ys, p_rwkv, p_shift, p_ssm, p_conv, p_k, p_v, s_rwkv, s_shift, s_ssm, s_conv, s_k, s_v)
```

```python
import numpy as np
import concourse.bass as bass
import concourse.mybir as mybir
from concourse.bass_utils import run_bass_kernel_spmd
from contextlib import ExitStack

F32 = mybir.dt.float32
BF16 = mybir.dt.bfloat16
AF = mybir.ActivationFunctionType
ALU = mybir.AluOpType
AX = mybir.AxisListType

ENGS = ("pe", "act", "dve", "pool", "sp")
NSLOT = 8


class Res:
    _n = 0

    def __init__(self, name=""):
        Res._n += 1
        self.id = Res._n
        self.name = name


class Tl(Res):
    def __init__(self, name, h):
        super().__init__(name)
        self.h = h

    def __getitem__(self, k):
        return self.h[k]


class Sched:
    def __init__(self, nc):
        self.nc = nc
        self.ops = []
        self.state = {}
        self.ndma = {e: 0 for e in ENGS}

    @staticmethod
    def _flat(lst):
        out = []
        for r in lst:
            if hasattr(r, "regions"):
                out.extend(r.regions)
            else:
                out.append(r)
        return out

    def _deps(self, idx, reads, writes, eng, is_dma):
        deps = set()
        reads = self._flat(reads)
        writes = self._flat(writes)
        rk = "dma%d" % idx if is_dma else eng
        for r in reads:
            rid, key = (r[0].id, r[1]) if isinstance(r, tuple) else (r.id, None)
            st = self.state.setdefault(rid, {})
            ks = list(st.keys()) if key is None else [k for k in (key, None) if k in st]
            for k in ks:
                if st[k][0] is not None:
                    deps.add(st[k][0])
            ent = st.setdefault(key, [None, {}])
            ent[1][rk] = idx
        for w in writes:
            rid, key = (w[0].id, w[1]) if isinstance(w, tuple) else (w.id, None)
            st = self.state.setdefault(rid, {})
            ks = list(st.keys()) if key is None else [k for k in (key, None) if k in st]
            for k in ks:
                if st[k][0] is not None:
                    deps.add(st[k][0])
                deps.update(st[k][1].values())
            if key is None:
                st.clear()
            st[key] = [idx, {}]
        deps.discard(idx)
        return deps

    def op(self, eng, emit, reads=(), writes=()):
        idx = len(self.ops)
        deps = self._deps(idx, reads, writes, eng, False)
        self.ops.append(dict(eng=eng, emit=emit, deps=deps, dma=False, sig=False))
        return idx

    def dma(self, eng, out, in_, reads=(), writes=(), **kw):
        idx = len(self.ops)
        deps = self._deps(idx, reads, writes, eng, True)
        n = self.ndma[eng]
        self.ndma[eng] += 1
        self.ops.append(dict(eng=eng, emit=lambda e: e.dma_start(out=out, in_=in_, **kw), deps=deps,
                             dma=True, sig=True, slot=n % NSLOT, val=16 * (n // NSLOT + 1)))
        return idx

    def finalize(self, stack):
        nc = self.nc
        ops = self.ops
        esem = {e: stack.enter_context(nc.semaphore("es_" + e)) for e in ENGS}
        dsem = {e: [stack.enter_context(nc.semaphore("ds_%s%d" % (e, i))) for i in range(NSLOT)]
                for e in ENGS if self.ndma[e] > 0}
        for i, o in enumerate(ops):
            for j in o["deps"]:
                pj = ops[j]
                if pj["dma"]:
                    continue
                if pj["eng"] == "pe" and o["eng"] == "pe":
                    continue
                pj["sig"] = True
        cnt = {e: 0 for e in ENGS}
        for o in ops:
            if o["dma"]:
                o["sem"] = dsem[o["eng"]][o["slot"]]
            elif o["sig"]:
                cnt[o["eng"]] += 1
                o["sem"] = esem[o["eng"]]
                o["val"] = cnt[o["eng"]]
        byeng = {e: [] for e in ENGS}
        for i, o in enumerate(ops):
            byeng[o["eng"]].append(i)

        def run(e, name):
            waited = {}
            for i in byeng[name]:
                o = ops[i]
                need = {}
                for j in o["deps"]:
                    pj = ops[j]
                    if (not pj["dma"]) and pj["eng"] == "pe" and name == "pe":
                        continue
                    s = pj["sem"]
                    if need.get(s, 0) < pj["val"]:
                        need[s] = pj["val"]
                if o["dma"] and o["val"] > 16:
                    s = o["sem"]
                    if need.get(s, 0) < o["val"] - 16:
                        need[s] = o["val"] - 16
                for s, v in need.items():
                    if waited.get(s, 0) < v:
                        e.wait_ge(s, v)
                        waited[s] = v
                ins = o["emit"](e)
                if o["dma"]:
                    ins.then_inc(o["sem"], 16)
                elif o["sig"]:
                    ins.then_inc(o["sem"], 1)
            if name in dsem:
                n = self.ndma[name]
                for s in range(NSLOT):
                    k = (n - s + NSLOT - 1) // NSLOT
                    if k > 0 and waited.get(dsem[name][s], 0) < 16 * k:
                        e.wait_ge(dsem[name][s], 16 * k)

        with nc.Block() as block:
            @block.tensor
            def _(e):
                run(e, "pe")

            @block.scalar
            def _(e):
                run(e, "act")

            @block.vector
            def _(e):
                run(e, "dve")

            @block.gpsimd
            def _(e):
                run(e, "pool")

            @block.sync
            def _(e):
                run(e, "sp")


D = 1024
DEPTH = 2
HD = 64
D_A = 768
D_B = 768
D_C = 512
H_A = 12
H_B = 12
H_C = 8
NST = 128
CONV_DIM = 1280
W_SHIFT = 2432
N_IN = 7308
GN_EPS = 64e-5
PAST = 1024
P_WA = 0
CT_ZB, CT_XS, CT_B, CT_C, CT_DT, CT_Q, CT_SK, CT_SV, CT_GC = 25, 31, 37, 39, 41, 42, 46, 50, 54


def P_R(m):
    return 1 + 4 * m


def P_K(m):
    return 2 + 4 * m


def P_V(m):
    return 3 + 4 * m


def P_GA(m):
    return 4 + 4 * m
NCT = 58
NP_ = NCT * 128


def col_map():
    m = -np.ones(NP_, np.int64)

    def put(pos, c0, n):
        m[pos * 128:pos * 128 + n] = np.arange(c0, c0 + n)
    put(P_WA, 2304, 128)
    for mm in range(6):
        put(P_R(mm), mm * 128, 128)
        put(P_K(mm), 768 + mm * 128, 128)
        put(P_V(mm), 1536 + mm * 128, 128)
        put(P_GA(mm), 2432 + mm * 128, 128)
    put(CT_ZB, 3200, 768)
    put(CT_XS, 3968, 1280)
    put(CT_DT, 5248, 12)
    put(CT_Q, 5260, 2048)
    return m


def param_layout():
    off = {}
    n = 0
    for name, k in (("normw", 8), ("mu", 19), ("w0", 6), ("a0", 6), ("k_k", 6), ("k_a", 6), ("r_k", 6),
                    ("ln_w", 6), ("ln_b", 6), ("cw0", 10), ("cw1", 10), ("cw2", 10), ("cw3", 10), ("cb", 10),
                    ("dtb", 1), ("alog", 1), ("Dv", 6), ("snw", 6), ("qnw", 1), ("knw", 1)):
        off[name] = n
        n += k
    return off, n


POFF, NPAR = param_layout()


def fm(v, ntile):
    return np.ascontiguousarray(np.asarray(v, np.float32).reshape(ntile, 128).T)


def build_params(inp, l):
    P = np.zeros((128, NPAR), np.float32)

    def put(name, arr):
        P[:, POFF[name]:POFF[name] + arr.shape[1]] = arr
    put("normw", fm(inp["norm_w"][l], 8))
    put("mu", fm(inp["rwkv_mu"][l], 19))
    for nm, key in (("w0", "rwkv_w0"), ("a0", "rwkv_a0"), ("k_k", "rwkv_k_k"), ("k_a", "rwkv_k_a"),
                    ("ln_w", "rwkv_ln_w"), ("ln_b", "rwkv_ln_b"), ("snw", "ssm_norm_w")):
        put(nm, fm(inp[key][l], 6))
    put("r_k", fm(np.asarray(inp["rwkv_r_k"][l]).reshape(-1), 6))
    for i in range(4):
        put("cw%d" % i, fm(inp["ssm_conv_w"][l][i], 10))
    put("cb", fm(inp["ssm_conv_b"][l], 10))
    dtb = np.zeros(128, np.float32)
    dtb[:12] = inp["ssm_dt_bias"][l]
    put("dtb", dtb[:, None])
    al = np.zeros(128, np.float32)
    al[:12] = inp["ssm_A_log"][l]
    put("alog", al[:, None])
    put("Dv", fm(np.repeat(np.asarray(inp["ssm_D"][l]), 64), 6))
    put("qnw", np.tile(np.asarray(inp["sb_q_norm_w"][l]), 2)[:, None])
    put("knw", np.tile(np.asarray(inp["sb_k_norm_w"][l]), 2)[:, None])
    return P


def make_consts():
    c = {}
    i = np.arange(128)
    c["ident"] = np.eye(128, dtype=np.float32)
    c["blk64"] = (i[:, None] // 64 == i[None, :] // 64).astype(np.float32)
    c["ones"] = np.ones((128, 128), np.float32)
    c["mlt"] = (i[:, None] < i[None, :]).astype(np.float32)
    c["mle"] = (i[:, None] <= i[None, :]).astype(np.float32)
    c["mgt"] = (i[:, None] > i[None, :]).astype(np.float32)
    c["nmle"] = np.where(i[:, None] <= i[None, :], 0.0, -30000.0).astype(np.float32)
    c["nuinc"] = -(i[:, None] >= i[None, :]).astype(np.float32)
    c["nones"] = -np.ones((128, 128), np.float32)
    e12 = np.zeros((128, 768), np.float32)
    for h in range(12):
        e12[h, h * 64:(h + 1) * 64] = 1.0
    c["e12"] = e12
    sel = np.zeros((128, 12 * 128), np.float32)
    for h in range(12):
        sel[h, h * 128:(h + 1) * 128] = 1.0
    c["sel12"] = sel
    for L in (64, 16):
        sl = np.zeros((128, 128), np.float32)
        sl[L - 1, :] = 1.0
        c["sell%d" % L] = sl
        rm = np.ones((128, 512 // L, L), np.float32)
        rm[:, :, 0] = 0.0
        c["rm%d" % L] = rm.reshape(128, 512)
    return c


CONST_SHAPES = {k: v.shape for k, v in make_consts().items()}


import os
KSEQ = os.environ.get('KSEQ', '')


class Stop(Exception):
    pass


class Cfg:
    def __init__(self, nseq_p=2, t_p=4096, nseq_s=4, debug=False, do_sample=True, nlayers=DEPTH):
        self.nseq_p = nseq_p
        self.t_p = t_p
        self.nseq_s = nseq_s
        self.ts = 16
        self.debug = debug
        self.do_sample = do_sample
        self.nlayers = nlayers
        import os
        self.stop = os.environ.get('KSTOP', '')


NREG = 50


def build(cfg):
    nc = bass.Bass("TRN2", target_bir_lowering=False)
    S = Sched(nc)
    stack = ExitStack()
    dbg_outs = []

    def dram(name, shape, dt, kind):
        return Tl(name, nc.dram_tensor(name, list(shape), dt, kind=kind).ap())

    def sb(name, shape, dt=F32):
        return Tl(name, stack.enter_context(nc.sbuf_tensor("s_" + name, list(shape), dt)))

    NSP, TP, NSS, TSS = cfg.nseq_p, cfg.t_p, cfg.nseq_s, cfg.ts
    NTOK_P = NSP * TP
    NTOK_S = NSS * TSS
    xp = dram("xp", [NTOK_P, D], F32, "ExternalInput")
    xs = dram("xs", [NTOK_S, D], F32, "ExternalInput")
    w_in = dram("w_in", [DEPTH, 128, 8, NP_], F32, "ExternalInput")
    w_out = dram("w_out", [DEPTH, 128, 16, D], F32, "ExternalInput")
    w2a2 = dram("w2a2", [DEPTH, 128, 768], F32, "ExternalInput")
    params = dram("params", [DEPTH, 128, NPAR], F32, "ExternalInput")
    cd = {k: dram("c_" + k, list(shp), F32, "ExternalInput") for k, shp in CONST_SHAPES.items()}
    st_rwkv = dram("st_rwkv", [DEPTH, NSS, 12, 64, 64], F32, "ExternalInput")
    st_shift = dram("st_shift", [DEPTH, NSS, W_SHIFT], F32, "ExternalInput")
    st_ssm = dram("st_ssm", [DEPTH, NSS, 12, 64, 128], F32, "ExternalInput")
    st_conv = dram("st_conv", [DEPTH, NSS, 3, CONV_DIM], F32, "ExternalInput")
    ck_d = dram("ck", [DEPTH, NSS, 8, PAST, 64], F32, "ExternalInput")
    cv_d = dram("cv", [DEPTH, NSS, 8, PAST, 64], F32, "ExternalInput")
    yp = dram("yp", [NTOK_P, D], F32, "ExternalOutput")
    ys = dram("ys", [NTOK_S, D], F32, "ExternalOutput")
    o_rwkv = {"p": dram("p_rwkv", [DEPTH, NSP, 12, 64, 64], F32, "ExternalOutput"),
              "s": dram("s_rwkv", [DEPTH, NSS, 12, 64, 64], F32, "ExternalOutput")}
    o_shift = {"p": dram("p_shift", [DEPTH, NSP, W_SHIFT], F32, "ExternalOutput"),
               "s": dram("s_shift", [DEPTH, NSS, W_SHIFT], F32, "ExternalOutput")}
    o_ssm = {"p": dram("p_ssm", [DEPTH, NSP, 12, 64, 128], F32, "ExternalOutput"),
             "s": dram("s_ssm", [DEPTH, NSS, 12, 64, 128], F32, "ExternalOutput")}
    o_conv = {"p": dram("p_conv", [DEPTH, NSP, 3, CONV_DIM], F32, "ExternalOutput"),
              "s": dram("s_conv", [DEPTH, NSS, 3, CONV_DIM], F32, "ExternalOutput")}
    o_k = {"p": dram("p_k", [DEPTH, NSP, 8, TP, 64], F32, "ExternalOutput"),
           "s": dram("s_k", [DEPTH, NSS, 8, TSS, 64], F32, "ExternalOutput")}
    o_v = {"p": dram("p_v", [DEPTH, NSP, 8, TP, 64], F32, "ExternalOutput"),
           "s": dram("s_v", [DEPTH, NSS, 8, TSS, 64], F32, "ExternalOutput")}
    w_in_bf = dram("w_in_bf", [DEPTH, 128, 8, NP_], BF16, "Internal")
    w_out_bf = dram("w_out_bf", [DEPTH, 128, 16, D], BF16, "Internal")
    xmid = dram("xmid", [NTOK_P, D], F32, "Internal")
    xmid_s = dram("xmid_s", [NTOK_S, D], F32, "Internal")

    def dbg(name, rd, ap, shape, dt=F32):
        if not cfg.debug:
            return
        o = dram("dbg_" + name, shape, dt, "ExternalOutput")
        S.dma("pool", o.h, ap, reads=[rd], writes=[o])
        dbg_outs.append("dbg_" + name)

    def ACT(out, in_, func, reads, writes, bias=None, scale=None, accum=None):
        kw = {}
        if bias is not None:
            kw["bias"] = bias
        if scale is not None:
            kw["scale"] = scale
        if accum is not None:
            kw["accum_out"] = accum
        S.op("act", lambda e: e.activation(out, in_, func, **kw), reads=reads, writes=writes)

    def TS(out, in0, s1, s2, op0, op1, reads, writes, eng="dve"):
        if s2 is None:
            S.op(eng, lambda e: e.tensor_scalar(out, in0, s1, None, op0), reads=reads, writes=writes)
        else:
            S.op(eng, lambda e: e.tensor_scalar(out, in0, s1, s2, op0, op1), reads=reads, writes=writes)

    def TT(out, in0, in1, op, reads, writes, eng="dve"):
        S.op(eng, lambda e: e.tensor_tensor(out, in0, in1, op), reads=reads, writes=writes)

    def STT(out, in0, sc, in1, op0, op1, reads, writes):
        S.op("dve", lambda e: e.scalar_tensor_tensor(out, in0, sc, in1, op0, op1), reads=reads, writes=writes)

    def MM(out, lhsT, rhs, start, stop, reads, writes, skip=False):
        if skip:
            S.op("pe", lambda e: e.matmul(out, lhsT, rhs, start=start, stop=stop, skip_group_check=True),
                 reads=reads, writes=writes)
        else:
            S.op("pe", lambda e: e.matmul(out, lhsT, rhs, start=start, stop=stop), reads=reads, writes=writes)

    def TR(out, in_, idn, reads, writes):
        S.op("pe", lambda e: e.transpose(out, in_, idn), reads=reads, writes=writes)

    def CP(out, in_, reads, writes, eng="act"):
        if eng == "act":
            S.op("act", lambda e: e.copy(out, in_), reads=reads, writes=writes)
        else:
            S.op(eng, lambda e: e.tensor_copy(out, in_), reads=reads, writes=writes)

    C = {}
    for k, shp in CONST_SHAPES.items():
        C[k] = sb("c_" + k, list(shp))
        S.dma("sp", C[k][:], cd[k].h, writes=[C[k]])
    Cb = {}
    for k in ("ident", "mlt", "nuinc", "nones"):
        Cb[k] = sb("cb_" + k, [128, 128], BF16)
        S.op("dve", lambda e, k=k: e.tensor_copy(Cb[k][:], C[k][:]), reads=[C[k]], writes=[Cb[k]])
    ident, ident_bf = C["ident"], Cb["ident"]
    par = [sb("par%d" % l, [128, NPAR]) for l in range(DEPTH)]
    omm = [sb("omm%d" % l, [128, 19]) for l in range(DEPTH)]
    aneg = [sb("aneg%d" % l, [128, 1]) for l in range(DEPTH)]
    w2a2f = sb("w2a2f", [128, 768])
    w2a2b = [sb("w2a2b%d" % l, [128, 768], BF16) for l in range(DEPTH)]
    for l in range(DEPTH):
        S.dma("sp", par[l][:], params.h[l], writes=[par[l]])
        c0 = POFF["mu"]
        TS(omm[l][:], par[l][:, c0:c0 + 19], -1.0, 1.0, ALU.mult, ALU.add, [par[l]], [omm[l]])
        c1 = POFF["alog"]
        ACT(aneg[l][:], par[l][:, c1:c1 + 1], AF.Exp, [par[l]], [aneg[l]])
        TS(aneg[l][:], aneg[l][:], -1.0, None, ALU.mult, None, [aneg[l]], [aneg[l]])
        S.dma("sp", w2a2f[:], w2a2.h[l], writes=[w2a2f])
        CP(w2a2b[l][:], w2a2f[:], [w2a2f], [w2a2b[l]], eng="dve")

    def pcol(l, name, i=0):
        c = POFF[name] + i
        return par[l][:, c:c + 1]

    for l in range(cfg.nlayers):
        for k in range(8):
            S.dma("pool", w_in_bf.h[l, :, k, :], w_in.h[l, :, k, :], reads=[w_in], writes=[(w_in_bf, (l, k))])
        for c in range(0, 16, 4):
            S.dma("pool", w_out_bf.h[l, :, c:c + 4, :], w_out.h[l, :, c:c + 4, :], reads=[w_out],
                  writes=[(w_out_bf, (l, c))])

    psb = [Tl("ps%d" % i, stack.enter_context(nc.psum_tensor("ps%d" % i, [128, 512], F32))) for i in range(8)]
    ps_i = [0]

    def PS():
        t = psb[ps_i[0] % 6]
        ps_i[0] += 1
        return t
    psl_i = [0]

    def PSL():
        t = psb[6 + psl_i[0] % 2]
        psl_i[0] += 1
        return t

    ssq = sb("ssq", [128, 4])
    rstd = sb("rstd", [128, 4])
    hT = sb("hT", [128, 8, 512], BF16)
    wbuf = [sb("wbuf%d" % i, [128, 8, 512], BF16) for i in range(2)]
    wb_i = [0]
    oT = sb("oT", [128, 16, 512], BF16)
    TK = max(TP, PAST + 4 * 128)
    NKB = (TK + 127) // 128
    ktscr = dram("ktscr", [128, 4, NKB * 128], BF16, "Internal")
    vscr = dram("vscr", [128, NKB, 512], BF16, "Internal")
    ST_r = sb("ST_r", [128, 4, 6, 64])
    STb_r = sb("STb_r", [128, 4, 6, 64], BF16)
    STblk = sb("STblk", [128, 4, 6, 128], BF16)
    shc = sb("shc", [128, 19, 4])
    cvc = sb("cvc", [128, 10, 4, 3])
    ST_s = sb("ST_s", [128, 2, 768])
    STb_s = sb("STb_s", [128, 2, 768], BF16)

    arena = stack.enter_context(nc.sbuf_tensor("s_arena", [128, NREG * 512], F32))
    regs = [Res("reg%d" % i) for i in range(NREG)]
    free = [True] * NREG

    class Al:
        def __init__(self, r0, n):
            self.r0, self.n = r0, n
            self.regions = regs[r0:r0 + n]

        def f32(self, *shape):
            n = int(np.prod(shape))
            assert n <= self.n * 512
            ap = arena[:, self.r0 * 512:self.r0 * 512 + n]
            return self._shape(ap, shape)

        def bf(self, *shape):
            n = int(np.prod(shape))
            assert n <= self.n * 1024 and n % 2 == 0
            ap = arena[:, self.r0 * 512:self.r0 * 512 + n // 2].bitcast(BF16)
            return self._shape(ap, shape)

        @staticmethod
        def _shape(ap, shape):
            if len(shape) == 1:
                return ap
            if len(shape) == 2:
                return ap.rearrange("p (a b) -> p a b", a=shape[0])
            return ap.rearrange("p (a b c) -> p a b c", a=shape[0], b=shape[1])

        def free(self):
            for i in range(self.r0, self.r0 + self.n):
                assert not free[i]
                free[i] = True

    def alloc(n=1):
        for r0 in range(NREG - n + 1):
            if all(free[r0:r0 + n]):
                for i in range(r0, r0 + n):
                    free[i] = False
                return Al(r0, n)
        raise RuntimeError("arena full")

    def layer_tile(l, grp, src, dst, tok0, seqs, ts_, pos0, last):
        nseq = len(seqs)
        N = nseq * ts_
        PB = min(128, N)
        nb = N // PB
        L = min(64, ts_)
        nch = N // L
        nlev = int(np.log2(L))
        cps = ts_ // L
        rm = C["rm%d" % L]
        sell = C["sell%d" % L]

        def cslot(c):
            return seqs[c // cps][1]

        jk = alloc()
        junk = jk.bf(D)
        for b in range(nb):
            xb_ = alloc(2)
            hb_ = alloc()
            xb, hb = xb_.f32(D), hb_.bf(D)
            S.dma("sp", xb[:PB, :], src.h[tok0 + b * PB:tok0 + (b + 1) * PB, :], reads=[src], writes=[xb_])
            ACT(junk[:PB, :], xb[:PB, :], AF.Square, [xb_], [jk, (ssq, b)], accum=ssq[:PB, b:b + 1])
            TS(rstd[:PB, b:b + 1], ssq[:PB, b:b + 1], 1.0 / D, 1e-6, ALU.mult, ALU.add, [(ssq, b)], [(rstd, b)])
            ACT(rstd[:PB, b:b + 1], rstd[:PB, b:b + 1], AF.Sqrt, [(rstd, b)], [(rstd, b)])
            S.op("dve", lambda e, b=b: e.reciprocal(rstd[:PB, b:b + 1], rstd[:PB, b:b + 1]),
                 reads=[(rstd, b)], writes=[(rstd, b)])
            TS(hb[:PB, :], xb[:PB, :], rstd[:PB, b:b + 1], None, ALU.mult, None, [xb_, (rstd, b)], [hb_])
            for half in range(2):
                pt = PS()
                ptb = pt[:].bitcast(BF16)
                for q in range(4):
                    ft = half * 4 + q
                    TR(ptb[:, q * 128:q * 128 + PB], hb[:PB, ft * 128:(ft + 1) * 128], ident_bf[:PB, :PB],
                       [hb_, ident_bf], [pt])
                for q in range(4):
                    ft = half * 4 + q
                    TS(hT[:, ft, b * PB:(b + 1) * PB], ptb[:, q * 128:q * 128 + PB], pcol(l, "normw", ft), None,
                       ALU.mult, None, [pt, par[l]], [(hT, (ft, b))])
            xb_.free()
            hb_.free()
        jk.free()

        cur = {"g": -1, "wb": None}

        def proj(ct):
            g = ct // 4
            if g != cur["g"]:
                ng = min(4, NCT - g * 4)
                wb = wbuf[wb_i[0] % 2]
                wb_i[0] += 1
                S.dma("sp", wb[:, :, :ng * 128], w_in_bf.h[l, :, :, g * 512:g * 512 + ng * 128],
                      reads=[w_in_bf], writes=[wb])
                cur["g"], cur["wb"] = g, wb
            wb = cur["wb"]
            c = ct % 4
            pt = PS()
            for k in range(8):
                MM(pt[:, :N], wb[:, k, c * 128:(c + 1) * 128], hT[:, k, :N], k == 0, k == 7, [wb, hT], [pt])
            return pt

        def v3(ap):
            return ap.rearrange("p (s t) -> p s t", s=nseq)

        def shifted(ct, pt):
            o = alloc()
            ov = o.f32(N)
            mu = pcol(l, "mu", ct)
            om = omm[l][:, ct:ct + 1]
            TS(ov, pt[:, :N], om, None, ALU.mult, None, [pt, omm[l]], [o])
            if ts_ > 1:
                STT(v3(ov)[:, :, 1:], v3(pt[:, :N])[:, :, 0:ts_ - 1], mu, v3(ov)[:, :, 1:], ALU.mult, ALU.add,
                    [pt, par[l], o], [o])
            for si, (bi, slot) in enumerate(seqs):
                c0 = si * ts_
                STT(ov[:, c0:c0 + 1], shc[:, ct, slot:slot + 1], mu, ov[:, c0:c0 + 1], ALU.mult, ALU.add,
                    [(shc, (ct, slot)), par[l], o], [o])
                CP(shc[:, ct, slot:slot + 1], pt[:, c0 + ts_ - 1:c0 + ts_], [pt], [(shc, (ct, slot))], eng="dve")
            return o

        if cfg.stop == 'A':
            raise Stop()
        wa_pt = proj(P_WA)
        wa = shifted(18, wa_pt)
        wab = alloc()
        wabv = wab.bf(N)
        ACT(wabv[0:64, :], wa.f32(N)[0:64, :], AF.Tanh, [wa], [wab])
        CP(wabv[64:128, :], wa.f32(N)[64:128, :], [wa], [wab], eng="dve")
        wa.free()
        lev = nlev
        if cfg.stop == 'WA':
            raise Stop()
        for m in range(6):
            r_ = shifted(m, proj(P_R(m)))
            k_ = shifted(6 + m, proj(P_K(m)))
            v_ = shifted(12 + m, proj(P_V(m)))
            gpt = proj(P_GA(m))
            sg = alloc()
            ACT(sg.f32(N), gpt[:, :N], AF.Silu, [gpt], [sg])
            rv, kv, vv = r_.f32(N), k_.f32(N), v_.f32(N)
            wps = PS()
            MM(wps[:, :N], w2a2b[l][0:64, m * 128:(m + 1) * 128], wabv[0:64, :], True, True, [w2a2b[l], wab], [wps])
            lw = alloc()
            ACT(lw.f32(N), wps[:, :N], AF.Sigmoid, [wps, par[l]], [lw], bias=pcol(l, "w0", m))
            TS(lw.f32(N), lw.f32(N), -float(np.exp(-0.5)), None, ALU.mult, None, [lw], [lw])
            aps = PS()
            MM(aps[:, :N], w2a2b[l][64:128, m * 128:(m + 1) * 128], wabv[64:128, :], True, True,
               [w2a2b[l], wab], [aps])
            a_ = alloc()
            ACT(a_.f32(N), aps[:, :N], AF.Sigmoid, [aps, par[l]], [a_], bias=pcol(l, "a0", m))
            kk = alloc()
            TS(kk.f32(N), kv, pcol(l, "k_k", m), None, ALU.mult, None, [k_, par[l]], [kk])
            sq = alloc()
            TT(sq.f32(N), kk.f32(N), kk.f32(N), ALU.mult, [kk], [sq])
            n2 = PS()
            MM(n2[:, :N], C["blk64"][:, :], sq.f32(N), True, True, [C["blk64"], sq], [n2])
            ACT(sq.f32(N), n2[:, :N], AF.Sqrt, [n2], [sq])
            TS(sq.f32(N), sq.f32(N), 1e-12, None, ALU.max, None, [sq], [sq])
            S.op("dve", lambda e, sq=sq: e.reciprocal(sq.f32(N), sq.f32(N)), reads=[sq], writes=[sq])
            TT(kk.f32(N), kk.f32(N), sq.f32(N), ALU.mult, [kk, sq], [kk])
            TS(sq.f32(N), a_.f32(N), -1.0, pcol(l, "k_a", m), ALU.add, ALU.mult, [a_, par[l]], [sq])
            kp = alloc()
            STT(kp.f32(N), sq.f32(N), 1.0, kv, ALU.add, ALU.mult, [sq, k_], [kp])
            STT(sq.f32(N), rv, pcol(l, "r_k", m), kp.f32(N), ALU.mult, ALU.mult, [r_, par[l], kp], [sq])
            rks = PS()
            MM(rks[:, :N], C["blk64"][:, :], sq.f32(N), True, True, [C["blk64"], sq], [rks])
            rkv = alloc()
            TT(rkv.f32(N), rks[:, :N], vv, ALU.mult, [rks, v_], [rkv])
            be = a_
            TT(be.f32(N), kk.f32(N), a_.f32(N), ALU.mult, [kk, a_], [be])
            cs = alloc()
            S.op("dve", lambda e, cs=cs, lw=lw: e.tensor_tensor_scan(cs.f32(N), rm[:, :N], lw.f32(N), 0.0,
                                                                     ALU.mult, ALU.add),
                 reads=[rm, lw], writes=[cs])
            TT(lw.f32(N), cs.f32(N), lw.f32(N), ALU.subtract, [cs, lw], [lw])
            e1 = alloc()
            ACT(e1.f32(N), cs.f32(N), AF.Exp, [cs], [e1])
            ACT(lw.f32(N), lw.f32(N), AF.Exp, [lw], [lw])
            ACT(cs.f32(N), cs.f32(N), AF.Exp, [cs], [cs], scale=-1.0)
            e2, e3 = cs, lw
            opb = alloc(7)
            opv = opb.bf(8, N)
            AbT, RbT, BtT, KtT, BgT, KgT, vbf = (opv[:, i, :] for i in range(7))
            blk_all = arena[:, (opb.r0 + 4) * 512:(opb.r0 + 7) * 512].bitcast(BF16)
            S.op("pool", lambda e, blk_all=blk_all: e.memset(blk_all, 0.0), writes=[opb])

            def blkv(i):
                return blk_all[:, i * 1024:i * 1024 + 2 * N].rearrange("p (c h t) -> p c h t", h=2, t=L)
            Ablk, Rblk, Bblk = blkv(0), blkv(1), blkv(2)

            def c3(ap):
                return ap.rearrange("p (c t) -> p c t", t=L)
            STT(AbT, kk.f32(N), -1.0, e3.f32(N), ALU.mult, ALU.mult, [kk, e3], [opb])
            TT(RbT, rv, e1.f32(N), ALU.mult, [r_, e1], [opb])
            TT(be.f32(N), be.f32(N), e2.f32(N), ALU.mult, [be, e2], [be])
            TT(kp.f32(N), kp.f32(N), e2.f32(N), ALU.mult, [kp, e2], [kp])
            CP(BtT, be.f32(N), [be], [opb])
            CP(KtT, kp.f32(N), [kp], [opb])
            for hh in range(2):
                rws = slice(hh * 64, hh * 64 + 64)
                CP(Ablk[rws, :, hh, :], c3(AbT)[rws], [opb], [opb], eng="pool")
                CP(Rblk[rws, :, hh, :], c3(RbT)[rws], [opb], [opb], eng="pool")
                CP(Bblk[rws, :, hh, :], c3(BtT)[rws], [opb], [opb], eng="pool")
            eend = c3(e1.f32(N))[:, :, L - 1:L].to_broadcast([128, nch, L])
            TT(c3(BgT), c3(be.f32(N)), eend, ALU.mult, [be, e1], [opb])
            TT(c3(KgT), c3(kp.f32(N)), eend, ALU.mult, [kp, e1], [opb])
            CP(vbf, vv, [v_], [opb], eng="dve")
            for t_ in (r_, k_, v_, kk, sq, kp, be, e2, e3):
                t_.free()
            if cfg.stop == 'EW':
                raise Stop()
            yT = alloc()
            ncg = max(1, min(nch, 512 // (2 * L)))
            for ch0 in range(0, nch, ncg):
                ncc = min(ncg, nch - ch0)
                nbb = ncc * 2
                W = nbb * L
                mats = alloc(5)
                mv = mats.bf(10, 512)
                PT, P_, MakT, NrbT, NrkT, XT, P2, PT2 = (mv[:L, i, :W] for i in range(8))
                pss = [PS() for _ in range(5)]
                for cl in range(ncc):
                    c = ch0 + cl
                    cols = slice(c * L, (c + 1) * L)
                    oc_ = slice(cl * 2 * L, (cl + 1) * 2 * L)
                    for pi, (lh, rh) in enumerate(((BtT, Ablk), (AbT, Bblk), (KtT, Ablk), (BtT, Rblk), (KtT, Rblk))):
                        MM(pss[pi][:L, oc_], lh[:, cols], rh[:, c, :, :].rearrange("p h t -> p (h t)"), True, True,
                           [opb], [pss[pi]])

                def bm(name):
                    return C[name][:L, :L].unsqueeze(1).to_broadcast([L, nbb, L])

                def b3(ap):
                    return ap.rearrange("p (b t) -> p b t", t=L)
                TT(b3(PT), b3(pss[0][:L, :W]), bm("mlt"), ALU.mult, [pss[0], C["mlt"]], [mats])
                TT(b3(P_), b3(pss[1][:L, :W]), bm("mgt"), ALU.mult, [pss[1], C["mgt"]], [mats])
                TT(b3(MakT), b3(pss[2][:L, :W]), bm("mlt"), ALU.mult, [pss[2], C["mlt"]], [mats])
                TT(b3(NrbT), b3(pss[3][:L, :W]), bm("mle"), ALU.mult, [pss[3], C["mle"]], [mats])
                TT(b3(NrkT), b3(pss[4][:L, :W]), bm("mle"), ALU.mult, [pss[4], C["mle"]], [mats])
                TT(b3(XT), b3(PT), C["ident"][:L, :L].unsqueeze(1).to_broadcast([L, nbb, L]), ALU.add,
                   [mats, C["ident"]], [mats])
                pa, pta = P_, PT
                pb_, ptb_ = P2, PT2
                for k in range(1, lev):
                    psP, psQ, psX = PS(), PS(), PS()
                    for b in range(nbb):
                        oc_ = slice(b * L, (b + 1) * L)
                        MM(psP[:L, oc_], pta[:, oc_], pa[:, oc_], True, True, [mats], [psP])
                        if k < lev - 1:
                            MM(psQ[:L, oc_], pa[:, oc_], pta[:, oc_], True, True, [mats], [psQ])
                    CP(pb_, psP[:L, :W], [psP], [mats])
                    if k < lev - 1:
                        CP(ptb_, psQ[:L, :W], [psQ], [mats], eng="dve")
                    for b in range(nbb):
                        oc_ = slice(b * L, (b + 1) * L)
                        MM(psX[:L, oc_], pb_[:, oc_], XT[:, oc_], True, True, [mats], [psX])
                    TT(XT, psX[:L, :W], XT, ALU.add, [psX, mats], [mats])
                    pa, pta, pb_, ptb_ = pb_, ptb_, pa, pta
                if cfg.stop == 'LV':
                    raise Stop()
                tok = alloc(2)
                tkv = tok.bf(4, ncc * 128)
                Vtok, Bgtok, Kgtok = (tkv[:L, i, :].rearrange("p (c f) -> p c f", f=128) for i in (1, 2, 3))
                zb_ = alloc(2)
                Zb = zb_.bf(nbb, 128)
                for i, srcop in enumerate((AbT, vbf, BgT, KgT)):
                    pt = PS()
                    ptb = pt[:].bitcast(BF16)
                    for cl in range(ncc):
                        c = ch0 + cl
                        TR(ptb[:L, cl * 128:(cl + 1) * 128], srcop[:, c * L:(c + 1) * L], ident_bf[:, :],
                           [opb, ident_bf], [pt])
                    if i == 0:
                        CP(Zb[:L, :, 0:64], ptb[:L, :ncc * 128].rearrange("p (b j) -> p b j", j=64), [pt], [zb_])
                    else:
                        CP(tkv[:L, i, :], ptb[:L, :ncc * 128], [pt], [tok], eng=("dve" if i % 2 else "act"))
                psW = PS()
                for cl in range(ncc):
                    for hh in range(2):
                        b = cl * 2 + hh
                        MM(psW[:L, b * 64:(b + 1) * 64], MakT[:, b * L:(b + 1) * L], Vtok[:, cl, hh * 64:hh * 64 + 64],
                           True, True, [mats, tok], [psW])
                CP(Zb[:L, :, 64:128], psW[:L, :nbb * 64].rearrange("p (b j) -> p b j", j=64), [psW], [zb_], eng="dve")
                au_ = alloc(2)
                AU = au_.bf(nbb, 128)
                for b0 in range(0, nbb, 4):
                    nb4 = min(4, nbb - b0)
                    psZ = PS()
                    for b in range(b0, b0 + nb4):
                        MM(psZ[:L, (b - b0) * 128:(b - b0 + 1) * 128], XT[:, b * L:(b + 1) * L], Zb[:L, b, :],
                           True, True, [mats, zb_], [psZ])
                    CP(AU[:L, b0:b0 + nb4, :], psZ[:L, :nb4 * 128].rearrange("p (b j) -> p b j", j=128), [psZ], [au_])
                psG, psR = PS(), PS()
                for cl in range(ncc):
                    for hh in range(2):
                        b = cl * 2 + hh
                        rows = slice(hh * 64, hh * 64 + 64)
                        MM(psG[rows, cl * 64:(cl + 1) * 64], AU[:L, b, 0:64], Bgtok[:, cl, hh * 64:hh * 64 + 64],
                           True, True, [au_, tok], [psG])
                        MM(psR[rows, cl * L:(cl + 1) * L], AU[:L, b, 0:64], NrbT[:, b * L:(b + 1) * L],
                           True, True, [au_, mats], [psR])
                gr_ = alloc(2)
                Gblk = gr_.bf(4, 512)[:, 0, :ncc * 128].rearrange("p (c f) -> p c f", f=128)
                RhT = gr_.bf(4, 512)[:, 2, :ncc * L]
                S.op("pool", lambda e, gr_=gr_: e.memset(gr_.bf(4, 512)[:, 0, :], 0.0), writes=[gr_])
                for hh in range(2):
                    rws = slice(hh * 64, hh * 64 + 64)
                    CP(Gblk[rws, :, hh * 64:hh * 64 + 64], psG[rws, :ncc * 64].rearrange("p (c f) -> p c f", f=64),
                       [psG], [gr_], eng=("act" if hh else "dve"))
                TT(RhT, psR[:, :ncc * L], RbT[:, ch0 * L:(ch0 + ncc) * L], ALU.add, [psR, opb], [gr_])
                if cfg.stop == 'GR':
                    raise Stop()
                psY = PS()
                for cl in range(ncc):
                    c = ch0 + cl
                    slot = cslot(c)
                    psS = PS()
                    for hh in range(2):
                        b = cl * 2 + hh
                        hc = slice(hh * 64, hh * 64 + 64)
                        yo = psY[:L, b * 64:(b + 1) * 64]
                        MM(yo, NrbT[:, b * L:(b + 1) * L], AU[:L, b, 64:128], b == 0, False, [mats, au_], [psY],
                           skip=True)
                        MM(yo, NrkT[:, b * L:(b + 1) * L], Vtok[:, cl, hc], False, False, [mats, tok], [psY],
                           skip=True)
                        so = psS[hc, 0:64]
                        MM(so, Bgtok[:, cl, hc], AU[:L, b, 64:128], True, False, [tok, au_], [psS])
                        MM(so, Kgtok[:, cl, hc], Vtok[:, cl, hc], False, False, [tok], [psS])
                    MM(psY[:L, cl * 128:(cl + 1) * 128], RhT[:, cl * L:(cl + 1) * L], STblk[:, slot, m, :], False, True,
                       [gr_, (STblk, (slot, m))], [psY], skip=True)
                    MM(psS[:, 0:64], Gblk[:, cl, :], STb_r[:, slot, m, :], False, True,
                       [gr_, (STb_r, (slot, m))], [psS])
                    cend = (c + 1) * L - 1
                    STT(ST_r[:, slot, m, :], ST_r[:, slot, m, :], e1.f32(N)[:, cend:cend + 1], psS[:, 0:64],
                        ALU.mult, ALU.add, [(ST_r, (slot, m)), e1, psS], [(ST_r, (slot, m))])
                    CP(STb_r[:, slot, m, :], ST_r[:, slot, m, :], [(ST_r, (slot, m))], [(STb_r, (slot, m))])
                    CP(STblk[0:64, slot, m, 0:64], ST_r[0:64, slot, m, :], [(ST_r, (slot, m))], [(STblk, (slot, m))],
                       eng="dve")
                    CP(STblk[64:128, slot, m, 64:128], ST_r[64:128, slot, m, :], [(ST_r, (slot, m))],
                       [(STblk, (slot, m))], eng="pool")
                if cfg.stop == 'SEQ':
                    raise Stop()
                gn = alloc(2)
                ysq = gn.f32(2, 512)[:L, 0, :nbb * 64]
                st4 = gn.f32(2, 512)[:L, 1, :]
                s1, s2, mean, rs_ = (st4[:, i * 16:i * 16 + nbb] for i in range(4))
                Y3 = psY[:L, :nbb * 64].rearrange("p (b i) -> p b i", i=64)
                S.op("dve", lambda e, s1=s1, Y3=Y3: e.reduce_sum(s1, Y3, AX.X), reads=[psY], writes=[gn])
                ACT(ysq, psY[:L, :nbb * 64], AF.Square, [psY], [gn])
                S.op("dve", lambda e, s2=s2, ysq=ysq: e.reduce_sum(s2, ysq.rearrange("p (b i) -> p b i", i=64), AX.X),
                     reads=[gn], writes=[gn])
                TS(mean, s1, 1.0 / 64, None, ALU.mult, None, [gn], [gn])
                TT(s1, mean, mean, ALU.mult, [gn], [gn])
                STT(s2, s2, 1.0 / 64, s1, ALU.mult, ALU.subtract, [gn], [gn])
                TS(s2, s2, GN_EPS, None, ALU.add, None, [gn], [gn])
                ACT(s2, s2, AF.Sqrt, [gn], [gn])
                S.op("dve", lambda e, rs_=rs_, s2=s2: e.reciprocal(rs_, s2), reads=[gn], writes=[gn])
                ysq3 = ysq.rearrange("p (b i) -> p b i", i=64)
                TT(ysq3, Y3, mean.unsqueeze(2).to_broadcast([L, nbb, 64]), ALU.subtract, [psY, gn], [gn])
                ynb = gn.bf(4, 512)[:L, 3, :nbb * 64]
                TT(ynb.rearrange("p (b i) -> p b i", i=64), ysq3, rs_.unsqueeze(2).to_broadcast([L, nbb, 64]),
                   ALU.mult, [gn], [gn])
                if cfg.debug and l == 0 and m == 0 and ch0 == 0:
                    dy = alloc(1)
                    CP(dy.f32(512)[:L, :nbb * 64], psY[:L, :nbb * 64], [psY], [dy])
                    dbg("yraw_%s%d" % (grp, tok0), dy, dy.f32(512)[:L, :nbb * 64], [L, nbb * 64])
                    dbg("ynb_%s%d" % (grp, tok0), gn, ynb, [L, nbb * 64], BF16)
                    dbg("st4_%s%d" % (grp, tok0), gn, st4[:, :64], [L, 64])
                    dy.free()
                pt = PS()
                ptb = pt[:].bitcast(BF16)
                for cl in range(ncc):
                    TR(ptb[:, cl * L:(cl + 1) * L], ynb[:, cl * 128:(cl + 1) * 128], ident_bf[:L, :L],
                       [gn, ident_bf], [pt])
                ACT(yT.f32(N)[:, ch0 * L:(ch0 + ncc) * L], ptb[:, :ncc * L], AF.Identity, [pt, par[l]], [yT],
                    bias=pcol(l, "ln_b", m), scale=pcol(l, "ln_w", m))
                for t_ in (mats, tok, zb_, au_, gr_, gn):
                    t_.free()
            if cfg.debug and l == 0 and m == 0:
                dbg("yT_%s%d" % (grp, tok0), yT, yT.f32(N), [128, N])
                dbg("rkv_%s%d" % (grp, tok0), rkv, rkv.f32(N), [128, N])
                dbg("sg_%s%d" % (grp, tok0), sg, sg.f32(N), [128, N])
            TT(yT.f32(N), yT.f32(N), rkv.f32(N), ALU.add, [yT, rkv], [yT])
            TT(oT[:, m, :N], yT.f32(N), sg.f32(N), ALU.mult, [yT, sg], [(oT, m)])
            for t_ in (yT, rkv, sg, e1, opb):
                t_.free()
        wab.free()
        if cfg.debug and l == 0:
            dbg("oa_%s%d" % (grp, tok0), oT, oT[:, 0:6, :N], [128, 6, N], BF16)
        if cfg.stop == 'OA':
            raise Stop()
        if last:
            for si, (bi, slot) in enumerate(seqs):
                so_ = alloc(2)
                sov = so_.f32(6, 128)
                for half in range(2):
                    pt = PS()
                    for q in range(3):
                        mm_ = half * 3 + q
                        TR(pt[:64, q * 128:(q + 1) * 128], ST_r[:, slot, mm_, :], ident[:, :],
                           [(ST_r, (slot, mm_)), ident], [pt])
                    CP(sov[:64, half * 3:half * 3 + 3, :], pt[:64, :384].rearrange("p (m f) -> p m f", f=128),
                       [pt], [so_])
                S.dma("pool", o_rwkv[grp].h[l, bi].rearrange("(m hh) i j -> i m hh j", hh=2),
                      sov[:64, :, :].rearrange("p m (hh j) -> p m hh j", hh=2), reads=[so_], writes=[o_rwkv[grp]])
                so_.free()
                S.dma("pool", o_shift[grp].h[l, bi].rearrange("(c p) -> p c", p=128), shc[:, :, slot],
                      reads=[shc], writes=[o_shift[grp]], allow_slow_non_contiguous=True)

        def c3(ap):
            return ap.rearrange("p (c t) -> p c t", t=L)
        szb = alloc(3)
        szT = szb.bf(6, N)
        for m in range(6):
            pt = proj(CT_ZB + m)
            ACT(szT[:, m, :], pt[:, :N], AF.Silu, [pt], [szb])
        xsb = alloc(6)
        xs_f = xsb.f32(6, N)
        bcb = alloc(2)
        BC = bcb.bf(4, N)
        for j in range(10):
            pt = proj(CT_XS + j)
            acc = alloc()
            av = acc.f32(N)
            cw = [pcol(l, "cw%d" % i, j) for i in range(4)]
            TS(av, pt[:, :N], cw[3], pcol(l, "cb", j), ALU.mult, ALU.add, [pt, par[l]], [acc])
            for d_ in (1, 2, 3):
                STT(v3(av)[:, :, d_:], v3(pt[:, :N])[:, :, 0:ts_ - d_], cw[3 - d_], v3(av)[:, :, d_:], ALU.mult, ALU.add,
                    [pt, par[l], acc], [acc])
            for si, (bi, slot) in enumerate(seqs):
                c0 = si * ts_
                for d_ in (3, 2, 1):
                    STT(av[:, c0:c0 + d_], cvc[:, j, slot, 3 - d_:3], cw[3 - d_], av[:, c0:c0 + d_], ALU.mult, ALU.add,
                        [(cvc, (j, slot)), par[l], acc], [acc])
                CP(cvc[:, j, slot, :], pt[:, c0 + ts_ - 3:c0 + ts_], [pt], [(cvc, (j, slot))], eng="dve")
            if j < 6:
                ACT(xs_f[:, j, :], av, AF.Silu, [acc], [xsb])
            else:
                ACT(BC[:, j - 6, :], av, AF.Silu, [acc], [bcb])
            acc.free()
        pt = proj(CT_DT)
        dtb_ = alloc()
        dtT = dtb_.f32(N)
        ACT(dtT[0:12, :], pt[0:12, :N], AF.Exp, [pt, par[l]], [dtb_], bias=pcol(l, "dtb")[0:12, :])
        ACT(dtT[0:12, :], dtT[0:12, :], AF.Ln, [dtb_], [dtb_], bias=1.0)
        acb = alloc()
        acs = acb.f32(N)
        dab = alloc()
        TS(dab.f32(N)[0:12, :], dtT[0:12, :], aneg[l][0:12, :], None, ALU.mult, None, [dtb_, aneg[l]], [dab])
        S.op("dve", lambda e: e.tensor_tensor_scan(acs[0:12, :], rm[0:12, :N], dab.f32(N)[0:12, :], 0.0, ALU.mult, ALU.add),
             reads=[rm, dab], writes=[acb])
        dab.free()
        xdb = alloc(3)
        xdt = xdb.bf(6, N)
        for m in range(6):
            psd = PS()
            MM(psd[:, :N], C["e12"][0:12, m * 128:(m + 1) * 128], dtT[0:12, :], True, True, [C["e12"], dtb_], [psd])
            TT(xdt[:, m, :], xs_f[:, m, :], psd[:, :N], ALU.mult, [xsb, psd], [xdb])
        dtb_.free()
        cpb = alloc(6)
        Cp = cpb.bf(12, N)
        for h in range(12):
            psb_ = PS()
            MM(psb_[:, :N], C["sel12"][0:12, h * 128:(h + 1) * 128], acs[0:12, :], True, True, [C["sel12"], acb], [psb_])
            ea = alloc()
            ACT(ea.f32(N), psb_[:, :N], AF.Exp, [psb_], [ea])
            TT(Cp[:, h, :], BC[:, 2 + h // 6, :], ea.f32(N), ALU.mult, [bcb, ea], [cpb])
            ea.free()
        ysb = alloc(6)
        yS = ysb.f32(6, N)
        for c in range(nch):
            slot = cslot(c) % 2
            cols = slice(c * L, (c + 1) * L)
            if grp == "s" and c % cps == 0:
                load_ssm(l, seqs[c // cps][0], slot)
            pt = PS()
            ptb = pt[:].bitcast(BF16)
            for m in range(6):
                TR(ptb[:L, m * 128:(m + 1) * 128], xdt[:, m, cols], ident_bf[:, :], [xdb, ident_bf], [pt])
            for g in range(2):
                TR(ptb[:L, 768 + g * 128:768 + (g + 1) * 128], BC[:, g, cols], ident_bf[:, :], [bcb, ident_bf], [pt])
            tkb = alloc()
            tokc = tkb.bf(1024)
            CP(tokc[:L, :], ptb[:L, :1024], [pt], [tkb])
            pa = PS()
            TR(pa[:L, 0:12], acs[0:12, cols], ident[0:12, 0:12], [acb, ident], [pa])
            atb = alloc()
            at = atb.f32(512)
            CP(at[:L, 0:12], pa[:L, 0:12], [pa], [atb], eng="dve")
            pe_ = PS()
            MM(pe_[:L, 0:12], sell[:L, :L], at[:L, 0:12], True, True, [sell, atb], [pe_])
            MM(pe_[:, 16:28], sell[:L, :], at[:L, 0:12], True, True, [sell, atb], [pe_])
            TT(at[:L, 16:28], pe_[:L, 0:12], at[:L, 0:12], ALU.subtract, [pe_, atb], [atb])
            ACT(at[:L, 32:44], at[:L, 16:28], AF.Exp, [atb], [atb])
            ACT(at[:, 48:60], pe_[:, 16:28], AF.Exp, [pe_], [atb])
            TS(at[:L, 64:76], at[:L, 0:12], -1.0, None, ALU.mult, None, [atb], [atb])
            gtb = alloc()
            GT = gtb.bf(12, L)
            for g in range(2):
                pq = PS()
                for hl in range(6):
                    h = g * 6 + hl
                    MM(pq[:L, hl * L:(hl + 1) * L], C["sel12"][0:12, h * 128:h * 128 + L], acs[0:12, cols], True, True,
                       [C["sel12"], acb], [pq])
                sgb = alloc()
                sg3 = sgb.f32(6, L)
                TT(sg3[:L], pq[:L, :6 * L].rearrange("p (h q) -> p h q", q=L),
                   at[:L, 64 + g * 6:64 + g * 6 + 6].unsqueeze(2).to_broadcast([L, 6, L]), ALU.add, [pq, atb], [sgb])
                TT(sg3[:L], sg3[:L], C["nmle"][:L, :L].unsqueeze(1).to_broadcast([L, 6, L]), ALU.add,
                   [sgb, C["nmle"]], [sgb])
                ACT(sg3[:L], sg3[:L], AF.Exp, [sgb], [sgb])
                pcb = PS()
                MM(pcb[:L, 0:L], BC[:, g, cols], BC[:, 2 + g, cols], True, True, [bcb], [pcb])
                TT(GT[:L, g * 6:(g + 1) * 6, :], sg3[:L], pcb[:L, 0:L].unsqueeze(1).to_broadcast([L, 6, L]), ALU.mult,
                   [sgb, pcb], [gtb])
                sgb.free()
            xeb = alloc()
            xdte = xeb.bf(768)
            TT(xdte[:L, :].rearrange("p (h q) -> p h q", q=64), tokc[:L, 0:768].rearrange("p (h q) -> p h q", q=64),
               at[:L, 32:44].unsqueeze(2).to_broadcast([L, 12, 64]), ALU.mult, [tkb, atb], [xeb])
            psy = PS()
            for m in range(6):
                for hh in range(2):
                    h = m * 2 + hh
                    hc = slice(h * 64, (h + 1) * 64)
                    out = psy[hh * 64:hh * 64 + 64, m * L:(m + 1) * L]
                    MM(out, tokc[:L, hc], GT[:L, h, :], m == 0, False, [tkb, gtb], [psy], skip=True)
                    MM(out, STb_s[:, slot, hc], Cp[:, h, cols], False, True, [(STb_s, slot), cpb], [psy], skip=True)
            CP(yS[:, :, cols], psy[:, :6 * L].rearrange("p (m q) -> p m q", q=L), [psy], [ysb])
            for g in range(2):
                pst = PS()
                MM(pst[:, :384], tokc[:L, 768 + g * 128:768 + (g + 1) * 128], xdte[:L, g * 384:(g + 1) * 384], True, True,
                   [tkb, xeb], [pst])
                stv = ST_s[:, slot, g * 384:(g + 1) * 384]
                TT(stv.rearrange("p (h q) -> p h q", q=64), stv.rearrange("p (h q) -> p h q", q=64),
                   at[:, 48 + g * 6:48 + g * 6 + 6].unsqueeze(2).to_broadcast([128, 6, 64]), ALU.mult,
                   [(ST_s, slot), atb], [(ST_s, slot)])
                TT(stv, stv, pst[:, :384], ALU.add, [(ST_s, slot), pst], [(ST_s, slot)])
            CP(STb_s[:, slot, :], ST_s[:, slot, :], [(ST_s, slot)], [(STb_s, slot)])
            for t_ in (tkb, atb, gtb, xeb):
                t_.free()
            if last and (c + 1) % cps == 0:
                bi = seqs[c // cps][0]
                sso = alloc(2)
                ssv = sso.f32(6, 128)
                for half in range(2):
                    pt = PS()
                    for q in range(3):
                        mm_ = half * 3 + q
                        TR(pt[:, q * 128:(q + 1) * 128], ST_s[:, slot, mm_ * 128:(mm_ + 1) * 128], ident[:, :],
                           [(ST_s, slot), ident], [pt])
                    CP(ssv[:, half * 3:half * 3 + 3, :], pt[:, :384].rearrange("p (m f) -> p m f", f=128), [pt], [sso])
                S.dma("pool", o_ssm[grp].h[l, bi].rearrange("(m hh) p n -> (hh p) m n", hh=2), ssv,
                      reads=[sso], writes=[o_ssm[grp]])
                sso.free()
                for r_i in range(3):
                    S.dma("pool", o_conv[grp].h[l, bi, r_i].rearrange("(j p) -> p j", p=128), cvc[:, :, cslot(c), r_i],
                          reads=[cvc], writes=[o_conv[grp]], allow_slow_non_contiguous=True)
        for t_ in (acb, xdb, cpb, bcb):
            t_.free()
        for m in range(6):
            STT(yS[:, m, :], xs_f[:, m, :], pcol(l, "Dv", m), yS[:, m, :], ALU.mult, ALU.add, [xsb, par[l], ysb], [ysb])
            TT(yS[:, m, :], yS[:, m, :], szT[:, m, :], ALU.mult, [ysb, szb], [ysb])
        xsb.free()
        szb.free()
        for g in range(2):
            pss_ = PS()
            for q in range(3):
                m = g * 3 + q
                sqb = alloc()
                TT(sqb.f32(N), yS[:, m, :], yS[:, m, :], ALU.mult, [ysb], [sqb])
                MM(pss_[:, :N], C["ones"][:, :], sqb.f32(N), q == 0, q == 2, [C["ones"], sqb], [pss_])
                sqb.free()
            rsb = alloc()
            TS(rsb.f32(N), pss_[:, :N], 1.0 / 384, 1e-5, ALU.mult, ALU.add, [pss_], [rsb])
            ACT(rsb.f32(N), rsb.f32(N), AF.Sqrt, [rsb], [rsb])
            S.op("dve", lambda e, rsb=rsb: e.reciprocal(rsb.f32(N), rsb.f32(N)), reads=[rsb], writes=[rsb])
            for q in range(3):
                m = g * 3 + q
                STT(oT[:, 6 + m, :N], yS[:, m, :], pcol(l, "snw", m), rsb.f32(N), ALU.mult, ALU.mult,
                    [ysb, par[l], rsb], [(oT, 6 + m)])
            rsb.free()
        ysb.free()
        if cfg.debug and l == 0:
            dbg("ob_%s%d" % (grp, tok0), oT, oT[:, 6:12, :N], [128, 6, N], BF16)
        if cfg.stop == 'SSD':
            raise Stop()
        NQ = ts_ if grp == "s" else N
        blk_len = min(128, ts_)
        nblk = ts_ // blk_len
        qzb = [alloc(2), alloc(2)]
        QTz = [qzb[i].bf(4, N) for i in range(2)]
        for i in range(2):
            S.op("pool", lambda e, i=i: e.memset(qzb[i].bf(4 * N), 0.0), writes=[qzb[i]])

        def headnorm(pt, wname):
            sq_ = alloc()
            ACT(sq_.f32(N), pt[:, :N], AF.Square, [pt], [sq_])
            ss_ = PS()
            MM(ss_[:, :N], C["blk64"][:, :], sq_.f32(N), True, True, [C["blk64"], sq_], [ss_])
            TS(sq_.f32(N), ss_[:, :N], 1.0 / 64, 1e-6, ALU.mult, ALU.add, [ss_], [sq_])
            ACT(sq_.f32(N), sq_.f32(N), AF.Sqrt, [sq_], [sq_])
            S.op("dve", lambda e, sq_=sq_: e.reciprocal(sq_.f32(N), sq_.f32(N)), reads=[sq_], writes=[sq_])
            STT(sq_.f32(N), pt[:, :N], pcol(l, wname), sq_.f32(N), ALU.mult, ALU.mult, [pt, par[l], sq_], [sq_])
            return sq_

        for m in range(4):
            qn = headnorm(proj(CT_Q + m), "qnw")
            for hh in range(2):
                rws = slice(hh * 64, hh * 64 + 64)
                TS(QTz[hh][rws, m, :], qn.f32(N)[rws, :], 0.125, None, ALU.mult, None, [qn], [qzb[hh]])
            qn.free()

        def tok_out(fm_al, m, odram, to_v):
            for si, (bi, slot) in enumerate(seqs):
                for jb in range(nblk):
                    c0 = si * ts_ + jb * blk_len
                    pt = PS()
                    TR(pt[:blk_len, 0:128], fm_al.f32(N)[:, c0:c0 + blk_len], ident[:, :], [fm_al, ident], [pt])
                    tk_ = alloc()
                    CP(tk_.f32(128)[:blk_len, :], pt[:blk_len, 0:128], [pt], [tk_], eng=("dve" if jb % 2 else "act"))
                    t0 = pos0 + jb * blk_len if grp == "p" else jb * blk_len
                    S.dma("pool", odram.h[l, bi, 2 * m:2 * m + 2, t0:t0 + blk_len, :].rearrange("h t d -> t h d"),
                          tk_.f32(128)[:blk_len, :].rearrange("p (h d) -> p h d", d=64), reads=[tk_], writes=[odram])
                    if to_v:
                        kb = (pos0 + jb * blk_len) // 128 if grp == "p" else PAST // 128
                        vb_ = alloc()
                        CP(vb_.bf(128)[:blk_len, :], tk_.f32(128)[:blk_len, :], [tk_], [vb_], eng="dve")
                        S.dma("pool", vscr.h[:blk_len, (si if grp == "s" else 0) * 0 + kb + (si if grp == "s" else 0),
                                             m * 128:(m + 1) * 128],
                              vb_.bf(128)[:blk_len, :], reads=[vb_], writes=[(vscr, ("new", si, m))])
                        vb_.free()
                    tk_.free()

        for m in range(4):
            kn = headnorm(proj(CT_SK + m), "knw")
            tok_out(kn, m, o_k[grp], False)
            kb_ = alloc()
            CP(kb_.bf(N), kn.f32(N), [kn], [kb_], eng="dve")
            for si, (bi, slot) in enumerate(seqs):
                kp0 = pos0 if grp == "p" else PAST + si * 128
                S.dma("pool", ktscr.h[:, m, kp0:kp0 + ts_], kb_.bf(N)[:, si * ts_:(si + 1) * ts_], reads=[kb_],
                      writes=[(ktscr, ("new", si, m))])
            kb_.free()
            kn.free()
        for m in range(4):
            pt = proj(CT_SV + m)
            vf_ = alloc()
            CP(vf_.f32(N), pt[:, :N], [pt], [vf_])
            tok_out(vf_, m, o_v[grp], True)
            vf_.free()
        if cfg.debug and l == 0:
            dbg("qtz0_%s%d" % (grp, tok0), qzb[0], QTz[0], [128, 4, N], BF16)
            dbg("qtz1_%s%d" % (grp, tok0), qzb[1], QTz[1], [128, 4, N], BF16)
        sgb_ = alloc(4)
        sgc = sgb_.f32(4, N)
        for m in range(4):
            pt = proj(CT_GC + m)
            ACT(sgc[:, m, :], pt[:, :N], AF.Silu, [pt], [sgb_])
        mlt_bf, nuinc_bf, nones_bf = Cb["mlt"], Cb["nuinc"], Cb["nones"]
        for si, (bi, slot) in enumerate(seqs):
            qc0 = si * ts_
            if grp == "p":
                npast = pos0 // 128
                blocks = [("d", npast + o, 128, o * 128) for o in range(nblk - 1, -1, -1)] + \
                         [("p", kb, 128, 0) for kb in range(npast - 1, -1, -1)]
                nkb_tot = npast + nblk
                klen_tot = nkb_tot * 128
                vrow = 0
            else:
                load_kv_cache(l, bi)
                npast = PAST // 128
                blocks = [("d", npast + si, ts_, 0)] + [("p", kb, 128, 0) for kb in range(npast - 1, -1, -1)]
                nkb_tot = npast + NSS
                klen_tot = nkb_tot * 128
            for m in range(4):
                nreg_k = (klen_tot * 2 + 2047) // 2048
                ktm_ = alloc(nreg_k)
                KTm = ktm_.bf(klen_tot)
                S.dma("sp", KTm, ktscr.h[:, m, 0:klen_tot], reads=[ktscr], writes=[ktm_])
                nreg_v = (nkb_tot * 128 * 2 + 2047) // 2048
                vm_ = alloc(nreg_v)
                Vm = vm_.bf(nkb_tot, 128)
                S.dma("sp", Vm, vscr.h[:, 0:nkb_tot, m * 128:(m + 1) * 128], reads=[vscr], writes=[vm_])
                ops_ = PSL()
                for hh in range(2):
                    acc_ = alloc()
                    ACC = acc_.bf(NQ)
                    S.op("pool", lambda e, ACC=ACC: e.memset(ACC, 0.0), writes=[acc_])
                    first = True
                    for (kind, kb, klen, qo) in blocks:
                        nq = NQ - qo
                        qsl = slice(qc0 + qo, qc0 + NQ)
                        ksl = slice(kb * 128, kb * 128 + klen)
                        zps = PS()
                        MM(zps[:klen, :nq], KTm[:, ksl], QTz[hh][:, m, qsl], True, True, [ktm_, qzb[hh]], [zps])
                        e_ = alloc()
                        ACT(e_.f32(NQ)[:klen, :nq], zps[:klen, :nq], AF.Exp, [zps], [e_])
                        sp_ = alloc()
                        SP = sp_.bf(NQ)
                        ACT(SP[:klen, :nq], e_.f32(NQ)[:klen, :nq], AF.Ln, [e_], [sp_], bias=1.0)
                        e_.free()
                        nm_ = min(128, nq)
                        if kind == "d":
                            TT(SP[:klen, :nm_], SP[:klen, :nm_], mlt_bf[:klen, :nm_], ALU.mult, [sp_, mlt_bf], [sp_])
                        tps = PS()
                        MM(tps[:klen, :nq], nuinc_bf[:klen, :klen], SP[:klen, :nq], True, False, [nuinc_bf, sp_], [tps])
                        if not first:
                            MM(tps[:klen, :nq], nones_bf[:, :klen], ACC[:, qo:NQ], False, False, [nones_bf, acc_], [tps])
                        MM(tps[:klen, :nq], KTm[:, ksl], QTz[hh][:, m, qsl], False, True, [ktm_, qzb[hh]], [tps])
                        at_ = alloc()
                        ATT = at_.bf(NQ)
                        ACT(ATT[:klen, :nq], tps[:klen, :nq], AF.Exp, [tps], [at_])
                        if kind == "d":
                            TT(ATT[:klen, :nm_], ATT[:klen, :nm_], mlt_bf[:klen, :nm_], ALU.mult, [at_, mlt_bf], [at_])
                        if cfg.debug and l == 0 and m == 0 and hh == 0 and si == 0 and kb == (3 if grp == "p" else 8):
                            dbg("sp_%s%d" % (grp, tok0), sp_, SP[:klen, :nq], [klen, nq], BF16)
                            dbg("att_%s%d" % (grp, tok0), at_, ATT[:klen, :nq], [klen, nq], BF16)
                            dbg("ktm_%s%d" % (grp, tok0), ktm_, KTm, [128, klen_tot], BF16)
                            dbg("vm_%s%d" % (grp, tok0), vm_, Vm, [128, nkb_tot, 128], BF16)
                        MM(ops_[hh * 64:hh * 64 + 64, qo:NQ], Vm[:klen, kb, hh * 64:hh * 64 + 64], ATT[:klen, :nq],
                           first, False, [vm_, at_], [ops_], skip=True)
                        TT(ACC[:klen, qo:NQ], ACC[:klen, qo:NQ], SP[:klen, :nq], ALU.add, [acc_, sp_], [acc_])
                        at_.free()
                        sp_.free()
                        first = False
                    acc_.free()
                if cfg.debug and l == 0 and m == 0 and si == 0:
                    do_ = alloc()
                    CP(do_.f32(NQ), ops_[:, :NQ], [ops_], [do_], eng="dve")
                    dbg("oraw_%s%d" % (grp, tok0), do_, do_.f32(NQ), [128, NQ])
                    do_.free()
                TT(oT[:, 12 + m, qc0:qc0 + NQ], ops_[:, :NQ], sgc[:, m, qc0:qc0 + NQ], ALU.mult, [ops_, sgb_],
                   [(oT, 12 + m)])
                ktm_.free()
                vm_.free()
        for t_ in (qzb[0], qzb[1], sgb_):
            t_.free()
        if cfg.debug and l == 0:
            dbg("oc_%s%d" % (grp, tok0), oT, oT[:, 12:16, :N], [128, 4, N], BF16)
        if cfg.stop == 'SB':
            raise Stop()
        wo_ = alloc(16)
        wov = wo_.bf(16, D)
        for c4 in range(0, 16, 4):
            S.dma("sp", wov[:, c4:c4 + 4, :], w_out_bf.h[l, :, c4:c4 + 4, :], reads=[w_out_bf], writes=[wo_])
        for b in range(nb):
            xr_ = alloc(2)
            xr = xr_.f32(D)
            S.dma("sp", xr[:PB, :], src.h[tok0 + b * PB:tok0 + (b + 1) * PB, :], reads=[src], writes=[xr_])
            for nh in range(2):
                pso = PS()
                for c in range(16):
                    MM(pso[:PB, :512], oT[:, c, b * PB:(b + 1) * PB], wov[:, c, nh * 512:(nh + 1) * 512], c == 0, c == 15,
                       [oT, wo_], [pso])
                TT(xr[:PB, nh * 512:(nh + 1) * 512], xr[:PB, nh * 512:(nh + 1) * 512], pso[:PB, :512], ALU.add,
                   [xr_, pso], [xr_])
            S.dma("pool", dst.h[tok0 + b * PB:tok0 + (b + 1) * PB, :], xr[:PB, :], reads=[xr_], writes=[dst])
            if cfg.debug and l == 0 and b == 0:
                dbg("y_%s%d" % (grp, tok0), xr_, xr[:PB, :], [PB, D])
            xr_.free()
        wo_.free()
        return N

    def load_ssm(l, bi, slot):
        si_ = alloc(2)
        sv = si_.f32(6, 128)
        S.dma("sp", sv, st_ssm.h[l, bi].rearrange("(m hh) p n -> (hh p) m n", hh=2), reads=[st_ssm], writes=[si_])
        for half in range(2):
            pt = PS()
            for q in range(3):
                m = half * 3 + q
                TR(pt[:, q * 128:(q + 1) * 128], sv[:, m, :], ident[:, :], [si_, ident], [pt])
            CP(ST_s[:, slot, half * 384:(half + 1) * 384], pt[:, :384], [pt], [(ST_s, slot)])
        CP(STb_s[:, slot, :], ST_s[:, slot, :], [(ST_s, slot)], [(STb_s, slot)], eng="dve")
        si_.free()

    def load_rwkv_states(l):
        for slot in range(NSS):
            si_ = alloc(2)
            sv = si_.f32(6, 128)
            S.dma("sp", sv[:64].rearrange("p m (hh j) -> p m hh j", hh=2),
                  st_rwkv.h[l, slot].rearrange("(m hh) i j -> i m hh j", hh=2), reads=[st_rwkv], writes=[si_])
            for half in range(2):
                pt = PS()
                for q in range(3):
                    m = half * 3 + q
                    TR(pt[:, q * 64:(q + 1) * 64], sv[:64, m, :], ident[:64, :64], [si_, ident], [pt])
                CP(ST_r[:, slot, half * 3:half * 3 + 3, :], pt[:, :192].rearrange("p (m i) -> p m i", i=64), [pt],
                   [ST_r])
            si_.free()
            CP(STb_r[:, slot, :, :], ST_r[:, slot, :, :], [ST_r], [STb_r], eng="dve")
            CP(STblk[0:64, slot, :, 0:64], ST_r[0:64, slot, :, :], [ST_r], [STblk], eng="dve")
            CP(STblk[64:128, slot, :, 64:128], ST_r[64:128, slot, :, :], [ST_r], [STblk], eng="pool")
            S.dma("sp", shc[:, :, slot], st_shift.h[l, slot].rearrange("(c p) -> p c", p=128), reads=[st_shift],
                  writes=[shc], allow_slow_non_contiguous=True)
            for r_i in range(3):
                S.dma("sp", cvc[:, :, slot, r_i], st_conv.h[l, slot, r_i].rearrange("(j p) -> p j", p=128),
                      reads=[st_conv], writes=[cvc], allow_slow_non_contiguous=True)

    def load_kv_cache(l, bi):
        npast = PAST // 128
        for h in range(8):
            S.dma("pool", vscr.h[:, 0:npast, h * 64:(h + 1) * 64],
                  cv_d.h[l, bi, h].rearrange("(kb p) d -> p kb d", p=128), reads=[cv_d], writes=[(vscr, "cache")])
        kc_ = alloc(4)
        kct = kc_.bf(npast, 512)
        for h in range(8):
            S.dma("pool", kct[:, :, h * 64:(h + 1) * 64], ck_d.h[l, bi, h].rearrange("(kb p) d -> p kb d", p=128),
                  reads=[ck_d], writes=[kc_])
        for m in range(4):
            for kb0 in range(0, npast, 4):
                pt = PS()
                ptb = pt[:].bitcast(BF16)
                for q in range(4):
                    TR(ptb[:, q * 128:(q + 1) * 128], kct[:, kb0 + q, m * 128:(m + 1) * 128], ident_bf[:, :],
                       [kc_, ident_bf], [pt])
                kt_ = alloc()
                CP(kt_.bf(512), ptb[:, :512], [pt], [kt_], eng=("dve" if (kb0 // 4) % 2 else "act"))
                S.dma("pool", ktscr.h[:, m, kb0 * 128:(kb0 + 4) * 128], kt_.bf(512), reads=[kt_],
                      writes=[(ktscr, "cache")])
                kt_.free()
        kc_.free()

    try:
        for l in range(cfg.nlayers):
            lastl = (l == DEPTH - 1)
            src = xp if l == 0 else xmid
            dst = yp if lastl else xmid
            for s_ in range(NSP):
                for t_ in (ST_r, STb_r, STblk, shc, cvc, ST_s, STb_s):
                    S.op("dve", lambda e, t_=t_: e.memset(t_[:], 0.0), writes=[t_])
                nt = TP // 512
                for ti in range(nt):
                    layer_tile(l, "p", src, dst, s_ * TP + ti * 512, [(s_, 0)], 512, ti * 512, ti == nt - 1)
            if cfg.do_sample:
                load_rwkv_states(l)
                layer_tile(l, "s", xs if l == 0 else xmid_s, ys if lastl else xmid_s, 0,
                           [(i, i) for i in range(NSS)], TSS, PAST, True)
    except Stop:
        pass

    S.finalize(stack)
    stack.close()
    return nc, dbg_outs


OUT_NAMES = ["yp", "ys", "p_rwkv", "p_shift", "p_ssm", "p_conv", "p_k", "p_v",
             "s_rwkv", "s_shift", "s_ssm", "s_conv", "s_k", "s_v"]


def core_inputs(inp, cfg, core, shared):
    NSP, TP, NSS = cfg.nseq_p, cfg.t_p, cfg.nseq_s
    im = dict(shared)
    im["xp"] = np.ascontiguousarray(np.asarray(inp["x_prompt"])[core * NSP:(core + 1) * NSP].reshape(NSP * TP, D))
    im["xs"] = np.ascontiguousarray(np.asarray(inp["x_sample"])[core * NSS:(core + 1) * NSS].reshape(NSS * 16, D))
    sl = slice(core * NSS, (core + 1) * NSS)
    im["st_rwkv"] = np.ascontiguousarray(np.asarray(inp["state_rwkv"])[:, sl])
    im["st_shift"] = np.ascontiguousarray(np.asarray(inp["state_rwkv_shift"])[:, sl, 0])
    im["st_ssm"] = np.ascontiguousarray(np.asarray(inp["state_ssm"])[:, sl])
    im["st_conv"] = np.ascontiguousarray(np.asarray(inp["state_conv"])[:, sl])
    im["ck"] = np.ascontiguousarray(np.asarray(inp["cache_sb_k"])[:, sl])
    im["cv"] = np.ascontiguousarray(np.asarray(inp["cache_sb_v"])[:, sl])
    return im


def shared_inputs(inp):
    cm = col_map()
    w_in = np.asarray(inp["w_in"], np.float32)
    w_in_p = np.zeros((DEPTH, D, NP_), np.float32)
    w_in_p[:, :, cm >= 0] = w_in[:, :, cm[cm >= 0]]
    sh = {}
    sh["w_in"] = np.ascontiguousarray(w_in_p.reshape(DEPTH, 8, 128, NP_).transpose(0, 2, 1, 3))
    sh["w_out"] = np.ascontiguousarray(np.asarray(inp["w_out"], np.float32).reshape(DEPTH, 16, 128, D).transpose(0, 2, 1, 3))
    sh["w2a2"] = np.ascontiguousarray(np.concatenate([np.asarray(inp["rwkv_w2"]), np.asarray(inp["rwkv_a2"])], axis=1))
    sh["params"] = np.stack([build_params(inp, l) for l in range(DEPTH)])
    for k, v in make_consts().items():
        sh["c_" + k] = v
    return sh


def gather(results, cfg, ncores):
    NSP, TP, NSS = cfg.nseq_p, cfg.t_p, cfg.nseq_s
    outs = []
    for nm in OUT_NAMES:
        parts = [np.asarray(results[c][nm]) for c in range(ncores)]
        if nm == "yp":
            o = np.concatenate([p.reshape(NSP, TP, D) for p in parts], axis=0)
        elif nm == "ys":
            o = np.concatenate([p.reshape(NSS, 16, D) for p in parts], axis=0)
        else:
            o = np.concatenate(parts, axis=1)
            if nm.endswith("_shift"):
                o = o[:, :, None, :]
        outs.append(np.ascontiguousarray(o.astype(np.float32)))
    return tuple(outs)


_CACHE = {}


def kernel(**inputs):
    ncores = 8
    B = np.asarray(inputs["x_prompt"]).shape[0]
    T = np.asarray(inputs["x_prompt"]).shape[1]
    BS = np.asarray(inputs["x_sample"]).shape[0]
    cfg = Cfg(nseq_p=B // ncores, t_p=T, nseq_s=BS // ncores)
    key = (cfg.nseq_p, cfg.t_p, cfg.nseq_s)
    if key not in _CACHE:
        _CACHE[key] = build(cfg)[0]
    nc = _CACHE[key]
    shared = shared_inputs(inputs)
    in_maps = [core_inputs(inputs, cfg, c, shared) for c in range(ncores)]
    res = run_bass_kernel_spmd(nc, in_maps, core_ids=list(range(ncores)))
    return gather(res.results, cfg, ncores)
```

```python
import numpy as np
import concourse.bass as bass
import concourse.mybir as mybir
from concourse.bass_utils import run_bass_kernel_spmd
from contextlib import ExitStack

F32 = mybir.dt.float32
BF16 = mybir.dt.bfloat16
AF = mybir.ActivationFunctionType
ALU = mybir.AluOpType
AX = mybir.AxisListType

ENGS = ("pe", "act", "dve", "pool", "sp")
NSLOT = 8


class Res:
    _n = 0

    def __init__(self, name=""):
        Res._n += 1
        self.id = Res._n
        self.name = name


class Tl(Res):
    def __init__(self, name, h):
        super().__init__(name)
        self.h = h

    def __getitem__(self, k):
        return self.h[k]


class Sched:
    def __init__(self, nc):
        self.nc = nc
        self.ops = []
        self.state = {}
        self.ndma = {e: 0 for e in ENGS}
        self._cap = None

    def capture(self):
        self._cap = []

    def end_capture(self):
        c, self._cap = self._cap, None
        return c

    def merge(self, lists):
        ptr = [0] * len(lists)
        while True:
            best, bf = -1, 2.0
            for i, lst in enumerate(lists):
                if ptr[i] < len(lst):
                    f = ptr[i] / len(lst)
                    if f < bf:
                        best, bf = i, f
            if best < 0:
                break
            it = lists[best][ptr[best]]
            ptr[best] += 1
            if it[0] == "op":
                self.op(it[1], it[2], it[3], it[4])
            else:
                self.dma(it[1], it[2], it[3], it[4], it[5], **it[6])

    @staticmethod
    def _flat(lst):
        out = []
        for r in lst:
            if hasattr(r, "regions"):
                out.extend(r.regions)
            else:
                out.append(r)
        return out

    def _deps(self, idx, reads, writes, eng, is_dma):
        deps = set()
        reads = self._flat(reads)
        writes = self._flat(writes)
        rk = "dma%d" % idx if is_dma else eng
        for r in reads:
            rid, key = (r[0].id, r[1]) if isinstance(r, tuple) else (r.id, None)
            st = self.state.setdefault(rid, {})
            ks = list(st.keys()) if key is None else [k for k in (key, None) if k in st]
            for k in ks:
                if st[k][0] is not None:
                    deps.add(st[k][0])
            ent = st.setdefault(key, [None, {}])
            ent[1][rk] = idx
        for w in writes:
            rid, key = (w[0].id, w[1]) if isinstance(w, tuple) else (w.id, None)
            st = self.state.setdefault(rid, {})
            ks = list(st.keys()) if key is None else [k for k in (key, None) if k in st]
            for k in ks:
                if st[k][0] is not None:
                    deps.add(st[k][0])
                deps.update(st[k][1].values())
            if key is None:
                st.clear()
            st[key] = [idx, {}]
        deps.discard(idx)
        return deps

    def op(self, eng, emit, reads=(), writes=()):
        if self._cap is not None:
            self._cap.append(("op", eng, emit, list(reads), list(writes)))
            return -1
        idx = len(self.ops)
        deps = self._deps(idx, reads, writes, eng, False)
        self.ops.append(dict(eng=eng, emit=emit, deps=deps, dma=False, sig=False))
        return idx

    def dma(self, eng, out, in_, reads=(), writes=(), **kw):
        if self._cap is not None:
            self._cap.append(("dma", eng, out, in_, list(reads), list(writes), kw))
            return -1
        idx = len(self.ops)
        deps = self._deps(idx, reads, writes, eng, True)
        n = self.ndma[eng]
        self.ndma[eng] += 1
        self.ops.append(dict(eng=eng, emit=lambda e: e.dma_start(out=out, in_=in_, **kw), deps=deps,
                             dma=True, sig=True, slot=n % NSLOT, val=16 * (n // NSLOT + 1)))
        return idx

    def finalize(self, stack):
        nc = self.nc
        ops = self.ops
        esem = {e: stack.enter_context(nc.semaphore("es_" + e)) for e in ENGS}
        dsem = {e: [stack.enter_context(nc.semaphore("ds_%s%d" % (e, i))) for i in range(NSLOT)]
                for e in ENGS if self.ndma[e] > 0}
        for i, o in enumerate(ops):
            for j in o["deps"]:
                pj = ops[j]
                if pj["dma"]:
                    continue
                if pj["eng"] == "pe" and o["eng"] == "pe":
                    continue
                pj["sig"] = True
        cnt = {e: 0 for e in ENGS}
        for o in ops:
            if o["dma"]:
                o["sem"] = dsem[o["eng"]][o["slot"]]
            elif o["sig"]:
                cnt[o["eng"]] += 1
                o["sem"] = esem[o["eng"]]
                o["val"] = cnt[o["eng"]]
        byeng = {e: [] for e in ENGS}
        for i, o in enumerate(ops):
            byeng[o["eng"]].append(i)

        def run(e, name):
            waited = {}
            for i in byeng[name]:
                o = ops[i]
                need = {}
                for j in o["deps"]:
                    pj = ops[j]
                    if (not pj["dma"]) and pj["eng"] == "pe" and name == "pe":
                        continue
                    s = pj["sem"]
                    if need.get(s, 0) < pj["val"]:
                        need[s] = pj["val"]
                if o["dma"] and o["val"] > 16:
                    s = o["sem"]
                    if need.get(s, 0) < o["val"] - 16:
                        need[s] = o["val"] - 16
                for s, v in need.items():
                    if waited.get(s, 0) < v:
                        e.wait_ge(s, v)
                        waited[s] = v
                ins = o["emit"](e)
                if o["dma"]:
                    ins.then_inc(o["sem"], 16)
                elif o["sig"]:
                    ins.then_inc(o["sem"], 1)
            if name in dsem:
                n = self.ndma[name]
                for s in range(NSLOT):
                    k = (n - s + NSLOT - 1) // NSLOT
                    if k > 0 and waited.get(dsem[name][s], 0) < 16 * k:
                        e.wait_ge(dsem[name][s], 16 * k)

        with nc.Block() as block:
            @block.tensor
            def _(e):
                run(e, "pe")

            @block.scalar
            def _(e):
                run(e, "act")

            @block.vector
            def _(e):
                run(e, "dve")

            @block.gpsimd
            def _(e):
                run(e, "pool")

            @block.sync
            def _(e):
                run(e, "sp")


D = 1024
DEPTH = 2
HD = 64
D_A = 768
D_B = 768
D_C = 512
H_A = 12
H_B = 12
H_C = 8
NST = 128
CONV_DIM = 1280
W_SHIFT = 2432
N_IN = 7308
GN_EPS = 64e-5
PAST = 1024
P_WA = 0
CT_ZB, CT_XS, CT_B, CT_C, CT_DT, CT_Q, CT_SK, CT_SV, CT_GC = 25, 31, 37, 39, 41, 42, 46, 50, 54


def P_R(m):
    return 1 + 4 * m


def P_K(m):
    return 2 + 4 * m


def P_V(m):
    return 3 + 4 * m


def P_GA(m):
    return 4 + 4 * m
NCT = 58
NP_ = NCT * 128


def col_map():
    m = -np.ones(NP_, np.int64)

    def put(pos, c0, n):
        m[pos * 128:pos * 128 + n] = np.arange(c0, c0 + n)
    put(P_WA, 2304, 128)
    for mm in range(6):
        put(P_R(mm), mm * 128, 128)
        put(P_K(mm), 768 + mm * 128, 128)
        put(P_V(mm), 1536 + mm * 128, 128)
        put(P_GA(mm), 2432 + mm * 128, 128)
    put(CT_ZB, 3200, 768)
    put(CT_XS, 3968, 1280)
    put(CT_DT, 5248, 12)
    put(CT_Q, 5260, 2048)
    return m


def param_layout():
    off = {}
    n = 0
    for name, k in (("normw", 8), ("mu", 19), ("w0", 6), ("a0", 6), ("k_k", 6), ("k_a", 6), ("r_k", 6),
                    ("ln_w", 6), ("ln_b", 6), ("cw0", 10), ("cw1", 10), ("cw2", 10), ("cw3", 10), ("cb", 10),
                    ("dtb", 1), ("alog", 1), ("Dv", 6), ("snw", 6), ("qnw", 1), ("knw", 1)):
        off[name] = n
        n += k
    return off, n


POFF, NPAR = param_layout()


def fm(v, ntile):
    return np.ascontiguousarray(np.asarray(v, np.float32).reshape(ntile, 128).T)


def build_params(inp, l):
    P = np.zeros((128, NPAR), np.float32)

    def put(name, arr):
        P[:, POFF[name]:POFF[name] + arr.shape[1]] = arr
    put("normw", fm(inp["norm_w"][l], 8))
    put("mu", fm(inp["rwkv_mu"][l], 19))
    for nm, key in (("w0", "rwkv_w0"), ("a0", "rwkv_a0"), ("k_k", "rwkv_k_k"), ("k_a", "rwkv_k_a"),
                    ("ln_w", "rwkv_ln_w"), ("ln_b", "rwkv_ln_b"), ("snw", "ssm_norm_w")):
        put(nm, fm(inp[key][l], 6))
    put("r_k", fm(np.asarray(inp["rwkv_r_k"][l]).reshape(-1), 6))
    for i in range(4):
        put("cw%d" % i, fm(inp["ssm_conv_w"][l][i], 10))
    put("cb", fm(inp["ssm_conv_b"][l], 10))
    dtb = np.zeros(128, np.float32)
    dtb[:12] = inp["ssm_dt_bias"][l]
    put("dtb", dtb[:, None])
    al = np.zeros(128, np.float32)
    al[:12] = inp["ssm_A_log"][l]
    put("alog", al[:, None])
    put("Dv", fm(np.repeat(np.asarray(inp["ssm_D"][l]), 64), 6))
    put("qnw", np.tile(np.asarray(inp["sb_q_norm_w"][l]), 2)[:, None])
    put("knw", np.tile(np.asarray(inp["sb_k_norm_w"][l]), 2)[:, None])
    return P


def make_consts():
    c = {}
    i = np.arange(128)
    c["ident"] = np.eye(128, dtype=np.float32)
    c["blk64"] = (i[:, None] // 64 == i[None, :] // 64).astype(np.float32)
    c["ones"] = np.ones((128, 128), np.float32)
    c["mlt"] = (i[:, None] < i[None, :]).astype(np.float32)
    c["mle"] = (i[:, None] <= i[None, :]).astype(np.float32)
    c["mgt"] = (i[:, None] > i[None, :]).astype(np.float32)
    c["nmle"] = np.where(i[:, None] <= i[None, :], 0.0, -30000.0).astype(np.float32)
    c["nuinc"] = -(i[:, None] >= i[None, :]).astype(np.float32)
    c["nones"] = -np.ones((128, 128), np.float32)
    e12 = np.zeros((128, 768), np.float32)
    for h in range(12):
        e12[h, h * 64:(h + 1) * 64] = 1.0
    c["e12"] = e12
    sel = np.zeros((128, 12 * 128), np.float32)
    for h in range(12):
        sel[h, h * 128:(h + 1) * 128] = 1.0
    c["sel12"] = sel
    for L in (64, 16):
        sl = np.zeros((128, 128), np.float32)
        sl[L - 1, :] = 1.0
        c["sell%d" % L] = sl
        rm = np.ones((128, 512 // L, L), np.float32)
        rm[:, :, 0] = 0.0
        c["rm%d" % L] = rm.reshape(128, 512)
    return c


CONST_SHAPES = {k: v.shape for k, v in make_consts().items()}


import os
KSEQ = os.environ.get('KSEQ', '')


class Stop(Exception):
    pass


class Cfg:
    def __init__(self, nseq_p=2, t_p=4096, nseq_s=4, debug=False, do_sample=True, nlayers=DEPTH):
        self.nseq_p = nseq_p
        self.t_p = t_p
        self.nseq_s = nseq_s
        self.ts = 16
        self.debug = debug
        self.do_sample = do_sample
        self.nlayers = nlayers
        import os
        self.stop = os.environ.get('KSTOP', '')


NREG = 56


def build(cfg):
    nc = bass.Bass("TRN2", target_bir_lowering=False)
    S = Sched(nc)
    stack = ExitStack()
    dbg_outs = []

    def dram(name, shape, dt, kind):
        return Tl(name, nc.dram_tensor(name, list(shape), dt, kind=kind).ap())

    def sb(name, shape, dt=F32):
        return Tl(name, stack.enter_context(nc.sbuf_tensor("s_" + name, list(shape), dt)))

    NSP, TP, NSS, TSS = cfg.nseq_p, cfg.t_p, cfg.nseq_s, cfg.ts
    NTOK_P = NSP * TP
    NTOK_S = NSS * TSS
    xp = dram("xp", [NTOK_P, D], F32, "ExternalInput")
    xs = dram("xs", [NTOK_S, D], F32, "ExternalInput")
    w_in = dram("w_in", [DEPTH, 128, 8, NP_], F32, "ExternalInput")
    w_out = dram("w_out", [DEPTH, 128, 16, D], F32, "ExternalInput")
    w2a2 = dram("w2a2", [DEPTH, 128, 768], F32, "ExternalInput")
    params = dram("params", [DEPTH, 128, NPAR], F32, "ExternalInput")
    cd = {k: dram("c_" + k, list(shp), F32, "ExternalInput") for k, shp in CONST_SHAPES.items()}
    st_rwkv = dram("st_rwkv", [DEPTH, NSS, 12, 64, 64], F32, "ExternalInput")
    st_shift = dram("st_shift", [DEPTH, NSS, W_SHIFT], F32, "ExternalInput")
    st_ssm = dram("st_ssm", [DEPTH, NSS, 12, 64, 128], F32, "ExternalInput")
    st_conv = dram("st_conv", [DEPTH, NSS, 3, CONV_DIM], F32, "ExternalInput")
    ck_d = dram("ck", [DEPTH, NSS, 8, PAST, 64], F32, "ExternalInput")
    cv_d = dram("cv", [DEPTH, NSS, 8, PAST, 64], F32, "ExternalInput")
    yp = dram("yp", [NTOK_P, D], F32, "ExternalOutput")
    ys = dram("ys", [NTOK_S, D], F32, "ExternalOutput")
    o_rwkv = {"p": dram("p_rwkv", [DEPTH, NSP, 12, 64, 64], F32, "ExternalOutput"),
              "s": dram("s_rwkv", [DEPTH, NSS, 12, 64, 64], F32, "ExternalOutput")}
    o_shift = {"p": dram("p_shift", [DEPTH, NSP, W_SHIFT], F32, "ExternalOutput"),
               "s": dram("s_shift", [DEPTH, NSS, W_SHIFT], F32, "ExternalOutput")}
    o_ssm = {"p": dram("p_ssm", [DEPTH, NSP, 12, 64, 128], F32, "ExternalOutput"),
             "s": dram("s_ssm", [DEPTH, NSS, 12, 64, 128], F32, "ExternalOutput")}
    o_conv = {"p": dram("p_conv", [DEPTH, NSP, 3, CONV_DIM], F32, "ExternalOutput"),
              "s": dram("s_conv", [DEPTH, NSS, 3, CONV_DIM], F32, "ExternalOutput")}
    o_k = {"p": dram("p_k", [DEPTH, NSP, 8, TP, 64], F32, "ExternalOutput"),
           "s": dram("s_k", [DEPTH, NSS, 8, TSS, 64], F32, "ExternalOutput")}
    o_v = {"p": dram("p_v", [DEPTH, NSP, 8, TP, 64], F32, "ExternalOutput"),
           "s": dram("s_v", [DEPTH, NSS, 8, TSS, 64], F32, "ExternalOutput")}
    w_in_bf = dram("w_in_bf", [DEPTH, 128, 8, NP_], BF16, "Internal")
    w_out_bf = dram("w_out_bf", [DEPTH, 128, 16, D], BF16, "Internal")
    xmid = dram("xmid", [NTOK_P, D], F32, "Internal")
    xmid_s = dram("xmid_s", [NTOK_S, D], F32, "Internal")

    def dbg(name, rd, ap, shape, dt=F32):
        if not cfg.debug:
            return
        o = dram("dbg_" + name, shape, dt, "ExternalOutput")
        S.dma("pool", o.h, ap, reads=[rd], writes=[o])
        dbg_outs.append("dbg_" + name)

    def ACT(out, in_, func, reads, writes, bias=None, scale=None, accum=None):
        kw = {}
        if bias is not None:
            kw["bias"] = bias
        if scale is not None:
            kw["scale"] = scale
        if accum is not None:
            kw["accum_out"] = accum
        S.op("act", lambda e: e.activation(out, in_, func, **kw), reads=reads, writes=writes)

    def TS(out, in0, s1, s2, op0, op1, reads, writes, eng="dve"):
        if s2 is None:
            S.op(eng, lambda e: e.tensor_scalar(out, in0, s1, None, op0), reads=reads, writes=writes)
        else:
            S.op(eng, lambda e: e.tensor_scalar(out, in0, s1, s2, op0, op1), reads=reads, writes=writes)

    def TT(out, in0, in1, op, reads, writes, eng="dve"):
        S.op(eng, lambda e: e.tensor_tensor(out, in0, in1, op), reads=reads, writes=writes)

    def STT(out, in0, sc, in1, op0, op1, reads, writes):
        S.op("dve", lambda e: e.scalar_tensor_tensor(out, in0, sc, in1, op0, op1), reads=reads, writes=writes)

    def MM(out, lhsT, rhs, start, stop, reads, writes, skip=False):
        if skip:
            S.op("pe", lambda e: e.matmul(out, lhsT, rhs, start=start, stop=stop, skip_group_check=True),
                 reads=reads, writes=writes)
        else:
            S.op("pe", lambda e: e.matmul(out, lhsT, rhs, start=start, stop=stop), reads=reads, writes=writes)

    def TR(out, in_, idn, reads, writes):
        S.op("pe", lambda e: e.transpose(out, in_, idn), reads=reads, writes=writes)

    def CP(out, in_, reads, writes, eng="act"):
        if eng == "act":
            S.op("act", lambda e: e.copy(out, in_), reads=reads, writes=writes)
        else:
            S.op(eng, lambda e: e.tensor_copy(out, in_), reads=reads, writes=writes)

    C = {}
    for k, shp in CONST_SHAPES.items():
        C[k] = sb("c_" + k, list(shp))
        S.dma("sp", C[k][:], cd[k].h, writes=[C[k]])
    Cb = {}
    for k in ("ident", "mlt", "nuinc", "nones"):
        Cb[k] = sb("cb_" + k, [128, 128], BF16)
        S.op("dve", lambda e, k=k: e.tensor_copy(Cb[k][:], C[k][:]), reads=[C[k]], writes=[Cb[k]])
    ident, ident_bf = C["ident"], Cb["ident"]
    par = [sb("par%d" % l, [128, NPAR]) for l in range(DEPTH)]
    omm = [sb("omm%d" % l, [128, 19]) for l in range(DEPTH)]
    aneg = [sb("aneg%d" % l, [128, 1]) for l in range(DEPTH)]
    w2a2f = sb("w2a2f", [128, 768])
    w2a2b = [sb("w2a2b%d" % l, [128, 768], BF16) for l in range(DEPTH)]
    for l in range(DEPTH):
        S.dma("sp", par[l][:], params.h[l], writes=[par[l]])
        c0 = POFF["mu"]
        TS(omm[l][:], par[l][:, c0:c0 + 19], -1.0, 1.0, ALU.mult, ALU.add, [par[l]], [omm[l]])
        c1 = POFF["alog"]
        ACT(aneg[l][:], par[l][:, c1:c1 + 1], AF.Exp, [par[l]], [aneg[l]])
        TS(aneg[l][:], aneg[l][:], -1.0, None, ALU.mult, None, [aneg[l]], [aneg[l]])
        S.dma("sp", w2a2f[:], w2a2.h[l], writes=[w2a2f])
        CP(w2a2b[l][:], w2a2f[:], [w2a2f], [w2a2b[l]], eng="dve")

    def pcol(l, name, i=0):
        c = POFF[name] + i
        return par[l][:, c:c + 1]

    for l in range(cfg.nlayers):
        for k in range(8):
            S.dma("pool", w_in_bf.h[l, :, k, :], w_in.h[l, :, k, :], reads=[w_in], writes=[(w_in_bf, (l, k))])
        for c in range(0, 16, 4):
            S.dma("pool", w_out_bf.h[l, :, c:c + 4, :], w_out.h[l, :, c:c + 4, :], reads=[w_out],
                  writes=[(w_out_bf, (l, c))])

    psb = [Tl("ps%d" % i, stack.enter_context(nc.psum_tensor("ps%d" % i, [128, 512], F32))) for i in range(8)]
    ps_i = [0]

    CTX = {"G": dict(rot=[0, 1, 2, 3, 4, 5], lng=[6, 7], lo=0, hi=NREG, wb=0),
           "A": dict(rot=[0, 1, 2], lng=[3], lo=0, hi=28, wb=0),
           "B": dict(rot=[4, 5, 6], lng=[7], lo=28, hi=NREG, wb=1)}
    for c_ in CTX.values():
        c_["pi"] = 0
        c_["li"] = 0
        c_["g"] = -1
        c_["wbi"] = 0
    ctx = ["G"]

    def PS():
        c_ = CTX[ctx[0]]
        t = psb[c_["rot"][c_["pi"] % len(c_["rot"])]]
        c_["pi"] += 1
        return t

    def PSL():
        c_ = CTX[ctx[0]]
        t = psb[c_["lng"][c_["li"] % len(c_["lng"])]]
        c_["li"] += 1
        return t

    ssq = sb("ssq", [128, 4])
    rstd = sb("rstd", [128, 4])
    hT = sb("hT", [128, 8, 512], BF16)
    wbuf = [[sb("wbuf%d_%d" % (j, i), [128, 8, 256], BF16) for i in range(2)] for j in range(2)]
    oT = sb("oT", [128, 16, 512], BF16)
    TK = max(TP, PAST + 4 * 128)
    NKB = (TK + 127) // 128
    ktscr = dram("ktscr", [128, 4, NKB * 128], BF16, "Internal")
    vscr = dram("vscr", [128, NKB, 512], BF16, "Internal")
    ST_r = sb("ST_r", [128, 4, 6, 64])
    STb_r = sb("STb_r", [128, 4, 6, 64], BF16)
    STblk = sb("STblk", [128, 4, 6, 128], BF16)
    shc = sb("shc", [128, 19, 4])
    cvc = sb("cvc", [128, 10, 4, 3])
    ST_s = sb("ST_s", [128, 2, 768])
    STb_s = sb("STb_s", [128, 2, 768], BF16)

    arena = stack.enter_context(nc.sbuf_tensor("s_arena", [128, NREG * 512], F32))
    regs = [Res("reg%d" % i) for i in range(NREG)]
    free = [True] * NREG

    class Al:
        def __init__(self, r0, n):
            self.r0, self.n = r0, n
            self.regions = regs[r0:r0 + n]

        def f32(self, *shape):
            n = int(np.prod(shape))
            assert n <= self.n * 512
            ap = arena[:, self.r0 * 512:self.r0 * 512 + n]
            return self._shape(ap, shape)

        def bf(self, *shape):
            n = int(np.prod(shape))
            assert n <= self.n * 1024 and n % 2 == 0
            ap = arena[:, self.r0 * 512:self.r0 * 512 + n // 2].bitcast(BF16)
            return self._shape(ap, shape)

        @staticmethod
        def _shape(ap, shape):
            if len(shape) == 1:
                return ap
            if len(shape) == 2:
                return ap.rearrange("p (a b) -> p a b", a=shape[0])
            return ap.rearrange("p (a b c) -> p a b c", a=shape[0], b=shape[1])

        def free(self):
            for i in range(self.r0, self.r0 + self.n):
                assert not free[i]
                free[i] = True

    def alloc(n=1):
        c_ = CTX[ctx[0]]
        for r0 in range(c_["lo"], c_["hi"] - n + 1):
            if all(free[r0:r0 + n]):
                for i in range(r0, r0 + n):
                    free[i] = False
                return Al(r0, n)
        raise RuntimeError("arena full")

    def layer_tile(l, grp, src, dst, tok0, seqs, ts_, pos0, last):
        nseq = len(seqs)
        N = nseq * ts_
        PB = min(128, N)
        nb = N // PB
        L = min(64, ts_)
        nch = N // L
        nlev = int(np.log2(L))
        cps = ts_ // L
        rm = C["rm%d" % L]
        sell = C["sell%d" % L]

        def cslot(c):
            return seqs[c // cps][1]

        jk = alloc()
        junk = jk.bf(D)
        for b in range(nb):
            xb_ = alloc(2)
            hb_ = alloc()
            xb, hb = xb_.f32(D), hb_.bf(D)
            S.dma("sp", xb[:PB, :], src.h[tok0 + b * PB:tok0 + (b + 1) * PB, :], reads=[src], writes=[xb_])
            ACT(junk[:PB, :], xb[:PB, :], AF.Square, [xb_], [jk, (ssq, b)], accum=ssq[:PB, b:b + 1])
            TS(rstd[:PB, b:b + 1], ssq[:PB, b:b + 1], 1.0 / D, 1e-6, ALU.mult, ALU.add, [(ssq, b)], [(rstd, b)])
            ACT(rstd[:PB, b:b + 1], rstd[:PB, b:b + 1], AF.Sqrt, [(rstd, b)], [(rstd, b)])
            S.op("dve", lambda e, b=b: e.reciprocal(rstd[:PB, b:b + 1], rstd[:PB, b:b + 1]),
                 reads=[(rstd, b)], writes=[(rstd, b)])
            TS(hb[:PB, :], xb[:PB, :], rstd[:PB, b:b + 1], None, ALU.mult, None, [xb_, (rstd, b)], [hb_])
            for half in range(2):
                pt = PS()
                ptb = pt[:].bitcast(BF16)
                for q in range(4):
                    ft = half * 4 + q
                    TR(ptb[:, q * 128:q * 128 + PB], hb[:PB, ft * 128:(ft + 1) * 128], ident_bf[:PB, :PB],
                       [hb_, ident_bf], [pt])
                for q in range(4):
                    ft = half * 4 + q
                    TS(hT[:, ft, b * PB:(b + 1) * PB], ptb[:, q * 128:q * 128 + PB], pcol(l, "normw", ft), None,
                       ALU.mult, None, [pt, par[l]], [(hT, (ft, b))])
            xb_.free()
            hb_.free()
        jk.free()

        for c_ in CTX.values():
            c_["g"] = -1

        def proj(ct):
            c_ = CTX[ctx[0]]
            g = ct // 2
            if g != c_["g"]:
                wb = wbuf[c_["wb"]][c_["wbi"] % 2]
                c_["wbi"] += 1
                S.dma("sp", wb[:, :, :], w_in_bf.h[l, :, :, g * 256:(g + 1) * 256], reads=[w_in_bf], writes=[wb])
                c_["g"], c_["cwb"] = g, wb
            wb = c_["cwb"]
            c = ct % 2
            pt = PS()
            for k in range(8):
                MM(pt[:, :N], wb[:, k, c * 128:(c + 1) * 128], hT[:, k, :N], k == 0, k == 7, [wb, hT], [pt])
            return pt

        def v3(ap):
            return ap.rearrange("p (s t) -> p s t", s=nseq)

        def shifted(ct, pt):
            o = alloc()
            ov = o.f32(N)
            mu = pcol(l, "mu", ct)
            om = omm[l][:, ct:ct + 1]
            TS(ov, pt[:, :N], om, None, ALU.mult, None, [pt, omm[l]], [o])
            if ts_ > 1:
                STT(v3(ov)[:, :, 1:], v3(pt[:, :N])[:, :, 0:ts_ - 1], mu, v3(ov)[:, :, 1:], ALU.mult, ALU.add,
                    [pt, par[l], o], [o])
            for si, (bi, slot) in enumerate(seqs):
                c0 = si * ts_
                STT(ov[:, c0:c0 + 1], shc[:, ct, slot:slot + 1], mu, ov[:, c0:c0 + 1], ALU.mult, ALU.add,
                    [(shc, (ct, slot)), par[l], o], [o])
                CP(shc[:, ct, slot:slot + 1], pt[:, c0 + ts_ - 1:c0 + ts_], [pt], [(shc, (ct, slot))], eng="dve")
            return o

        if cfg.stop == 'A':
            raise Stop()
        ctx[0] = "A"
        S.capture()
        wa_pt = proj(P_WA)
        wa = shifted(18, wa_pt)
        wab = alloc()
        wabv = wab.bf(N)
        ACT(wabv[0:64, :], wa.f32(N)[0:64, :], AF.Tanh, [wa], [wab])
        CP(wabv[64:128, :], wa.f32(N)[64:128, :], [wa], [wab], eng="dve")
        wa.free()
        lev = nlev
        if cfg.stop == 'WA':
            raise Stop()
        for m in range(6):
            r_ = shifted(m, proj(P_R(m)))
            k_ = shifted(6 + m, proj(P_K(m)))
            v_ = shifted(12 + m, proj(P_V(m)))
            gpt = proj(P_GA(m))
            sg = alloc()
            ACT(sg.f32(N), gpt[:, :N], AF.Silu, [gpt], [sg])
            rv, kv, vv = r_.f32(N), k_.f32(N), v_.f32(N)
            wps = PS()
            MM(wps[:, :N], w2a2b[l][0:64, m * 128:(m + 1) * 128], wabv[0:64, :], True, True, [w2a2b[l], wab], [wps])
            lw = alloc()
            ACT(lw.f32(N), wps[:, :N], AF.Sigmoid, [wps, par[l]], [lw], bias=pcol(l, "w0", m))
            TS(lw.f32(N), lw.f32(N), -float(np.exp(-0.5)), None, ALU.mult, None, [lw], [lw])
            aps = PS()
            MM(aps[:, :N], w2a2b[l][64:128, m * 128:(m + 1) * 128], wabv[64:128, :], True, True,
               [w2a2b[l], wab], [aps])
            a_ = alloc()
            ACT(a_.f32(N), aps[:, :N], AF.Sigmoid, [aps, par[l]], [a_], bias=pcol(l, "a0", m))
            kk = alloc()
            TS(kk.f32(N), kv, pcol(l, "k_k", m), None, ALU.mult, None, [k_, par[l]], [kk])
            sq = alloc()
            TT(sq.f32(N), kk.f32(N), kk.f32(N), ALU.mult, [kk], [sq])
            n2 = PS()
            MM(n2[:, :N], C["blk64"][:, :], sq.f32(N), True, True, [C["blk64"], sq], [n2])
            ACT(sq.f32(N), n2[:, :N], AF.Sqrt, [n2], [sq])
            TS(sq.f32(N), sq.f32(N), 1e-12, None, ALU.max, None, [sq], [sq])
            S.op("dve", lambda e, sq=sq: e.reciprocal(sq.f32(N), sq.f32(N)), reads=[sq], writes=[sq])
            TT(kk.f32(N), kk.f32(N), sq.f32(N), ALU.mult, [kk, sq], [kk])
            TS(sq.f32(N), a_.f32(N), -1.0, pcol(l, "k_a", m), ALU.add, ALU.mult, [a_, par[l]], [sq])
            kp = alloc()
            STT(kp.f32(N), sq.f32(N), 1.0, kv, ALU.add, ALU.mult, [sq, k_], [kp])
            STT(sq.f32(N), rv, pcol(l, "r_k", m), kp.f32(N), ALU.mult, ALU.mult, [r_, par[l], kp], [sq])
            rks = PS()
            MM(rks[:, :N], C["blk64"][:, :], sq.f32(N), True, True, [C["blk64"], sq], [rks])
            rkv = alloc()
            TT(rkv.f32(N), rks[:, :N], vv, ALU.mult, [rks, v_], [rkv])
            be = a_
            TT(be.f32(N), kk.f32(N), a_.f32(N), ALU.mult, [kk, a_], [be])
            cs = alloc()
            S.op("dve", lambda e, cs=cs, lw=lw: e.tensor_tensor_scan(cs.f32(N), rm[:, :N], lw.f32(N), 0.0,
                                                                     ALU.mult, ALU.add),
                 reads=[rm, lw], writes=[cs])
            TT(lw.f32(N), cs.f32(N), lw.f32(N), ALU.subtract, [cs, lw], [lw])
            e1 = alloc()
            ACT(e1.f32(N), cs.f32(N), AF.Exp, [cs], [e1])
            ACT(lw.f32(N), lw.f32(N), AF.Exp, [lw], [lw])
            ACT(cs.f32(N), cs.f32(N), AF.Exp, [cs], [cs], scale=-1.0)
            e2, e3 = cs, lw
            opb = alloc(7)
            opv = opb.bf(8, N)
            AbT, RbT, BtT, KtT, BgT, KgT, vbf = (opv[:, i, :] for i in range(7))
            blk_all = arena[:, (opb.r0 + 4) * 512:(opb.r0 + 7) * 512].bitcast(BF16)
            S.op("pool", lambda e, blk_all=blk_all: e.memset(blk_all, 0.0), writes=[opb])

            def blkv(i):
                return blk_all[:, i * 1024:i * 1024 + 2 * N].rearrange("p (c h t) -> p c h t", h=2, t=L)
            Ablk, Rblk, Bblk = blkv(0), blkv(1), blkv(2)

            def c3(ap):
                return ap.rearrange("p (c t) -> p c t", t=L)
            STT(AbT, kk.f32(N), -1.0, e3.f32(N), ALU.mult, ALU.mult, [kk, e3], [opb])
            TT(RbT, rv, e1.f32(N), ALU.mult, [r_, e1], [opb])
            TT(be.f32(N), be.f32(N), e2.f32(N), ALU.mult, [be, e2], [be])
            TT(kp.f32(N), kp.f32(N), e2.f32(N), ALU.mult, [kp, e2], [kp])
            CP(BtT, be.f32(N), [be], [opb])
            CP(KtT, kp.f32(N), [kp], [opb])
            for hh in range(2):
                rws = slice(hh * 64, hh * 64 + 64)
                CP(Ablk[rws, :, hh, :], c3(AbT)[rws], [opb], [opb], eng=("act" if hh else "dve"))
                CP(Rblk[rws, :, hh, :], c3(RbT)[rws], [opb], [opb], eng=("dve" if hh else "act"))
                CP(Bblk[rws, :, hh, :], c3(BtT)[rws], [opb], [opb], eng=("act" if hh else "dve"))
            eend = c3(e1.f32(N))[:, :, L - 1:L].to_broadcast([128, nch, L])
            TT(c3(BgT), c3(be.f32(N)), eend, ALU.mult, [be, e1], [opb])
            TT(c3(KgT), c3(kp.f32(N)), eend, ALU.mult, [kp, e1], [opb])
            CP(vbf, vv, [v_], [opb], eng="dve")
            for t_ in (r_, k_, v_, kk, sq, kp, be, e2, e3):
                t_.free()
            if cfg.stop == 'EW':
                raise Stop()
            yT = alloc()
            ncg = max(1, min(nch, 512 // (2 * L)))
            for ch0 in range(0, nch, ncg):
                ncc = min(ncg, nch - ch0)
                nbb = ncc * 2
                W = nbb * L
                mats = alloc(5)
                mv = mats.bf(10, 512)
                PT, P_, MakT, NrbT, NrkT, XT, P2, PT2 = (mv[:L, i, :W] for i in range(8))
                def first_stage(pi):
                    lh, rh = ((BtT, Ablk), (AbT, Bblk), (KtT, Ablk), (BtT, Rblk), (KtT, Rblk))[pi]
                    ps_ = PS()
                    for cl in range(ncc):
                        c = ch0 + cl
                        cols = slice(c * L, (c + 1) * L)
                        oc_ = slice(cl * 2 * L, (cl + 1) * 2 * L)
                        MM(ps_[:L, oc_], lh[:, cols], rh[:, c, :, :].rearrange("p h t -> p (h t)"), True, True,
                           [opb], [ps_])
                    return ps_

                def bm(name):
                    return C[name][:L, :L].unsqueeze(1).to_broadcast([L, nbb, L])

                def b3(ap):
                    return ap.rearrange("p (b t) -> p b t", t=L)
                ps_ = first_stage(0)
                TT(b3(PT), b3(ps_[:L, :W]), bm("mlt"), ALU.mult, [ps_, C["mlt"]], [mats])
                ps_ = first_stage(1)
                TT(b3(P_), b3(ps_[:L, :W]), bm("mgt"), ALU.mult, [ps_, C["mgt"]], [mats])
                ps_ = first_stage(2)
                TT(b3(MakT), b3(ps_[:L, :W]), bm("mlt"), ALU.mult, [ps_, C["mlt"]], [mats])
                ps_ = first_stage(3)
                TT(b3(NrbT), b3(ps_[:L, :W]), bm("mle"), ALU.mult, [ps_, C["mle"]], [mats])
                ps_ = first_stage(4)
                TT(b3(NrkT), b3(ps_[:L, :W]), bm("mle"), ALU.mult, [ps_, C["mle"]], [mats])
                TT(b3(XT), b3(PT), C["ident"][:L, :L].unsqueeze(1).to_broadcast([L, nbb, L]), ALU.add,
                   [mats, C["ident"]], [mats])
                pa, pta = P_, PT
                pb_, ptb_ = P2, PT2
                for k in range(1, lev):
                    psP, psQ, psX = PS(), PS(), PS()
                    for b in range(nbb):
                        oc_ = slice(b * L, (b + 1) * L)
                        MM(psP[:L, oc_], pta[:, oc_], pa[:, oc_], True, True, [mats], [psP])
                        if k < lev - 1:
                            MM(psQ[:L, oc_], pa[:, oc_], pta[:, oc_], True, True, [mats], [psQ])
                    CP(pb_, psP[:L, :W], [psP], [mats])
                    if k < lev - 1:
                        CP(ptb_, psQ[:L, :W], [psQ], [mats], eng="dve")
                    for b in range(nbb):
                        oc_ = slice(b * L, (b + 1) * L)
                        MM(psX[:L, oc_], pb_[:, oc_], XT[:, oc_], True, True, [mats], [psX])
                    TT(XT, psX[:L, :W], XT, ALU.add, [psX, mats], [mats])
                    pa, pta, pb_, ptb_ = pb_, ptb_, pa, pta
                if cfg.stop == 'LV':
                    raise Stop()
                tok = alloc(2)
                tkv = tok.bf(4, ncc * 128)
                Vtok, Bgtok, Kgtok = (tkv[:L, i, :].rearrange("p (c f) -> p c f", f=128) for i in (1, 2, 3))
                zb_ = alloc(2)
                Zb = zb_.bf(nbb, 128)
                for i, srcop in enumerate((AbT, vbf, BgT, KgT)):
                    pt = PS()
                    ptb = pt[:].bitcast(BF16)
                    for cl in range(ncc):
                        c = ch0 + cl
                        TR(ptb[:L, cl * 128:(cl + 1) * 128], srcop[:, c * L:(c + 1) * L], ident_bf[:, :],
                           [opb, ident_bf], [pt])
                    if i == 0:
                        CP(Zb[:L, :, 0:64], ptb[:L, :ncc * 128].rearrange("p (b j) -> p b j", j=64), [pt], [zb_])
                    else:
                        CP(tkv[:L, i, :], ptb[:L, :ncc * 128], [pt], [tok], eng=("dve" if i % 2 else "act"))
                psW = PS()
                for cl in range(ncc):
                    for hh in range(2):
                        b = cl * 2 + hh
                        MM(psW[:L, b * 64:(b + 1) * 64], MakT[:, b * L:(b + 1) * L], Vtok[:, cl, hh * 64:hh * 64 + 64],
                           True, True, [mats, tok], [psW])
                CP(Zb[:L, :, 64:128], psW[:L, :nbb * 64].rearrange("p (b j) -> p b j", j=64), [psW], [zb_], eng="dve")
                au_ = alloc(2)
                AU = au_.bf(nbb, 128)
                for b0 in range(0, nbb, 4):
                    nb4 = min(4, nbb - b0)
                    psZ = PS()
                    for b in range(b0, b0 + nb4):
                        MM(psZ[:L, (b - b0) * 128:(b - b0 + 1) * 128], XT[:, b * L:(b + 1) * L], Zb[:L, b, :],
                           True, True, [mats, zb_], [psZ])
                    CP(AU[:L, b0:b0 + nb4, :], psZ[:L, :nb4 * 128].rearrange("p (b j) -> p b j", j=128), [psZ], [au_])
                psG, psR = PS(), PS()
                for cl in range(ncc):
                    for hh in range(2):
                        b = cl * 2 + hh
                        rows = slice(hh * 64, hh * 64 + 64)
                        MM(psG[rows, cl * 64:(cl + 1) * 64], AU[:L, b, 0:64], Bgtok[:, cl, hh * 64:hh * 64 + 64],
                           True, True, [au_, tok], [psG])
                        MM(psR[rows, cl * L:(cl + 1) * L], AU[:L, b, 0:64], NrbT[:, b * L:(b + 1) * L],
                           True, True, [au_, mats], [psR])
                gr_ = alloc(2)
                Gblk = gr_.bf(4, 512)[:, 0, :ncc * 128].rearrange("p (c f) -> p c f", f=128)
                RhT = gr_.bf(4, 512)[:, 2, :ncc * L]
                S.op("pool", lambda e, gr_=gr_: e.memset(gr_.bf(4, 512)[:, 0, :], 0.0), writes=[gr_])
                for hh in range(2):
                    rws = slice(hh * 64, hh * 64 + 64)
                    CP(Gblk[rws, :, hh * 64:hh * 64 + 64], psG[rws, :ncc * 64].rearrange("p (c f) -> p c f", f=64),
                       [psG], [gr_], eng=("act" if hh else "dve"))
                TT(RhT, psR[:, :ncc * L], RbT[:, ch0 * L:(ch0 + ncc) * L], ALU.add, [psR, opb], [gr_])
                if cfg.stop == 'GR':
                    raise Stop()
                psY = PSL()
                for cl in range(ncc):
                    c = ch0 + cl
                    slot = cslot(c)
                    psS = PS()
                    for hh in range(2):
                        b = cl * 2 + hh
                        hc = slice(hh * 64, hh * 64 + 64)
                        yo = psY[:L, b * 64:(b + 1) * 64]
                        MM(yo, NrbT[:, b * L:(b + 1) * L], AU[:L, b, 64:128], b == 0, False, [mats, au_], [psY],
                           skip=True)
                        MM(yo, NrkT[:, b * L:(b + 1) * L], Vtok[:, cl, hc], False, False, [mats, tok], [psY],
                           skip=True)
                        so = psS[hc, 0:64]
                        MM(so, Bgtok[:, cl, hc], AU[:L, b, 64:128], True, False, [tok, au_], [psS])
                        MM(so, Kgtok[:, cl, hc], Vtok[:, cl, hc], False, False, [tok], [psS])
                    MM(psY[:L, cl * 128:(cl + 1) * 128], RhT[:, cl * L:(cl + 1) * L], STblk[:, slot, m, :], False, True,
                       [gr_, (STblk, (slot, m))], [psY], skip=True)
                    MM(psS[:, 0:64], Gblk[:, cl, :], STb_r[:, slot, m, :], False, True,
                       [gr_, (STb_r, (slot, m))], [psS])
                    cend = (c + 1) * L - 1
                    STT(ST_r[:, slot, m, :], ST_r[:, slot, m, :], e1.f32(N)[:, cend:cend + 1], psS[:, 0:64],
                        ALU.mult, ALU.add, [(ST_r, (slot, m)), e1, psS], [(ST_r, (slot, m))])
                    CP(STb_r[:, slot, m, :], ST_r[:, slot, m, :], [(ST_r, (slot, m))], [(STb_r, (slot, m))])
                    CP(STblk[0:64, slot, m, 0:64], ST_r[0:64, slot, m, :], [(ST_r, (slot, m))], [(STblk, (slot, m))],
                       eng="dve")
                    CP(STblk[64:128, slot, m, 64:128], ST_r[64:128, slot, m, :], [(ST_r, (slot, m))],
                       [(STblk, (slot, m))], eng="dve")
                if cfg.stop == 'SEQ':
                    raise Stop()
                gn = alloc(2)
                ysq = gn.f32(2, 512)[:L, 0, :nbb * 64]
                st4 = gn.f32(2, 512)[:L, 1, :]
                s1, s2, mean, rs_ = (st4[:, i * 16:i * 16 + nbb] for i in range(4))
                Y3 = psY[:L, :nbb * 64].rearrange("p (b i) -> p b i", i=64)
                S.op("dve", lambda e, s1=s1, Y3=Y3: e.reduce_sum(s1, Y3, AX.X), reads=[psY], writes=[gn])
                ACT(ysq, psY[:L, :nbb * 64], AF.Square, [psY], [gn])
                S.op("dve", lambda e, s2=s2, ysq=ysq: e.reduce_sum(s2, ysq.rearrange("p (b i) -> p b i", i=64), AX.X),
                     reads=[gn], writes=[gn])
                TS(mean, s1, 1.0 / 64, None, ALU.mult, None, [gn], [gn])
                TT(s1, mean, mean, ALU.mult, [gn], [gn])
                STT(s2, s2, 1.0 / 64, s1, ALU.mult, ALU.subtract, [gn], [gn])
                TS(s2, s2, GN_EPS, None, ALU.add, None, [gn], [gn])
                ACT(s2, s2, AF.Sqrt, [gn], [gn])
                S.op("dve", lambda e, rs_=rs_, s2=s2: e.reciprocal(rs_, s2), reads=[gn], writes=[gn])
                ysq3 = ysq.rearrange("p (b i) -> p b i", i=64)
                TT(ysq3, Y3, mean.unsqueeze(2).to_broadcast([L, nbb, 64]), ALU.subtract, [psY, gn], [gn])
                ynb = gn.bf(4, 512)[:L, 3, :nbb * 64]
                TT(ynb.rearrange("p (b i) -> p b i", i=64), ysq3, rs_.unsqueeze(2).to_broadcast([L, nbb, 64]),
                   ALU.mult, [gn], [gn])
                if cfg.debug and l == 0 and m == 0 and ch0 == 0:
                    dy = alloc(1)
                    CP(dy.f32(512)[:L, :nbb * 64], psY[:L, :nbb * 64], [psY], [dy])
                    dbg("yraw_%s%d" % (grp, tok0), dy, dy.f32(512)[:L, :nbb * 64], [L, nbb * 64])
                    dbg("ynb_%s%d" % (grp, tok0), gn, ynb, [L, nbb * 64], BF16)
                    dbg("st4_%s%d" % (grp, tok0), gn, st4[:, :64], [L, 64])
                    dy.free()
                pt = PS()
                ptb = pt[:].bitcast(BF16)
                for cl in range(ncc):
                    TR(ptb[:, cl * L:(cl + 1) * L], ynb[:, cl * 128:(cl + 1) * 128], ident_bf[:L, :L],
                       [gn, ident_bf], [pt])
                ACT(yT.f32(N)[:, ch0 * L:(ch0 + ncc) * L], ptb[:, :ncc * L], AF.Identity, [pt, par[l]], [yT],
                    bias=pcol(l, "ln_b", m), scale=pcol(l, "ln_w", m))
                for t_ in (mats, tok, zb_, au_, gr_, gn):
                    t_.free()
            if cfg.debug and l == 0 and m == 0:
                dbg("yT_%s%d" % (grp, tok0), yT, yT.f32(N), [128, N])
                dbg("rkv_%s%d" % (grp, tok0), rkv, rkv.f32(N), [128, N])
                dbg("sg_%s%d" % (grp, tok0), sg, sg.f32(N), [128, N])
            TT(yT.f32(N), yT.f32(N), rkv.f32(N), ALU.add, [yT, rkv], [yT])
            TT(oT[:, m, :N], yT.f32(N), sg.f32(N), ALU.mult, [yT, sg], [(oT, m)])
            for t_ in (yT, rkv, sg, e1, opb):
                t_.free()
        wab.free()
        if cfg.debug and l == 0:
            dbg("oa_%s%d" % (grp, tok0), oT, oT[:, 0:6, :N], [128, 6, N], BF16)
        if cfg.stop == 'OA':
            raise Stop()
        if last:
            for si, (bi, slot) in enumerate(seqs):
                so_ = alloc(2)
                sov = so_.f32(6, 128)
                for half in range(2):
                    pt = PS()
                    for q in range(3):
                        mm_ = half * 3 + q
                        TR(pt[:64, q * 128:(q + 1) * 128], ST_r[:, slot, mm_, :], ident[:, :],
                           [(ST_r, (slot, mm_)), ident], [pt])
                    CP(sov[:64, half * 3:half * 3 + 3, :], pt[:64, :384].rearrange("p (m f) -> p m f", f=128),
                       [pt], [so_])
                S.dma("pool", o_rwkv[grp].h[l, bi].rearrange("(m hh) i j -> i m hh j", hh=2),
                      sov[:64, :, :].rearrange("p m (hh j) -> p m hh j", hh=2), reads=[so_], writes=[o_rwkv[grp]])
                so_.free()
                S.dma("pool", o_shift[grp].h[l, bi].rearrange("(c p) -> p c", p=128), shc[:, :, slot],
                      reads=[shc], writes=[o_shift[grp]], allow_slow_non_contiguous=True)

        opsA = S.end_capture()
        ctx[0] = "B"
        S.capture()

        def c3(ap):
            return ap.rearrange("p (c t) -> p c t", t=L)
        szb = alloc(3)
        szT = szb.bf(6, N)
        for m in range(6):
            pt = proj(CT_ZB + m)
            ACT(szT[:, m, :], pt[:, :N], AF.Silu, [pt], [szb])
        xsb = alloc(6)
        xs_f = xsb.f32(6, N)
        bcb = alloc(2)
        BC = bcb.bf(4, N)
        for j in range(10):
            pt = proj(CT_XS + j)
            acc = alloc()
            av = acc.f32(N)
            cw = [pcol(l, "cw%d" % i, j) for i in range(4)]
            TS(av, pt[:, :N], cw[3], pcol(l, "cb", j), ALU.mult, ALU.add, [pt, par[l]], [acc])
            for d_ in (1, 2, 3):
                STT(v3(av)[:, :, d_:], v3(pt[:, :N])[:, :, 0:ts_ - d_], cw[3 - d_], v3(av)[:, :, d_:], ALU.mult, ALU.add,
                    [pt, par[l], acc], [acc])
            for si, (bi, slot) in enumerate(seqs):
                c0 = si * ts_
                for d_ in (3, 2, 1):
                    STT(av[:, c0:c0 + d_], cvc[:, j, slot, 3 - d_:3], cw[3 - d_], av[:, c0:c0 + d_], ALU.mult, ALU.add,
                        [(cvc, (j, slot)), par[l], acc], [acc])
                CP(cvc[:, j, slot, :], pt[:, c0 + ts_ - 3:c0 + ts_], [pt], [(cvc, (j, slot))], eng="dve")
            if j < 6:
                ACT(xs_f[:, j, :], av, AF.Silu, [acc], [xsb])
            else:
                ACT(BC[:, j - 6, :], av, AF.Silu, [acc], [bcb])
            acc.free()
        pt = proj(CT_DT)
        dtb_ = alloc()
        dtT = dtb_.f32(N)
        ACT(dtT[0:12, :], pt[0:12, :N], AF.Exp, [pt, par[l]], [dtb_], bias=pcol(l, "dtb")[0:12, :])
        ACT(dtT[0:12, :], dtT[0:12, :], AF.Ln, [dtb_], [dtb_], bias=1.0)
        acb = alloc()
        acs = acb.f32(N)
        dab = alloc()
        TS(dab.f32(N)[0:12, :], dtT[0:12, :], aneg[l][0:12, :], None, ALU.mult, None, [dtb_, aneg[l]], [dab])
        S.op("dve", lambda e: e.tensor_tensor_scan(acs[0:12, :], rm[0:12, :N], dab.f32(N)[0:12, :], 0.0, ALU.mult, ALU.add),
             reads=[rm, dab], writes=[acb])
        dab.free()
        xdb = alloc(3)
        xdt = xdb.bf(6, N)
        for m in range(6):
            psd = PS()
            MM(psd[:, :N], C["e12"][0:12, m * 128:(m + 1) * 128], dtT[0:12, :], True, True, [C["e12"], dtb_], [psd])
            TT(xdt[:, m, :], xs_f[:, m, :], psd[:, :N], ALU.mult, [xsb, psd], [xdb])
        dtb_.free()
        ysb = alloc(6)
        yS = ysb.f32(6, N)
        for m in range(6):
            TS(yS[:, m, :], xs_f[:, m, :], pcol(l, "Dv", m), None, ALU.mult, None, [xsb, par[l]], [ysb])
        xsb.free()
        cpb = alloc(6)
        Cp = cpb.bf(12, N)
        for h in range(12):
            psb_ = PS()
            MM(psb_[:, :N], C["sel12"][0:12, h * 128:(h + 1) * 128], acs[0:12, :], True, True, [C["sel12"], acb], [psb_])
            ea = alloc()
            ACT(ea.f32(N), psb_[:, :N], AF.Exp, [psb_], [ea])
            TT(Cp[:, h, :], BC[:, 2 + h // 6, :], ea.f32(N), ALU.mult, [bcb, ea], [cpb])
            ea.free()
        for c in range(nch):
            slot = cslot(c) % 2
            cols = slice(c * L, (c + 1) * L)
            if grp == "s" and c % cps == 0:
                load_ssm(l, seqs[c // cps][0], slot)
            pt = PS()
            ptb = pt[:].bitcast(BF16)
            for m in range(6):
                TR(ptb[:L, m * 128:(m + 1) * 128], xdt[:, m, cols], ident_bf[:, :], [xdb, ident_bf], [pt])
            for g in range(2):
                TR(ptb[:L, 768 + g * 128:768 + (g + 1) * 128], BC[:, g, cols], ident_bf[:, :], [bcb, ident_bf], [pt])
            tkb = alloc()
            tokc = tkb.bf(1024)
            CP(tokc[:L, :], ptb[:L, :1024], [pt], [tkb])
            pa = PS()
            TR(pa[:L, 0:12], acs[0:12, cols], ident[0:12, 0:12], [acb, ident], [pa])
            atb = alloc()
            at = atb.f32(512)
            CP(at[:L, 0:12], pa[:L, 0:12], [pa], [atb], eng="dve")
            pe_ = PS()
            MM(pe_[:L, 0:12], sell[:L, :L], at[:L, 0:12], True, True, [sell, atb], [pe_])
            MM(pe_[:, 16:28], sell[:L, :], at[:L, 0:12], True, True, [sell, atb], [pe_])
            TT(at[:L, 16:28], pe_[:L, 0:12], at[:L, 0:12], ALU.subtract, [pe_, atb], [atb])
            ACT(at[:L, 32:44], at[:L, 16:28], AF.Exp, [atb], [atb])
            ACT(at[:, 48:60], pe_[:, 16:28], AF.Exp, [pe_], [atb])
            TS(at[:L, 64:76], at[:L, 0:12], -1.0, None, ALU.mult, None, [atb], [atb])
            gtb = alloc()
            GT = gtb.bf(12, L)
            for g in range(2):
                pq = PS()
                for hl in range(6):
                    h = g * 6 + hl
                    MM(pq[:L, hl * L:(hl + 1) * L], C["sel12"][0:12, h * 128:h * 128 + L], acs[0:12, cols], True, True,
                       [C["sel12"], acb], [pq])
                sgb = alloc()
                sg3 = sgb.f32(6, L)
                TT(sg3[:L], pq[:L, :6 * L].rearrange("p (h q) -> p h q", q=L),
                   at[:L, 64 + g * 6:64 + g * 6 + 6].unsqueeze(2).to_broadcast([L, 6, L]), ALU.add, [pq, atb], [sgb])
                TT(sg3[:L], sg3[:L], C["nmle"][:L, :L].unsqueeze(1).to_broadcast([L, 6, L]), ALU.add,
                   [sgb, C["nmle"]], [sgb])
                ACT(sg3[:L], sg3[:L], AF.Exp, [sgb], [sgb])
                pcb = PS()
                MM(pcb[:L, 0:L], BC[:, g, cols], BC[:, 2 + g, cols], True, True, [bcb], [pcb])
                TT(GT[:L, g * 6:(g + 1) * 6, :], sg3[:L], pcb[:L, 0:L].unsqueeze(1).to_broadcast([L, 6, L]), ALU.mult,
                   [sgb, pcb], [gtb])
                sgb.free()
            xeb = alloc()
            xdte = xeb.bf(768)
            TT(xdte[:L, :].rearrange("p (h q) -> p h q", q=64), tokc[:L, 0:768].rearrange("p (h q) -> p h q", q=64),
               at[:L, 32:44].unsqueeze(2).to_broadcast([L, 12, 64]), ALU.mult, [tkb, atb], [xeb])
            psy = PS()
            for m in range(6):
                for hh in range(2):
                    h = m * 2 + hh
                    hc = slice(h * 64, (h + 1) * 64)
                    out = psy[hh * 64:hh * 64 + 64, m * L:(m + 1) * L]
                    MM(out, tokc[:L, hc], GT[:L, h, :], m == 0, False, [tkb, gtb], [psy], skip=True)
                    MM(out, STb_s[:, slot, hc], Cp[:, h, cols], False, True, [(STb_s, slot), cpb], [psy], skip=True)
            TT(yS[:, :, cols], yS[:, :, cols], psy[:, :6 * L].rearrange("p (m q) -> p m q", q=L), ALU.add,
               [psy, ysb], [ysb])
            for g in range(2):
                pst = PS()
                MM(pst[:, :384], tokc[:L, 768 + g * 128:768 + (g + 1) * 128], xdte[:L, g * 384:(g + 1) * 384], True, True,
                   [tkb, xeb], [pst])
                stv = ST_s[:, slot, g * 384:(g + 1) * 384]
                TT(stv.rearrange("p (h q) -> p h q", q=64), stv.rearrange("p (h q) -> p h q", q=64),
                   at[:, 48 + g * 6:48 + g * 6 + 6].unsqueeze(2).to_broadcast([128, 6, 64]), ALU.mult,
                   [(ST_s, slot), atb], [(ST_s, slot)])
                TT(stv, stv, pst[:, :384], ALU.add, [(ST_s, slot), pst], [(ST_s, slot)])
            CP(STb_s[:, slot, :], ST_s[:, slot, :], [(ST_s, slot)], [(STb_s, slot)])
            for t_ in (tkb, atb, gtb, xeb):
                t_.free()
            if last and (c + 1) % cps == 0:
                bi = seqs[c // cps][0]
                sso = alloc(2)
                ssv = sso.f32(6, 128)
                for half in range(2):
                    pt = PS()
                    for q in range(3):
                        mm_ = half * 3 + q
                        TR(pt[:, q * 128:(q + 1) * 128], ST_s[:, slot, mm_ * 128:(mm_ + 1) * 128], ident[:, :],
                           [(ST_s, slot), ident], [pt])
                    CP(ssv[:, half * 3:half * 3 + 3, :], pt[:, :384].rearrange("p (m f) -> p m f", f=128), [pt], [sso])
                S.dma("pool", o_ssm[grp].h[l, bi].rearrange("(m hh) p n -> (hh p) m n", hh=2), ssv,
                      reads=[sso], writes=[o_ssm[grp]])
                sso.free()
                for r_i in range(3):
                    S.dma("pool", o_conv[grp].h[l, bi, r_i].rearrange("(j p) -> p j", p=128), cvc[:, :, cslot(c), r_i],
                          reads=[cvc], writes=[o_conv[grp]], allow_slow_non_contiguous=True)
        for t_ in (acb, xdb, cpb, bcb):
            t_.free()
        for m in range(6):
            TT(yS[:, m, :], yS[:, m, :], szT[:, m, :], ALU.mult, [ysb, szb], [ysb])
        szb.free()
        for g in range(2):
            pss_ = PS()
            for q in range(3):
                m = g * 3 + q
                sqb = alloc()
                TT(sqb.f32(N), yS[:, m, :], yS[:, m, :], ALU.mult, [ysb], [sqb])
                MM(pss_[:, :N], C["ones"][:, :], sqb.f32(N), q == 0, q == 2, [C["ones"], sqb], [pss_])
                sqb.free()
            rsb = alloc()
            TS(rsb.f32(N), pss_[:, :N], 1.0 / 384, 1e-5, ALU.mult, ALU.add, [pss_], [rsb])
            ACT(rsb.f32(N), rsb.f32(N), AF.Sqrt, [rsb], [rsb])
            S.op("dve", lambda e, rsb=rsb: e.reciprocal(rsb.f32(N), rsb.f32(N)), reads=[rsb], writes=[rsb])
            for q in range(3):
                m = g * 3 + q
                STT(oT[:, 6 + m, :N], yS[:, m, :], pcol(l, "snw", m), rsb.f32(N), ALU.mult, ALU.mult,
                    [ysb, par[l], rsb], [(oT, 6 + m)])
            rsb.free()
        ysb.free()
        if cfg.debug and l == 0:
            dbg("ob_%s%d" % (grp, tok0), oT, oT[:, 6:12, :N], [128, 6, N], BF16)
        if cfg.stop == 'SSD':
            raise Stop()
        NQ = ts_ if grp == "s" else N
        blk_len = min(128, ts_)
        nblk = ts_ // blk_len
        qzb = [alloc(2), alloc(2)]
        QTz = [qzb[i].bf(4, N) for i in range(2)]
        for i in range(2):
            S.op("pool", lambda e, i=i: e.memset(qzb[i].bf(4 * N), 0.0), writes=[qzb[i]])

        def headnorm(pt, wname):
            sq_ = alloc()
            ACT(sq_.f32(N), pt[:, :N], AF.Square, [pt], [sq_])
            ss_ = PS()
            MM(ss_[:, :N], C["blk64"][:, :], sq_.f32(N), True, True, [C["blk64"], sq_], [ss_])
            TS(sq_.f32(N), ss_[:, :N], 1.0 / 64, 1e-6, ALU.mult, ALU.add, [ss_], [sq_])
            ACT(sq_.f32(N), sq_.f32(N), AF.Sqrt, [sq_], [sq_])
            S.op("dve", lambda e, sq_=sq_: e.reciprocal(sq_.f32(N), sq_.f32(N)), reads=[sq_], writes=[sq_])
            STT(sq_.f32(N), pt[:, :N], pcol(l, wname), sq_.f32(N), ALU.mult, ALU.mult, [pt, par[l], sq_], [sq_])
            return sq_

        for m in range(4):
            qn = headnorm(proj(CT_Q + m), "qnw")
            for hh in range(2):
                rws = slice(hh * 64, hh * 64 + 64)
                TS(QTz[hh][rws, m, :], qn.f32(N)[rws, :], 0.125, None, ALU.mult, None, [qn], [qzb[hh]])
            qn.free()

        def tok_out(fm_al, m, odram, to_v):
            for si, (bi, slot) in enumerate(seqs):
                for jb in range(nblk):
                    c0 = si * ts_ + jb * blk_len
                    pt = PS()
                    TR(pt[:blk_len, 0:128], fm_al.f32(N)[:, c0:c0 + blk_len], ident[:, :], [fm_al, ident], [pt])
                    tk_ = alloc()
                    CP(tk_.f32(128)[:blk_len, :], pt[:blk_len, 0:128], [pt], [tk_], eng=("dve" if jb % 2 else "act"))
                    t0 = pos0 + jb * blk_len if grp == "p" else jb * blk_len
                    S.dma("pool", odram.h[l, bi, 2 * m:2 * m + 2, t0:t0 + blk_len, :].rearrange("h t d -> t h d"),
                          tk_.f32(128)[:blk_len, :].rearrange("p (h d) -> p h d", d=64), reads=[tk_], writes=[odram])
                    if to_v:
                        kb = (pos0 + jb * blk_len) // 128 if grp == "p" else PAST // 128
                        vb_ = alloc()
                        CP(vb_.bf(128)[:blk_len, :], tk_.f32(128)[:blk_len, :], [tk_], [vb_], eng="dve")
                        S.dma("pool", vscr.h[:blk_len, (si if grp == "s" else 0) * 0 + kb + (si if grp == "s" else 0),
                                             m * 128:(m + 1) * 128],
                              vb_.bf(128)[:blk_len, :], reads=[vb_], writes=[(vscr, ("new", si, m))])
                        vb_.free()
                    tk_.free()

        for m in range(4):
            kn = headnorm(proj(CT_SK + m), "knw")
            tok_out(kn, m, o_k[grp], False)
            kb_ = alloc()
            CP(kb_.bf(N), kn.f32(N), [kn], [kb_], eng="dve")
            for si, (bi, slot) in enumerate(seqs):
                kp0 = pos0 if grp == "p" else PAST + si * 128
                S.dma("pool", ktscr.h[:, m, kp0:kp0 + ts_], kb_.bf(N)[:, si * ts_:(si + 1) * ts_], reads=[kb_],
                      writes=[(ktscr, ("new", si, m))])
            kb_.free()
            kn.free()
        for m in range(4):
            pt = proj(CT_SV + m)
            vf_ = alloc()
            CP(vf_.f32(N), pt[:, :N], [pt], [vf_])
            tok_out(vf_, m, o_v[grp], True)
            vf_.free()
        if cfg.debug and l == 0:
            dbg("qtz0_%s%d" % (grp, tok0), qzb[0], QTz[0], [128, 4, N], BF16)
            dbg("qtz1_%s%d" % (grp, tok0), qzb[1], QTz[1], [128, 4, N], BF16)
        sgb_ = alloc(4)
        sgc = sgb_.f32(4, N)
        for m in range(4):
            pt = proj(CT_GC + m)
            ACT(sgc[:, m, :], pt[:, :N], AF.Silu, [pt], [sgb_])
        mlt_bf, nuinc_bf, nones_bf = Cb["mlt"], Cb["nuinc"], Cb["nones"]
        for si, (bi, slot) in enumerate(seqs):
            qc0 = si * ts_
            if grp == "p":
                npast = pos0 // 128
                blocks = [("d", npast + o, 128, o * 128) for o in range(nblk - 1, -1, -1)] + \
                         [("p", kb, 128, 0) for kb in range(npast - 1, -1, -1)]
                nkb_tot = npast + nblk
                klen_tot = nkb_tot * 128
                vrow = 0
            else:
                load_kv_cache(l, bi)
                npast = PAST // 128
                blocks = [("d", npast + si, ts_, 0)] + [("p", kb, 128, 0) for kb in range(npast - 1, -1, -1)]
                nkb_tot = npast + NSS
                klen_tot = nkb_tot * 128
            for m in range(4):
                nreg_k = (klen_tot * 2 + 2047) // 2048
                ktm_ = alloc(nreg_k)
                KTm = ktm_.bf(klen_tot)
                S.dma("sp", KTm, ktscr.h[:, m, 0:klen_tot], reads=[ktscr], writes=[ktm_])
                nreg_v = (nkb_tot * 128 * 2 + 2047) // 2048
                vm_ = alloc(nreg_v)
                Vm = vm_.bf(nkb_tot, 128)
                S.dma("sp", Vm, vscr.h[:, 0:nkb_tot, m * 128:(m + 1) * 128], reads=[vscr], writes=[vm_])
                ops_ = PSL()
                accs = [alloc(), alloc()]
                for hh in range(2):
                    S.op("pool", lambda e, a_=accs[hh]: e.memset(a_.bf(NQ), 0.0), writes=[accs[hh]])
                first = True
                for (kind, kb, klen, qo) in blocks:
                    for hh in range(2):
                        acc_ = accs[hh]
                        ACC = acc_.bf(NQ)
                        nq = NQ - qo
                        qsl = slice(qc0 + qo, qc0 + NQ)
                        ksl = slice(kb * 128, kb * 128 + klen)
                        zps = PS()
                        MM(zps[:klen, :nq], KTm[:, ksl], QTz[hh][:, m, qsl], True, True, [ktm_, qzb[hh]], [zps])
                        e_ = alloc()
                        ACT(e_.f32(NQ)[:klen, :nq], zps[:klen, :nq], AF.Exp, [zps], [e_])
                        sp_ = alloc()
                        SP = sp_.bf(NQ)
                        ACT(SP[:klen, :nq], e_.f32(NQ)[:klen, :nq], AF.Ln, [e_], [sp_], bias=1.0)
                        e_.free()
                        nm_ = min(128, nq)
                        if kind == "d":
                            TT(SP[:klen, :nm_], SP[:klen, :nm_], mlt_bf[:klen, :nm_], ALU.mult, [sp_, mlt_bf], [sp_])
                        tps = PS()
                        MM(tps[:klen, :nq], nuinc_bf[:klen, :klen], SP[:klen, :nq], True, False, [nuinc_bf, sp_], [tps])
                        if not first:
                            MM(tps[:klen, :nq], nones_bf[:, :klen], ACC[:, qo:NQ], False, False, [nones_bf, acc_], [tps])
                        MM(tps[:klen, :nq], KTm[:, ksl], QTz[hh][:, m, qsl], False, True, [ktm_, qzb[hh]], [tps])
                        at_ = alloc()
                        ATT = at_.bf(NQ)
                        ACT(ATT[:klen, :nq], tps[:klen, :nq], AF.Exp, [tps], [at_])
                        if kind == "d":
                            TT(ATT[:klen, :nm_], ATT[:klen, :nm_], mlt_bf[:klen, :nm_], ALU.mult, [at_, mlt_bf], [at_])
                        MM(ops_[hh * 64:hh * 64 + 64, qo:NQ], Vm[:klen, kb, hh * 64:hh * 64 + 64], ATT[:klen, :nq],
                           first, False, [vm_, at_], [ops_], skip=True)
                        TT(ACC[:klen, qo:NQ], ACC[:klen, qo:NQ], SP[:klen, :nq], ALU.add, [acc_, sp_], [acc_], eng="pool")
                        at_.free()
                        sp_.free()
                    first = False
                for a_ in accs:
                    a_.free()
                if cfg.debug and l == 0 and m == 0 and si == 0:
                    do_ = alloc()
                    CP(do_.f32(NQ), ops_[:, :NQ], [ops_], [do_], eng="dve")
                    dbg("oraw_%s%d" % (grp, tok0), do_, do_.f32(NQ), [128, NQ])
                    do_.free()
                TT(oT[:, 12 + m, qc0:qc0 + NQ], ops_[:, :NQ], sgc[:, m, qc0:qc0 + NQ], ALU.mult, [ops_, sgb_],
                   [(oT, 12 + m)])
                ktm_.free()
                vm_.free()
        for t_ in (qzb[0], qzb[1], sgb_):
            t_.free()
        if cfg.debug and l == 0:
            dbg("oc_%s%d" % (grp, tok0), oT, oT[:, 12:16, :N], [128, 4, N], BF16)
        if cfg.stop == 'SB':
            raise Stop()
        opsB = S.end_capture()
        ctx[0] = "G"
        S.merge([opsA, opsB])
        wo_ = alloc(16)
        wov = wo_.bf(16, D)
        for c4 in range(0, 16, 4):
            S.dma("sp", wov[:, c4:c4 + 4, :], w_out_bf.h[l, :, c4:c4 + 4, :], reads=[w_out_bf], writes=[wo_])
        for b in range(nb):
            xr_ = alloc(2)
            xr = xr_.f32(D)
            S.dma("sp", xr[:PB, :], src.h[tok0 + b * PB:tok0 + (b + 1) * PB, :], reads=[src], writes=[xr_])
            for nh in range(2):
                pso = PS()
                for c in range(16):
                    MM(pso[:PB, :512], oT[:, c, b * PB:(b + 1) * PB], wov[:, c, nh * 512:(nh + 1) * 512], c == 0, c == 15,
                       [oT, wo_], [pso])
                TT(xr[:PB, nh * 512:(nh + 1) * 512], xr[:PB, nh * 512:(nh + 1) * 512], pso[:PB, :512], ALU.add,
                   [xr_, pso], [xr_])
            S.dma("pool", dst.h[tok0 + b * PB:tok0 + (b + 1) * PB, :], xr[:PB, :], reads=[xr_], writes=[dst])
            if cfg.debug and l == 0 and b == 0:
                dbg("y_%s%d" % (grp, tok0), xr_, xr[:PB, :], [PB, D])
            xr_.free()
        wo_.free()
        return N

    def load_ssm(l, bi, slot):
        si_ = alloc(2)
        sv = si_.f32(6, 128)
        S.dma("sp", sv, st_ssm.h[l, bi].rearrange("(m hh) p n -> (hh p) m n", hh=2), reads=[st_ssm], writes=[si_])
        for half in range(2):
            pt = PS()
            for q in range(3):
                m = half * 3 + q
                TR(pt[:, q * 128:(q + 1) * 128], sv[:, m, :], ident[:, :], [si_, ident], [pt])
            CP(ST_s[:, slot, half * 384:(half + 1) * 384], pt[:, :384], [pt], [(ST_s, slot)])
        CP(STb_s[:, slot, :], ST_s[:, slot, :], [(ST_s, slot)], [(STb_s, slot)], eng="dve")
        si_.free()

    def load_rwkv_states(l):
        for slot in range(NSS):
            si_ = alloc(2)
            sv = si_.f32(6, 128)
            S.dma("sp", sv[:64].rearrange("p m (hh j) -> p m hh j", hh=2),
                  st_rwkv.h[l, slot].rearrange("(m hh) i j -> i m hh j", hh=2), reads=[st_rwkv], writes=[si_])
            for half in range(2):
                pt = PS()
                for q in range(3):
                    m = half * 3 + q
                    TR(pt[:, q * 64:(q + 1) * 64], sv[:64, m, :], ident[:64, :64], [si_, ident], [pt])
                CP(ST_r[:, slot, half * 3:half * 3 + 3, :], pt[:, :192].rearrange("p (m i) -> p m i", i=64), [pt],
                   [ST_r])
            si_.free()
            CP(STb_r[:, slot, :, :], ST_r[:, slot, :, :], [ST_r], [STb_r], eng="dve")
            CP(STblk[0:64, slot, :, 0:64], ST_r[0:64, slot, :, :], [ST_r], [STblk], eng="dve")
            CP(STblk[64:128, slot, :, 64:128], ST_r[64:128, slot, :, :], [ST_r], [STblk], eng="pool")
            S.dma("sp", shc[:, :, slot], st_shift.h[l, slot].rearrange("(c p) -> p c", p=128), reads=[st_shift],
                  writes=[shc], allow_slow_non_contiguous=True)
            for r_i in range(3):
                S.dma("sp", cvc[:, :, slot, r_i], st_conv.h[l, slot, r_i].rearrange("(j p) -> p j", p=128),
                      reads=[st_conv], writes=[cvc], allow_slow_non_contiguous=True)

    def load_kv_cache(l, bi):
        npast = PAST // 128
        for h in range(8):
            S.dma("pool", vscr.h[:, 0:npast, h * 64:(h + 1) * 64],
                  cv_d.h[l, bi, h].rearrange("(kb p) d -> p kb d", p=128), reads=[cv_d], writes=[(vscr, "cache")])
        kc_ = alloc(4)
        kct = kc_.bf(npast, 512)
        for h in range(8):
            S.dma("pool", kct[:, :, h * 64:(h + 1) * 64], ck_d.h[l, bi, h].rearrange("(kb p) d -> p kb d", p=128),
                  reads=[ck_d], writes=[kc_])
        for m in range(4):
            for kb0 in range(0, npast, 4):
                pt = PS()
                ptb = pt[:].bitcast(BF16)
                for q in range(4):
                    TR(ptb[:, q * 128:(q + 1) * 128], kct[:, kb0 + q, m * 128:(m + 1) * 128], ident_bf[:, :],
                       [kc_, ident_bf], [pt])
                kt_ = alloc()
                CP(kt_.bf(512), ptb[:, :512], [pt], [kt_], eng=("dve" if (kb0 // 4) % 2 else "act"))
                S.dma("pool", ktscr.h[:, m, kb0 * 128:(kb0 + 4) * 128], kt_.bf(512), reads=[kt_],
                      writes=[(ktscr, "cache")])
                kt_.free()
        kc_.free()

    try:
        for l in range(cfg.nlayers):
            lastl = (l == DEPTH - 1)
            src = xp if l == 0 else xmid
            dst = yp if lastl else xmid
            for s_ in range(NSP):
                for t_ in (ST_r, STb_r, STblk, shc, cvc, ST_s, STb_s):
                    S.op("dve", lambda e, t_=t_: e.memset(t_[:], 0.0), writes=[t_])
                nt = TP // 512
                for ti in range(nt):
                    layer_tile(l, "p", src, dst, s_ * TP + ti * 512, [(s_, 0)], 512, ti * 512, ti == nt - 1)
            if cfg.do_sample:
                load_rwkv_states(l)
                layer_tile(l, "s", xs if l == 0 else xmid_s, ys if lastl else xmid_s, 0,
                           [(i, i) for i in range(NSS)], TSS, PAST, True)
    except Stop:
        pass

    S.finalize(stack)
    stack.close()
    return nc, dbg_outs


OUT_NAMES = ["yp", "ys", "p_rwkv", "p_shift", "p_ssm", "p_conv", "p_k", "p_v",
             "s_rwkv", "s_shift", "s_ssm", "s_conv", "s_k", "s_v"]


def core_inputs(inp, cfg, core, shared):
    NSP, TP, NSS = cfg.nseq_p, cfg.t_p, cfg.nseq_s
    im = dict(shared)
    im["xp"] = np.ascontiguousarray(np.asarray(inp["x_prompt"])[core * NSP:(core + 1) * NSP].reshape(NSP * TP, D))
    im["xs"] = np.ascontiguousarray(np.asarray(inp["x_sample"])[core * NSS:(core + 1) * NSS].reshape(NSS * 16, D))
    sl = slice(core * NSS, (core + 1) * NSS)
    im["st_rwkv"] = np.ascontiguousarray(np.asarray(inp["state_rwkv"])[:, sl])
    im["st_shift"] = np.ascontiguousarray(np.asarray(inp["state_rwkv_shift"])[:, sl, 0])
    im["st_ssm"] = np.ascontiguousarray(np.asarray(inp["state_ssm"])[:, sl])
    im["st_conv"] = np.ascontiguousarray(np.asarray(inp["state_conv"])[:, sl])
    im["ck"] = np.ascontiguousarray(np.asarray(inp["cache_sb_k"])[:, sl])
    im["cv"] = np.ascontiguousarray(np.asarray(inp["cache_sb_v"])[:, sl])
    return im


def shared_inputs(inp):
    cm = col_map()
    w_in = np.asarray(inp["w_in"], np.float32)
    w_in_p = np.zeros((DEPTH, D, NP_), np.float32)
    w_in_p[:, :, cm >= 0] = w_in[:, :, cm[cm >= 0]]
    sh = {}
    sh["w_in"] = np.ascontiguousarray(w_in_p.reshape(DEPTH, 8, 128, NP_).transpose(0, 2, 1, 3))
    sh["w_out"] = np.ascontiguousarray(np.asarray(inp["w_out"], np.float32).reshape(DEPTH, 16, 128, D).transpose(0, 2, 1, 3))
    sh["w2a2"] = np.ascontiguousarray(np.concatenate([np.asarray(inp["rwkv_w2"]), np.asarray(inp["rwkv_a2"])], axis=1))
    sh["params"] = np.stack([build_params(inp, l) for l in range(DEPTH)])
    for k, v in make_consts().items():
        sh["c_" + k] = v
    return sh


def gather(results, cfg, ncores):
    NSP, TP, NSS = cfg.nseq_p, cfg.t_p, cfg.nseq_s
    outs = []
    for nm in OUT_NAMES:
        parts = [np.asarray(results[c][nm]) for c in range(ncores)]
        if nm == "yp":
            o = np.concatenate([p.reshape(NSP, TP, D) for p in parts], axis=0)
        elif nm == "ys":
            o = np.concatenate([p.reshape(NSS, 16, D) for p in parts], axis=0)
        else:
            o = np.concatenate(parts, axis=1)
            if nm.endswith("_shift"):
                o = o[:, :, None, :]
        outs.append(np.ascontiguousarray(o.astype(np.float32)))
    return tuple(outs)


_CACHE = {}


def kernel(**inputs):
    ncores = 8
    B = np.asarray(inputs["x_prompt"]).shape[0]
    T = np.asarray(inputs["x_prompt"]).shape[1]
    BS = np.asarray(inputs["x_sample"]).shape[0]
    cfg = Cfg(nseq_p=B // ncores, t_p=T, nseq_s=BS // ncores)
    key = (cfg.nseq_p, cfg.t_p, cfg.nseq_s)
    if key not in _CACHE:
        _CACHE[key] = build(cfg)[0]
    nc = _CACHE[key]
    shared = shared_inputs(inputs)
    in_maps = [core_inputs(inputs, cfg, c, shared) for c in range(ncores)]
    res = run_bass_kernel_spmd(nc, in_maps, core_ids=list(range(ncores)))
    return gather(res.results, cfg, ncores)
```

```python
import numpy as np
import concourse.bass as bass
import concourse.mybir as mybir
from concourse.bass_utils import run_bass_kernel_spmd
from contextlib import ExitStack

F32 = mybir.dt.float32
BF16 = mybir.dt.bfloat16
AF = mybir.ActivationFunctionType
ALU = mybir.AluOpType
AX = mybir.AxisListType

ENGS = ("pe", "act", "dve", "pool", "sp")
NSLOT = 8


class Res:
    _n = 0

    def __init__(self, name=""):
        Res._n += 1
        self.id = Res._n
        self.name = name


class Tl(Res):
    def __init__(self, name, h):
        super().__init__(name)
        self.h = h

    def __getitem__(self, k):
        return self.h[k]


class Sched:
    def __init__(self, nc):
        self.nc = nc
        self.ops = []
        self.state = {}
        self.ndma = {e: 0 for e in ENGS}
        self._cap = None

    def capture(self):
        self._cap = []

    def end_capture(self):
        c, self._cap = self._cap, None
        return c

    def merge(self, lists):
        ptr = [0] * len(lists)
        while True:
            best, bf = -1, 2.0
            for i, lst in enumerate(lists):
                if ptr[i] < len(lst):
                    f = ptr[i] / len(lst)
                    if f < bf:
                        best, bf = i, f
            if best < 0:
                break
            it = lists[best][ptr[best]]
            ptr[best] += 1
            if it[0] == "op":
                self.op(it[1], it[2], it[3], it[4])
            else:
                self.dma(it[1], it[2], it[3], it[4], it[5], **it[6])

    @staticmethod
    def _flat(lst):
        out = []
        for r in lst:
            if hasattr(r, "regions"):
                out.extend(r.regions)
            else:
                out.append(r)
        return out

    def _deps(self, idx, reads, writes, eng, is_dma):
        deps = set()
        reads = self._flat(reads)
        writes = self._flat(writes)
        rk = "dma%d" % idx if is_dma else eng
        for r in reads:
            rid, key = (r[0].id, r[1]) if isinstance(r, tuple) else (r.id, None)
            st = self.state.setdefault(rid, {})
            ks = list(st.keys()) if key is None else [k for k in (key, None) if k in st]
            for k in ks:
                if st[k][0] is not None:
                    deps.add(st[k][0])
            ent = st.setdefault(key, [None, {}])
            ent[1][rk] = idx
        for w in writes:
            rid, key = (w[0].id, w[1]) if isinstance(w, tuple) else (w.id, None)
            st = self.state.setdefault(rid, {})
            ks = list(st.keys()) if key is None else [k for k in (key, None) if k in st]
            for k in ks:
                if st[k][0] is not None:
                    deps.add(st[k][0])
                deps.update(st[k][1].values())
            if key is None:
                st.clear()
            st[key] = [idx, {}]
        deps.discard(idx)
        return deps

    def op(self, eng, emit, reads=(), writes=()):
        if self._cap is not None:
            self._cap.append(("op", eng, emit, list(reads), list(writes)))
            return -1
        idx = len(self.ops)
        deps = self._deps(idx, reads, writes, eng, False)
        self.ops.append(dict(eng=eng, emit=emit, deps=deps, dma=False, sig=False))
        return idx

    def dma(self, eng, out, in_, reads=(), writes=(), **kw):
        if self._cap is not None:
            self._cap.append(("dma", eng, out, in_, list(reads), list(writes), kw))
            return -1
        idx = len(self.ops)
        deps = self._deps(idx, reads, writes, eng, True)
        n = self.ndma[eng]
        self.ndma[eng] += 1
        self.ops.append(dict(eng=eng, emit=lambda e: e.dma_start(out=out, in_=in_, **kw), deps=deps,
                             dma=True, sig=True, slot=n % NSLOT, val=16 * (n // NSLOT + 1)))
        return idx

    def finalize(self, stack):
        nc = self.nc
        ops = self.ops
        esem = {e: stack.enter_context(nc.semaphore("es_" + e)) for e in ENGS}
        dsem = {e: [stack.enter_context(nc.semaphore("ds_%s%d" % (e, i))) for i in range(NSLOT)]
                for e in ENGS if self.ndma[e] > 0}
        for i, o in enumerate(ops):
            for j in o["deps"]:
                pj = ops[j]
                if pj["dma"]:
                    continue
                if pj["eng"] == "pe" and o["eng"] == "pe":
                    continue
                pj["sig"] = True
        cnt = {e: 0 for e in ENGS}
        for o in ops:
            if o["dma"]:
                o["sem"] = dsem[o["eng"]][o["slot"]]
            elif o["sig"]:
                cnt[o["eng"]] += 1
                o["sem"] = esem[o["eng"]]
                o["val"] = cnt[o["eng"]]
        byeng = {e: [] for e in ENGS}
        for i, o in enumerate(ops):
            byeng[o["eng"]].append(i)

        def run(e, name):
            waited = {}
            for i in byeng[name]:
                o = ops[i]
                need = {}
                for j in o["deps"]:
                    pj = ops[j]
                    if (not pj["dma"]) and pj["eng"] == "pe" and name == "pe":
                        continue
                    s = pj["sem"]
                    if need.get(s, 0) < pj["val"]:
                        need[s] = pj["val"]
                if o["dma"] and o["val"] > 16:
                    s = o["sem"]
                    if need.get(s, 0) < o["val"] - 16:
                        need[s] = o["val"] - 16
                for s, v in need.items():
                    if waited.get(s, 0) < v:
                        e.wait_ge(s, v)
                        waited[s] = v
                ins = o["emit"](e)
                if o["dma"]:
                    ins.then_inc(o["sem"], 16)
                elif o["sig"]:
                    ins.then_inc(o["sem"], 1)
            if name in dsem:
                n = self.ndma[name]
                for s in range(NSLOT):
                    k = (n - s + NSLOT - 1) // NSLOT
                    if k > 0 and waited.get(dsem[name][s], 0) < 16 * k:
                        e.wait_ge(dsem[name][s], 16 * k)

        with nc.Block() as block:
            @block.tensor
            def _(e):
                run(e, "pe")

            @block.scalar
            def _(e):
                run(e, "act")

            @block.vector
            def _(e):
                run(e, "dve")

            @block.gpsimd
            def _(e):
                run(e, "pool")

            @block.sync
            def _(e):
                run(e, "sp")


D = 1024
DEPTH = 2
HD = 64
D_A = 768
D_B = 768
D_C = 512
H_A = 12
H_B = 12
H_C = 8
NST = 128
CONV_DIM = 1280
W_SHIFT = 2432
N_IN = 7308
GN_EPS = 64e-5
PAST = 1024
P_WA = 0
CT_ZB, CT_XS, CT_B, CT_C, CT_DT, CT_Q, CT_SK, CT_SV, CT_GC = 25, 31, 37, 39, 41, 42, 46, 50, 54


def P_R(m):
    return 1 + 4 * m


def P_K(m):
    return 2 + 4 * m


def P_V(m):
    return 3 + 4 * m


def P_GA(m):
    return 4 + 4 * m
NCT = 58
NP_ = NCT * 128


def col_map():
    m = -np.ones(NP_, np.int64)

    def put(pos, c0, n):
        m[pos * 128:pos * 128 + n] = np.arange(c0, c0 + n)
    put(P_WA, 2304, 128)
    for mm in range(6):
        put(P_R(mm), mm * 128, 128)
        put(P_K(mm), 768 + mm * 128, 128)
        put(P_V(mm), 1536 + mm * 128, 128)
        put(P_GA(mm), 2432 + mm * 128, 128)
    put(CT_ZB, 3200, 768)
    put(CT_XS, 3968, 1280)
    put(CT_DT, 5248, 12)
    put(CT_Q, 5260, 2048)
    return m


def param_layout():
    off = {}
    n = 0
    for name, k in (("normw", 8), ("mu", 19), ("w0", 6), ("a0", 6), ("k_k", 6), ("k_a", 6), ("r_k", 6),
                    ("ln_w", 6), ("ln_b", 6), ("cw0", 10), ("cw1", 10), ("cw2", 10), ("cw3", 10), ("cb", 10),
                    ("dtb", 1), ("alog", 1), ("Dv", 6), ("snw", 6), ("qnw", 1), ("knw", 1)):
        off[name] = n
        n += k
    return off, n


POFF, NPAR = param_layout()


def fm(v, ntile):
    return np.ascontiguousarray(np.asarray(v, np.float32).reshape(ntile, 128).T)


def build_params(inp, l):
    P = np.zeros((128, NPAR), np.float32)

    def put(name, arr):
        P[:, POFF[name]:POFF[name] + arr.shape[1]] = arr
    put("normw", fm(inp["norm_w"][l], 8))
    put("mu", fm(inp["rwkv_mu"][l], 19))
    for nm, key in (("w0", "rwkv_w0"), ("a0", "rwkv_a0"), ("k_k", "rwkv_k_k"), ("k_a", "rwkv_k_a"),
                    ("ln_w", "rwkv_ln_w"), ("ln_b", "rwkv_ln_b"), ("snw", "ssm_norm_w")):
        put(nm, fm(inp[key][l], 6))
    put("r_k", fm(np.asarray(inp["rwkv_r_k"][l]).reshape(-1), 6))
    for i in range(4):
        put("cw%d" % i, fm(inp["ssm_conv_w"][l][i], 10))
    put("cb", fm(inp["ssm_conv_b"][l], 10))
    dtb = np.zeros(128, np.float32)
    dtb[:12] = inp["ssm_dt_bias"][l]
    put("dtb", dtb[:, None])
    al = np.zeros(128, np.float32)
    al[:12] = inp["ssm_A_log"][l]
    put("alog", al[:, None])
    put("Dv", fm(np.repeat(np.asarray(inp["ssm_D"][l]), 64), 6))
    put("qnw", np.tile(np.asarray(inp["sb_q_norm_w"][l]), 2)[:, None])
    put("knw", np.tile(np.asarray(inp["sb_k_norm_w"][l]), 2)[:, None])
    return P


def make_consts():
    c = {}
    i = np.arange(128)
    c["ident"] = np.eye(128, dtype=np.float32)
    c["blk64"] = (i[:, None] // 64 == i[None, :] // 64).astype(np.float32)
    c["ones"] = np.ones((128, 128), np.float32)
    c["mlt"] = (i[:, None] < i[None, :]).astype(np.float32)
    c["mle"] = (i[:, None] <= i[None, :]).astype(np.float32)
    c["mgt"] = (i[:, None] > i[None, :]).astype(np.float32)
    c["nmle"] = np.where(i[:, None] <= i[None, :], 0.0, -30000.0).astype(np.float32)
    c["nuinc"] = -(i[:, None] >= i[None, :]).astype(np.float32)
    c["nones"] = -np.ones((128, 128), np.float32)
    e12 = np.zeros((128, 768), np.float32)
    for h in range(12):
        e12[h, h * 64:(h + 1) * 64] = 1.0
    c["e12"] = e12
    sel = np.zeros((128, 12 * 128), np.float32)
    for h in range(12):
        sel[h, h * 128:(h + 1) * 128] = 1.0
    c["sel12"] = sel
    for L in (64, 16):
        sl = np.zeros((128, 128), np.float32)
        sl[L - 1, :] = 1.0
        c["sell%d" % L] = sl
        rm = np.ones((128, 512 // L, L), np.float32)
        rm[:, :, 0] = 0.0
        c["rm%d" % L] = rm.reshape(128, 512)
    return c


CONST_SHAPES = {k: v.shape for k, v in make_consts().items()}


import os
KSEQ = os.environ.get('KSEQ', '')


class Stop(Exception):
    pass


class Cfg:
    def __init__(self, nseq_p=2, t_p=4096, nseq_s=4, debug=False, do_sample=True, nlayers=DEPTH):
        self.nseq_p = nseq_p
        self.t_p = t_p
        self.nseq_s = nseq_s
        self.ts = 16
        self.debug = debug
        self.do_sample = do_sample
        self.nlayers = nlayers
        import os
        self.stop = os.environ.get('KSTOP', '')


NREG = 56


def build(cfg):
    nc = bass.Bass("TRN2", target_bir_lowering=False)
    S = Sched(nc)
    stack = ExitStack()
    dbg_outs = []

    def dram(name, shape, dt, kind):
        return Tl(name, nc.dram_tensor(name, list(shape), dt, kind=kind).ap())

    def sb(name, shape, dt=F32):
        return Tl(name, stack.enter_context(nc.sbuf_tensor("s_" + name, list(shape), dt)))

    NSP, TP, NSS, TSS = cfg.nseq_p, cfg.t_p, cfg.nseq_s, cfg.ts
    NTOK_P = NSP * TP
    NTOK_S = NSS * TSS
    xp = dram("xp", [NTOK_P, D], F32, "ExternalInput")
    xs = dram("xs", [NTOK_S, D], F32, "ExternalInput")
    w_in = dram("w_in", [DEPTH, 128, 8, NP_], F32, "ExternalInput")
    w_out = dram("w_out", [DEPTH, 128, 16, D], F32, "ExternalInput")
    w2a2 = dram("w2a2", [DEPTH, 128, 768], F32, "ExternalInput")
    params = dram("params", [DEPTH, 128, NPAR], F32, "ExternalInput")
    cd = {k: dram("c_" + k, list(shp), F32, "ExternalInput") for k, shp in CONST_SHAPES.items()}
    st_rwkv = dram("st_rwkv", [DEPTH, NSS, 12, 64, 64], F32, "ExternalInput")
    st_shift = dram("st_shift", [DEPTH, NSS, W_SHIFT], F32, "ExternalInput")
    st_ssm = dram("st_ssm", [DEPTH, NSS, 12, 64, 128], F32, "ExternalInput")
    st_conv = dram("st_conv", [DEPTH, NSS, 3, CONV_DIM], F32, "ExternalInput")
    ck_d = dram("ck", [DEPTH, NSS, 8, PAST, 64], F32, "ExternalInput")
    cv_d = dram("cv", [DEPTH, NSS, 8, PAST, 64], F32, "ExternalInput")
    yp = dram("yp", [NTOK_P, D], F32, "ExternalOutput")
    ys = dram("ys", [NTOK_S, D], F32, "ExternalOutput")
    o_rwkv = {"p": dram("p_rwkv", [DEPTH, NSP, 12, 64, 64], F32, "ExternalOutput"),
              "s": dram("s_rwkv", [DEPTH, NSS, 12, 64, 64], F32, "ExternalOutput")}
    o_shift = {"p": dram("p_shift", [DEPTH, NSP, W_SHIFT], F32, "ExternalOutput"),
               "s": dram("s_shift", [DEPTH, NSS, W_SHIFT], F32, "ExternalOutput")}
    o_ssm = {"p": dram("p_ssm", [DEPTH, NSP, 12, 64, 128], F32, "ExternalOutput"),
             "s": dram("s_ssm", [DEPTH, NSS, 12, 64, 128], F32, "ExternalOutput")}
    o_conv = {"p": dram("p_conv", [DEPTH, NSP, 3, CONV_DIM], F32, "ExternalOutput"),
              "s": dram("s_conv", [DEPTH, NSS, 3, CONV_DIM], F32, "ExternalOutput")}
    o_k = {"p": dram("p_k", [DEPTH, NSP, 8, TP, 64], F32, "ExternalOutput"),
           "s": dram("s_k", [DEPTH, NSS, 8, TSS, 64], F32, "ExternalOutput")}
    o_v = {"p": dram("p_v", [DEPTH, NSP, 8, TP, 64], F32, "ExternalOutput"),
           "s": dram("s_v", [DEPTH, NSS, 8, TSS, 64], F32, "ExternalOutput")}
    w_in_bf = dram("w_in_bf", [DEPTH, 128, 8, NP_], BF16, "Internal")
    w_out_bf = dram("w_out_bf", [DEPTH, 128, 16, D], BF16, "Internal")
    xmid = dram("xmid", [NTOK_P, D], F32, "Internal")
    xmid_s = dram("xmid_s", [NTOK_S, D], F32, "Internal")

    def dbg(name, rd, ap, shape, dt=F32):
        if not cfg.debug:
            return
        o = dram("dbg_" + name, shape, dt, "ExternalOutput")
        S.dma("pool", o.h, ap, reads=[rd], writes=[o])
        dbg_outs.append("dbg_" + name)

    def ACT(out, in_, func, reads, writes, bias=None, scale=None, accum=None):
        kw = {}
        if bias is not None:
            kw["bias"] = bias
        if scale is not None:
            kw["scale"] = scale
        if accum is not None:
            kw["accum_out"] = accum
        S.op("act", lambda e: e.activation(out, in_, func, **kw), reads=reads, writes=writes)

    def TS(out, in0, s1, s2, op0, op1, reads, writes, eng="dve"):
        if s2 is None:
            S.op(eng, lambda e: e.tensor_scalar(out, in0, s1, None, op0), reads=reads, writes=writes)
        else:
            S.op(eng, lambda e: e.tensor_scalar(out, in0, s1, s2, op0, op1), reads=reads, writes=writes)

    def TT(out, in0, in1, op, reads, writes, eng="dve"):
        S.op(eng, lambda e: e.tensor_tensor(out, in0, in1, op), reads=reads, writes=writes)

    def STT(out, in0, sc, in1, op0, op1, reads, writes):
        S.op("dve", lambda e: e.scalar_tensor_tensor(out, in0, sc, in1, op0, op1), reads=reads, writes=writes)

    def MM(out, lhsT, rhs, start, stop, reads, writes, skip=False):
        if skip:
            S.op("pe", lambda e: e.matmul(out, lhsT, rhs, start=start, stop=stop, skip_group_check=True),
                 reads=reads, writes=writes)
        else:
            S.op("pe", lambda e: e.matmul(out, lhsT, rhs, start=start, stop=stop), reads=reads, writes=writes)

    def TR(out, in_, idn, reads, writes):
        S.op("pe", lambda e: e.transpose(out, in_, idn), reads=reads, writes=writes)

    def CP(out, in_, reads, writes, eng="act"):
        if eng == "act":
            S.op("act", lambda e: e.copy(out, in_), reads=reads, writes=writes)
        else:
            S.op(eng, lambda e: e.tensor_copy(out, in_), reads=reads, writes=writes)

    C = {}
    for k, shp in CONST_SHAPES.items():
        C[k] = sb("c_" + k, list(shp))
        S.dma("sp", C[k][:], cd[k].h, writes=[C[k]])
    Cb = {}
    for k in ("ident", "mlt", "nuinc", "nones"):
        Cb[k] = sb("cb_" + k, [128, 128], BF16)
        S.op("dve", lambda e, k=k: e.tensor_copy(Cb[k][:], C[k][:]), reads=[C[k]], writes=[Cb[k]])
    ident, ident_bf = C["ident"], Cb["ident"]
    par = [sb("par%d" % l, [128, NPAR]) for l in range(DEPTH)]
    omm = [sb("omm%d" % l, [128, 19]) for l in range(DEPTH)]
    aneg = [sb("aneg%d" % l, [128, 1]) for l in range(DEPTH)]
    w2a2f = sb("w2a2f", [128, 768])
    w2a2b = [sb("w2a2b%d" % l, [128, 768], BF16) for l in range(DEPTH)]
    for l in range(DEPTH):
        S.dma("sp", par[l][:], params.h[l], writes=[par[l]])
        c0 = POFF["mu"]
        TS(omm[l][:], par[l][:, c0:c0 + 19], -1.0, 1.0, ALU.mult, ALU.add, [par[l]], [omm[l]])
        c1 = POFF["alog"]
        ACT(aneg[l][:], par[l][:, c1:c1 + 1], AF.Exp, [par[l]], [aneg[l]])
        TS(aneg[l][:], aneg[l][:], -1.0, None, ALU.mult, None, [aneg[l]], [aneg[l]])
        S.dma("sp", w2a2f[:], w2a2.h[l], writes=[w2a2f])
        CP(w2a2b[l][:], w2a2f[:], [w2a2f], [w2a2b[l]], eng="dve")

    def pcol(l, name, i=0):
        c = POFF[name] + i
        return par[l][:, c:c + 1]

    for l in range(cfg.nlayers):
        for k in range(8):
            S.dma("pool", w_in_bf.h[l, :, k, :], w_in.h[l, :, k, :], reads=[w_in], writes=[(w_in_bf, (l, k))])
        for c in range(0, 16, 4):
            S.dma("pool", w_out_bf.h[l, :, c:c + 4, :], w_out.h[l, :, c:c + 4, :], reads=[w_out],
                  writes=[(w_out_bf, (l, c))])

    psb = [Tl("ps%d" % i, stack.enter_context(nc.psum_tensor("ps%d" % i, [128, 512], F32))) for i in range(8)]
    ps_i = [0]

    CTX = {"G": dict(rot=[0, 1, 2, 3, 4, 5], lng=[6, 7], lo=0, hi=NREG, wb=0),
           "A": dict(rot=[0, 1, 2], lng=[3], lo=0, hi=28, wb=0),
           "B": dict(rot=[4, 5, 6], lng=[7], lo=28, hi=NREG, wb=1)}
    for c_ in CTX.values():
        c_["pi"] = 0
        c_["li"] = 0
        c_["g"] = -1
        c_["wbi"] = 0
    ctx = ["G"]

    def PS():
        c_ = CTX[ctx[0]]
        t = psb[c_["rot"][c_["pi"] % len(c_["rot"])]]
        c_["pi"] += 1
        return t

    def PSL():
        c_ = CTX[ctx[0]]
        t = psb[c_["lng"][c_["li"] % len(c_["lng"])]]
        c_["li"] += 1
        return t

    ssq = sb("ssq", [128, 4])
    rstd = sb("rstd", [128, 4])
    hT = sb("hT", [128, 8, 512], BF16)
    wbuf = [[sb("wbuf%d_%d" % (j, i), [128, 8, 256], BF16) for i in range(2)] for j in range(2)]
    oT = sb("oT", [128, 16, 512], BF16)
    TK = max(TP, PAST + 4 * 128)
    NKB = (TK + 127) // 128
    ktscr = dram("ktscr", [128, 4, NKB * 128], BF16, "Internal")
    vscr = dram("vscr", [128, NKB, 512], BF16, "Internal")
    ST_r = sb("ST_r", [128, 4, 6, 64])
    STb_r = sb("STb_r", [128, 4, 6, 64], BF16)
    STblk = sb("STblk", [128, 4, 6, 128], BF16)
    shc = sb("shc", [128, 19, 4])
    cvc = sb("cvc", [128, 10, 4, 3])
    ST_s = sb("ST_s", [128, 2, 768])
    STb_s = sb("STb_s", [128, 2, 768], BF16)

    arena = stack.enter_context(nc.sbuf_tensor("s_arena", [128, NREG * 512], F32))
    regs = [Res("reg%d" % i) for i in range(NREG)]
    free = [True] * NREG

    class Al:
        def __init__(self, r0, n):
            self.r0, self.n = r0, n
            self.regions = regs[r0:r0 + n]

        def f32(self, *shape):
            n = int(np.prod(shape))
            assert n <= self.n * 512
            ap = arena[:, self.r0 * 512:self.r0 * 512 + n]
            return self._shape(ap, shape)

        def bf(self, *shape):
            n = int(np.prod(shape))
            assert n <= self.n * 1024 and n % 2 == 0
            ap = arena[:, self.r0 * 512:self.r0 * 512 + n // 2].bitcast(BF16)
            return self._shape(ap, shape)

        @staticmethod
        def _shape(ap, shape):
            if len(shape) == 1:
                return ap
            if len(shape) == 2:
                return ap.rearrange("p (a b) -> p a b", a=shape[0])
            return ap.rearrange("p (a b c) -> p a b c", a=shape[0], b=shape[1])

        def free(self):
            for i in range(self.r0, self.r0 + self.n):
                assert not free[i]
                free[i] = True

    def alloc(n=1):
        c_ = CTX[ctx[0]]
        for r0 in range(c_["lo"], c_["hi"] - n + 1):
            if all(free[r0:r0 + n]):
                for i in range(r0, r0 + n):
                    free[i] = False
                return Al(r0, n)
        raise RuntimeError("arena full")

    def layer_tile(l, grp, src, dst, tok0, seqs, ts_, pos0, last):
        nseq = len(seqs)
        N = nseq * ts_
        PB = min(128, N)
        nb = N // PB
        L = min(64, ts_)
        nch = N // L
        nlev = int(np.log2(L))
        cps = ts_ // L
        rm = C["rm%d" % L]
        sell = C["sell%d" % L]

        def cslot(c):
            return seqs[c // cps][1]

        jk = alloc()
        junk = jk.bf(D)
        for b in range(nb):
            xb_ = alloc(2)
            hb_ = alloc()
            xb, hb = xb_.f32(D), hb_.bf(D)
            S.dma("sp", xb[:PB, :], src.h[tok0 + b * PB:tok0 + (b + 1) * PB, :], reads=[src], writes=[xb_])
            ACT(junk[:PB, :], xb[:PB, :], AF.Square, [xb_], [jk, (ssq, b)], accum=ssq[:PB, b:b + 1])
            TS(rstd[:PB, b:b + 1], ssq[:PB, b:b + 1], 1.0 / D, 1e-6, ALU.mult, ALU.add, [(ssq, b)], [(rstd, b)])
            ACT(rstd[:PB, b:b + 1], rstd[:PB, b:b + 1], AF.Sqrt, [(rstd, b)], [(rstd, b)])
            S.op("dve", lambda e, b=b: e.reciprocal(rstd[:PB, b:b + 1], rstd[:PB, b:b + 1]),
                 reads=[(rstd, b)], writes=[(rstd, b)])
            TS(hb[:PB, :], xb[:PB, :], rstd[:PB, b:b + 1], None, ALU.mult, None, [xb_, (rstd, b)], [hb_])
            for half in range(2):
                pt = PS()
                ptb = pt[:].bitcast(BF16)
                for q in range(4):
                    ft = half * 4 + q
                    TR(ptb[:, q * 128:q * 128 + PB], hb[:PB, ft * 128:(ft + 1) * 128], ident_bf[:PB, :PB],
                       [hb_, ident_bf], [pt])
                for q in range(4):
                    ft = half * 4 + q
                    TS(hT[:, ft, b * PB:(b + 1) * PB], ptb[:, q * 128:q * 128 + PB], pcol(l, "normw", ft), None,
                       ALU.mult, None, [pt, par[l]], [(hT, (ft, b))])
            xb_.free()
            hb_.free()
        jk.free()

        for c_ in CTX.values():
            c_["g"] = -1

        def proj(ct):
            c_ = CTX[ctx[0]]
            g = ct // 2
            if g != c_["g"]:
                wb = wbuf[c_["wb"]][c_["wbi"] % 2]
                c_["wbi"] += 1
                S.dma("sp", wb[:, :, :], w_in_bf.h[l, :, :, g * 256:(g + 1) * 256], reads=[w_in_bf], writes=[wb])
                c_["g"], c_["cwb"] = g, wb
            wb = c_["cwb"]
            c = ct % 2
            pt = PS()
            for k in range(8):
                MM(pt[:, :N], wb[:, k, c * 128:(c + 1) * 128], hT[:, k, :N], k == 0, k == 7, [wb, hT], [pt])
            return pt

        def v3(ap):
            return ap.rearrange("p (s t) -> p s t", s=nseq)

        def shifted(ct, pt):
            o = alloc()
            ov = o.f32(N)
            mu = pcol(l, "mu", ct)
            om = omm[l][:, ct:ct + 1]
            TS(ov, pt[:, :N], om, None, ALU.mult, None, [pt, omm[l]], [o])
            if ts_ > 1:
                STT(v3(ov)[:, :, 1:], v3(pt[:, :N])[:, :, 0:ts_ - 1], mu, v3(ov)[:, :, 1:], ALU.mult, ALU.add,
                    [pt, par[l], o], [o])
            for si, (bi, slot) in enumerate(seqs):
                c0 = si * ts_
                STT(ov[:, c0:c0 + 1], shc[:, ct, slot:slot + 1], mu, ov[:, c0:c0 + 1], ALU.mult, ALU.add,
                    [(shc, (ct, slot)), par[l], o], [o])
                CP(shc[:, ct, slot:slot + 1], pt[:, c0 + ts_ - 1:c0 + ts_], [pt], [(shc, (ct, slot))], eng="dve")
            return o

        if cfg.stop == 'A':
            raise Stop()
        ctx[0] = "A"
        S.capture()
        wa_pt = proj(P_WA)
        wa = shifted(18, wa_pt)
        wab = alloc()
        wabv = wab.bf(N)
        ACT(wabv[0:64, :], wa.f32(N)[0:64, :], AF.Tanh, [wa], [wab])
        CP(wabv[64:128, :], wa.f32(N)[64:128, :], [wa], [wab], eng="dve")
        wa.free()
        lev = nlev
        if cfg.stop == 'WA':
            raise Stop()
        for m in range(6):
            r_ = shifted(m, proj(P_R(m)))
            k_ = shifted(6 + m, proj(P_K(m)))
            v_ = shifted(12 + m, proj(P_V(m)))
            gpt = proj(P_GA(m))
            sg = alloc()
            ACT(sg.f32(N), gpt[:, :N], AF.Silu, [gpt], [sg])
            rv, kv, vv = r_.f32(N), k_.f32(N), v_.f32(N)
            wps = PS()
            MM(wps[:, :N], w2a2b[l][0:64, m * 128:(m + 1) * 128], wabv[0:64, :], True, True, [w2a2b[l], wab], [wps])
            lw = alloc()
            ACT(lw.f32(N), wps[:, :N], AF.Sigmoid, [wps, par[l]], [lw], bias=pcol(l, "w0", m))
            TS(lw.f32(N), lw.f32(N), -float(np.exp(-0.5)), None, ALU.mult, None, [lw], [lw])
            aps = PS()
            MM(aps[:, :N], w2a2b[l][64:128, m * 128:(m + 1) * 128], wabv[64:128, :], True, True,
               [w2a2b[l], wab], [aps])
            a_ = alloc()
            ACT(a_.f32(N), aps[:, :N], AF.Sigmoid, [aps, par[l]], [a_], bias=pcol(l, "a0", m))
            kk = alloc()
            TS(kk.f32(N), kv, pcol(l, "k_k", m), None, ALU.mult, None, [k_, par[l]], [kk])
            sq = alloc()
            TT(sq.f32(N), kk.f32(N), kk.f32(N), ALU.mult, [kk], [sq])
            n2 = PS()
            MM(n2[:, :N], C["blk64"][:, :], sq.f32(N), True, True, [C["blk64"], sq], [n2])
            ACT(sq.f32(N), n2[:, :N], AF.Sqrt, [n2], [sq])
            TS(sq.f32(N), sq.f32(N), 1e-12, None, ALU.max, None, [sq], [sq])
            S.op("dve", lambda e, sq=sq: e.reciprocal(sq.f32(N), sq.f32(N)), reads=[sq], writes=[sq])
            TT(kk.f32(N), kk.f32(N), sq.f32(N), ALU.mult, [kk, sq], [kk])
            TS(sq.f32(N), a_.f32(N), -1.0, pcol(l, "k_a", m), ALU.add, ALU.mult, [a_, par[l]], [sq])
            kp = alloc()
            STT(kp.f32(N), sq.f32(N), 1.0, kv, ALU.add, ALU.mult, [sq, k_], [kp])
            STT(sq.f32(N), rv, pcol(l, "r_k", m), kp.f32(N), ALU.mult, ALU.mult, [r_, par[l], kp], [sq])
            rks = PS()
            MM(rks[:, :N], C["blk64"][:, :], sq.f32(N), True, True, [C["blk64"], sq], [rks])
            rkv = alloc()
            TT(rkv.f32(N), rks[:, :N], vv, ALU.mult, [rks, v_], [rkv])
            be = a_
            TT(be.f32(N), kk.f32(N), a_.f32(N), ALU.mult, [kk, a_], [be])
            cs = alloc()
            S.op("dve", lambda e, cs=cs, lw=lw: e.tensor_tensor_scan(cs.f32(N), rm[:, :N], lw.f32(N), 0.0,
                                                                     ALU.mult, ALU.add),
                 reads=[rm, lw], writes=[cs])
            TT(lw.f32(N), cs.f32(N), lw.f32(N), ALU.subtract, [cs, lw], [lw])
            e1 = alloc()
            ACT(e1.f32(N), cs.f32(N), AF.Exp, [cs], [e1])
            ACT(lw.f32(N), lw.f32(N), AF.Exp, [lw], [lw])
            ACT(cs.f32(N), cs.f32(N), AF.Exp, [cs], [cs], scale=-1.0)
            e2, e3 = cs, lw
            opb = alloc(7)
            opv = opb.bf(8, N)
            AbT, RbT, BtT, KtT, BgT, KgT, vbf = (opv[:, i, :] for i in range(7))
            blk_all = arena[:, (opb.r0 + 4) * 512:(opb.r0 + 7) * 512].bitcast(BF16)
            S.op("pool", lambda e, blk_all=blk_all: e.memset(blk_all, 0.0), writes=[opb])

            def blkv(i):
                return blk_all[:, i * 1024:i * 1024 + 2 * N].rearrange("p (c h t) -> p c h t", h=2, t=L)
            Ablk, Rblk, Bblk = blkv(0), blkv(1), blkv(2)

            def c3(ap):
                return ap.rearrange("p (c t) -> p c t", t=L)
            STT(AbT, kk.f32(N), -1.0, e3.f32(N), ALU.mult, ALU.mult, [kk, e3], [opb])
            TT(RbT, rv, e1.f32(N), ALU.mult, [r_, e1], [opb])
            TT(be.f32(N), be.f32(N), e2.f32(N), ALU.mult, [be, e2], [be])
            TT(kp.f32(N), kp.f32(N), e2.f32(N), ALU.mult, [kp, e2], [kp])
            CP(BtT, be.f32(N), [be], [opb])
            CP(KtT, kp.f32(N), [kp], [opb])
            for hh in range(2):
                rws = slice(hh * 64, hh * 64 + 64)
                CP(Ablk[rws, :, hh, :], c3(AbT)[rws], [opb], [opb], eng=("act" if hh else "dve"))
                CP(Rblk[rws, :, hh, :], c3(RbT)[rws], [opb], [opb], eng=("dve" if hh else "act"))
                CP(Bblk[rws, :, hh, :], c3(BtT)[rws], [opb], [opb], eng=("act" if hh else "dve"))
            eend = c3(e1.f32(N))[:, :, L - 1:L].to_broadcast([128, nch, L])
            TT(c3(BgT), c3(be.f32(N)), eend, ALU.mult, [be, e1], [opb])
            TT(c3(KgT), c3(kp.f32(N)), eend, ALU.mult, [kp, e1], [opb])
            CP(vbf, vv, [v_], [opb], eng="dve")
            for t_ in (r_, k_, v_, kk, sq, kp, be, e2, e3):
                t_.free()
            if cfg.stop == 'EW':
                raise Stop()
            yT = alloc()
            ncg = max(1, min(nch, 512 // (2 * L)))
            for ch0 in range(0, nch, ncg):
                ncc = min(ncg, nch - ch0)
                nbb = ncc * 2
                W = nbb * L
                mats = alloc(5)
                mv = mats.bf(10, 512)
                PT, P_, MakT, NrbT, NrkT, XT, P2, PT2 = (mv[:L, i, :W] for i in range(8))
                def first_stage(pi):
                    lh, rh = ((BtT, Ablk), (AbT, Bblk), (KtT, Ablk), (BtT, Rblk), (KtT, Rblk))[pi]
                    ps_ = PS()
                    for cl in range(ncc):
                        c = ch0 + cl
                        cols = slice(c * L, (c + 1) * L)
                        oc_ = slice(cl * 2 * L, (cl + 1) * 2 * L)
                        MM(ps_[:L, oc_], lh[:, cols], rh[:, c, :, :].rearrange("p h t -> p (h t)"), True, True,
                           [opb], [ps_])
                    return ps_

                def bm(name):
                    return C[name][:L, :L].unsqueeze(1).to_broadcast([L, nbb, L])

                def b3(ap):
                    return ap.rearrange("p (b t) -> p b t", t=L)
                ps_ = first_stage(0)
                TT(b3(PT), b3(ps_[:L, :W]), bm("mlt"), ALU.mult, [ps_, C["mlt"]], [mats])
                ps_ = first_stage(1)
                TT(b3(P_), b3(ps_[:L, :W]), bm("mgt"), ALU.mult, [ps_, C["mgt"]], [mats])
                ps_ = first_stage(2)
                TT(b3(MakT), b3(ps_[:L, :W]), bm("mlt"), ALU.mult, [ps_, C["mlt"]], [mats])
                ps_ = first_stage(3)
                TT(b3(NrbT), b3(ps_[:L, :W]), bm("mle"), ALU.mult, [ps_, C["mle"]], [mats])
                ps_ = first_stage(4)
                TT(b3(NrkT), b3(ps_[:L, :W]), bm("mle"), ALU.mult, [ps_, C["mle"]], [mats])
                TT(b3(XT), b3(PT), C["ident"][:L, :L].unsqueeze(1).to_broadcast([L, nbb, L]), ALU.add,
                   [mats, C["ident"]], [mats])
                pa, pta = P_, PT
                pb_, ptb_ = P2, PT2
                for k in range(1, lev):
                    psP, psQ, psX = PS(), PS(), PS()
                    for b in range(nbb):
                        oc_ = slice(b * L, (b + 1) * L)
                        MM(psP[:L, oc_], pta[:, oc_], pa[:, oc_], True, True, [mats], [psP])
                        if k < lev - 1:
                            MM(psQ[:L, oc_], pa[:, oc_], pta[:, oc_], True, True, [mats], [psQ])
                    CP(pb_, psP[:L, :W], [psP], [mats])
                    if k < lev - 1:
                        CP(ptb_, psQ[:L, :W], [psQ], [mats], eng="dve")
                    for b in range(nbb):
                        oc_ = slice(b * L, (b + 1) * L)
                        MM(psX[:L, oc_], pb_[:, oc_], XT[:, oc_], True, True, [mats], [psX])
                    TT(XT, psX[:L, :W], XT, ALU.add, [psX, mats], [mats])
                    pa, pta, pb_, ptb_ = pb_, ptb_, pa, pta
                if cfg.stop == 'LV':
                    raise Stop()
                tok = alloc(2)
                tkv = tok.bf(4, ncc * 128)
                Vtok, Bgtok, Kgtok = (tkv[:L, i, :].rearrange("p (c f) -> p c f", f=128) for i in (1, 2, 3))
                zb_ = alloc(2)
                Zb = zb_.bf(nbb, 128)
                for i, srcop in enumerate((AbT, vbf, BgT, KgT)):
                    pt = PS()
                    ptb = pt[:].bitcast(BF16)
                    for cl in range(ncc):
                        c = ch0 + cl
                        TR(ptb[:L, cl * 128:(cl + 1) * 128], srcop[:, c * L:(c + 1) * L], ident_bf[:, :],
                           [opb, ident_bf], [pt])
                    if i == 0:
                        CP(Zb[:L, :, 0:64], ptb[:L, :ncc * 128].rearrange("p (b j) -> p b j", j=64), [pt], [zb_])
                    else:
                        CP(tkv[:L, i, :], ptb[:L, :ncc * 128], [pt], [tok], eng=("dve" if i % 2 else "act"))
                psW = PS()
                for cl in range(ncc):
                    for hh in range(2):
                        b = cl * 2 + hh
                        MM(psW[:L, b * 64:(b + 1) * 64], MakT[:, b * L:(b + 1) * L], Vtok[:, cl, hh * 64:hh * 64 + 64],
                           True, True, [mats, tok], [psW])
                CP(Zb[:L, :, 64:128], psW[:L, :nbb * 64].rearrange("p (b j) -> p b j", j=64), [psW], [zb_], eng="dve")
                au_ = alloc(2)
                AU = au_.bf(nbb, 128)
                for b0 in range(0, nbb, 4):
                    nb4 = min(4, nbb - b0)
                    psZ = PS()
                    for b in range(b0, b0 + nb4):
                        MM(psZ[:L, (b - b0) * 128:(b - b0 + 1) * 128], XT[:, b * L:(b + 1) * L], Zb[:L, b, :],
                           True, True, [mats, zb_], [psZ])
                    CP(AU[:L, b0:b0 + nb4, :], psZ[:L, :nb4 * 128].rearrange("p (b j) -> p b j", j=128), [psZ], [au_])
                psG, psR = PS(), PS()
                for cl in range(ncc):
                    for hh in range(2):
                        b = cl * 2 + hh
                        rows = slice(hh * 64, hh * 64 + 64)
                        MM(psG[rows, cl * 64:(cl + 1) * 64], AU[:L, b, 0:64], Bgtok[:, cl, hh * 64:hh * 64 + 64],
                           True, True, [au_, tok], [psG])
                        MM(psR[rows, cl * L:(cl + 1) * L], AU[:L, b, 0:64], NrbT[:, b * L:(b + 1) * L],
                           True, True, [au_, mats], [psR])
                gr_ = alloc(2)
                Gblk = gr_.bf(4, 512)[:, 0, :ncc * 128].rearrange("p (c f) -> p c f", f=128)
                RhT = gr_.bf(4, 512)[:, 2, :ncc * L]
                S.op("pool", lambda e, gr_=gr_: e.memset(gr_.bf(4, 512)[:, 0, :], 0.0), writes=[gr_])
                for hh in range(2):
                    rws = slice(hh * 64, hh * 64 + 64)
                    CP(Gblk[rws, :, hh * 64:hh * 64 + 64], psG[rws, :ncc * 64].rearrange("p (c f) -> p c f", f=64),
                       [psG], [gr_], eng=("act" if hh else "dve"))
                TT(RhT, psR[:, :ncc * L], RbT[:, ch0 * L:(ch0 + ncc) * L], ALU.add, [psR, opb], [gr_])
                if cfg.stop == 'GR':
                    raise Stop()
                psY = PSL()
                for cl in range(ncc):
                    c = ch0 + cl
                    slot = cslot(c)
                    psS = PS()
                    for hh in range(2):
                        b = cl * 2 + hh
                        hc = slice(hh * 64, hh * 64 + 64)
                        yo = psY[:L, b * 64:(b + 1) * 64]
                        MM(yo, NrbT[:, b * L:(b + 1) * L], AU[:L, b, 64:128], b == 0, False, [mats, au_], [psY],
                           skip=True)
                        MM(yo, NrkT[:, b * L:(b + 1) * L], Vtok[:, cl, hc], False, False, [mats, tok], [psY],
                           skip=True)
                        so = psS[hc, 0:64]
                        MM(so, Bgtok[:, cl, hc], AU[:L, b, 64:128], True, False, [tok, au_], [psS])
                        MM(so, Kgtok[:, cl, hc], Vtok[:, cl, hc], False, False, [tok], [psS])
                    MM(psY[:L, cl * 128:(cl + 1) * 128], RhT[:, cl * L:(cl + 1) * L], STblk[:, slot, m, :], False, True,
                       [gr_, (STblk, (slot, m))], [psY], skip=True)
                    MM(psS[:, 0:64], Gblk[:, cl, :], STb_r[:, slot, m, :], False, True,
                       [gr_, (STb_r, (slot, m))], [psS])
                    cend = (c + 1) * L - 1
                    STT(ST_r[:, slot, m, :], ST_r[:, slot, m, :], e1.f32(N)[:, cend:cend + 1], psS[:, 0:64],
                        ALU.mult, ALU.add, [(ST_r, (slot, m)), e1, psS], [(ST_r, (slot, m))])
                    CP(STb_r[:, slot, m, :], ST_r[:, slot, m, :], [(ST_r, (slot, m))], [(STb_r, (slot, m))])
                    CP(STblk[0:64, slot, m, 0:64], ST_r[0:64, slot, m, :], [(ST_r, (slot, m))], [(STblk, (slot, m))],
                       eng="dve")
                    CP(STblk[64:128, slot, m, 64:128], ST_r[64:128, slot, m, :], [(ST_r, (slot, m))],
                       [(STblk, (slot, m))], eng="dve")
                if cfg.stop == 'SEQ':
                    raise Stop()
                gn = alloc(2)
                ysq = gn.f32(2, 512)[:L, 0, :nbb * 64]
                st4 = gn.f32(2, 512)[:L, 1, :]
                s1, s2, mean, rs_ = (st4[:, i * 16:i * 16 + nbb] for i in range(4))
                Y3 = psY[:L, :nbb * 64].rearrange("p (b i) -> p b i", i=64)
                S.op("dve", lambda e, s1=s1, Y3=Y3: e.reduce_sum(s1, Y3, AX.X), reads=[psY], writes=[gn])
                ACT(ysq, psY[:L, :nbb * 64], AF.Square, [psY], [gn])
                S.op("dve", lambda e, s2=s2, ysq=ysq: e.reduce_sum(s2, ysq.rearrange("p (b i) -> p b i", i=64), AX.X),
                     reads=[gn], writes=[gn])
                TS(mean, s1, 1.0 / 64, None, ALU.mult, None, [gn], [gn])
                TT(s1, mean, mean, ALU.mult, [gn], [gn])
                STT(s2, s2, 1.0 / 64, s1, ALU.mult, ALU.subtract, [gn], [gn])
                TS(s2, s2, GN_EPS, None, ALU.add, None, [gn], [gn])
                ACT(s2, s2, AF.Sqrt, [gn], [gn])
                S.op("dve", lambda e, rs_=rs_, s2=s2: e.reciprocal(rs_, s2), reads=[gn], writes=[gn])
                ysq3 = ysq.rearrange("p (b i) -> p b i", i=64)
                TT(ysq3, Y3, mean.unsqueeze(2).to_broadcast([L, nbb, 64]), ALU.subtract, [psY, gn], [gn])
                ynb = gn.bf(4, 512)[:L, 3, :nbb * 64]
                TT(ynb.rearrange("p (b i) -> p b i", i=64), ysq3, rs_.unsqueeze(2).to_broadcast([L, nbb, 64]),
                   ALU.mult, [gn], [gn])
                if cfg.debug and l == 0 and m == 0 and ch0 == 0:
                    dy = alloc(1)
                    CP(dy.f32(512)[:L, :nbb * 64], psY[:L, :nbb * 64], [psY], [dy])
                    dbg("yraw_%s%d" % (grp, tok0), dy, dy.f32(512)[:L, :nbb * 64], [L, nbb * 64])
                    dbg("ynb_%s%d" % (grp, tok0), gn, ynb, [L, nbb * 64], BF16)
                    dbg("st4_%s%d" % (grp, tok0), gn, st4[:, :64], [L, 64])
                    dy.free()
                pt = PS()
                ptb = pt[:].bitcast(BF16)
                for cl in range(ncc):
                    TR(ptb[:, cl * L:(cl + 1) * L], ynb[:, cl * 128:(cl + 1) * 128], ident_bf[:L, :L],
                       [gn, ident_bf], [pt])
                ACT(yT.f32(N)[:, ch0 * L:(ch0 + ncc) * L], ptb[:, :ncc * L], AF.Identity, [pt, par[l]], [yT],
                    bias=pcol(l, "ln_b", m), scale=pcol(l, "ln_w", m))
                for t_ in (mats, tok, zb_, au_, gr_, gn):
                    t_.free()
            if cfg.debug and l == 0 and m == 0:
                dbg("yT_%s%d" % (grp, tok0), yT, yT.f32(N), [128, N])
                dbg("rkv_%s%d" % (grp, tok0), rkv, rkv.f32(N), [128, N])
                dbg("sg_%s%d" % (grp, tok0), sg, sg.f32(N), [128, N])
            TT(yT.f32(N), yT.f32(N), rkv.f32(N), ALU.add, [yT, rkv], [yT])
            TT(oT[:, m, :N], yT.f32(N), sg.f32(N), ALU.mult, [yT, sg], [(oT, m)])
            for t_ in (yT, rkv, sg, e1, opb):
                t_.free()
        wab.free()
        if cfg.debug and l == 0:
            dbg("oa_%s%d" % (grp, tok0), oT, oT[:, 0:6, :N], [128, 6, N], BF16)
        if cfg.stop == 'OA':
            raise Stop()
        if last:
            for si, (bi, slot) in enumerate(seqs):
                so_ = alloc(2)
                sov = so_.f32(6, 128)
                for half in range(2):
                    pt = PS()
                    for q in range(3):
                        mm_ = half * 3 + q
                        TR(pt[:64, q * 128:(q + 1) * 128], ST_r[:, slot, mm_, :], ident[:, :],
                           [(ST_r, (slot, mm_)), ident], [pt])
                    CP(sov[:64, half * 3:half * 3 + 3, :], pt[:64, :384].rearrange("p (m f) -> p m f", f=128),
                       [pt], [so_])
                S.dma("pool", o_rwkv[grp].h[l, bi].rearrange("(m hh) i j -> i m hh j", hh=2),
                      sov[:64, :, :].rearrange("p m (hh j) -> p m hh j", hh=2), reads=[so_], writes=[o_rwkv[grp]])
                so_.free()
                S.dma("pool", o_shift[grp].h[l, bi].rearrange("(c p) -> p c", p=128), shc[:, :, slot],
                      reads=[shc], writes=[o_shift[grp]], allow_slow_non_contiguous=True)

        opsA = S.end_capture()
        ctx[0] = "B"
        S.capture()

        def c3(ap):
            return ap.rearrange("p (c t) -> p c t", t=L)
        szb = alloc(3)
        szT = szb.bf(6, N)
        for m in range(6):
            pt = proj(CT_ZB + m)
            ACT(szT[:, m, :], pt[:, :N], AF.Silu, [pt], [szb])
        xsb = alloc(6)
        xs_f = xsb.f32(6, N)
        bcb = alloc(2)
        BC = bcb.bf(4, N)
        for j in range(10):
            pt = proj(CT_XS + j)
            acc = alloc()
            av = acc.f32(N)
            cw = [pcol(l, "cw%d" % i, j) for i in range(4)]
            TS(av, pt[:, :N], cw[3], pcol(l, "cb", j), ALU.mult, ALU.add, [pt, par[l]], [acc])
            for d_ in (1, 2, 3):
                STT(v3(av)[:, :, d_:], v3(pt[:, :N])[:, :, 0:ts_ - d_], cw[3 - d_], v3(av)[:, :, d_:], ALU.mult, ALU.add,
                    [pt, par[l], acc], [acc])
            for si, (bi, slot) in enumerate(seqs):
                c0 = si * ts_
                for d_ in (3, 2, 1):
                    STT(av[:, c0:c0 + d_], cvc[:, j, slot, 3 - d_:3], cw[3 - d_], av[:, c0:c0 + d_], ALU.mult, ALU.add,
                        [(cvc, (j, slot)), par[l], acc], [acc])
                CP(cvc[:, j, slot, :], pt[:, c0 + ts_ - 3:c0 + ts_], [pt], [(cvc, (j, slot))], eng="dve")
            if j < 6:
                ACT(xs_f[:, j, :], av, AF.Silu, [acc], [xsb])
            else:
                ACT(BC[:, j - 6, :], av, AF.Silu, [acc], [bcb])
            acc.free()
        pt = proj(CT_DT)
        dtb_ = alloc()
        dtT = dtb_.f32(N)
        ACT(dtT[0:12, :], pt[0:12, :N], AF.Exp, [pt, par[l]], [dtb_], bias=pcol(l, "dtb")[0:12, :])
        ACT(dtT[0:12, :], dtT[0:12, :], AF.Ln, [dtb_], [dtb_], bias=1.0)
        acb = alloc()
        acs = acb.f32(N)
        dab = alloc()
        TS(dab.f32(N)[0:12, :], dtT[0:12, :], aneg[l][0:12, :], None, ALU.mult, None, [dtb_, aneg[l]], [dab])
        S.op("dve", lambda e: e.tensor_tensor_scan(acs[0:12, :], rm[0:12, :N], dab.f32(N)[0:12, :], 0.0, ALU.mult, ALU.add),
             reads=[rm, dab], writes=[acb])
        dab.free()
        xdb = alloc(3)
        xdt = xdb.bf(6, N)
        for m in range(6):
            psd = PS()
            MM(psd[:, :N], C["e12"][0:12, m * 128:(m + 1) * 128], dtT[0:12, :], True, True, [C["e12"], dtb_], [psd])
            TT(xdt[:, m, :], xs_f[:, m, :], psd[:, :N], ALU.mult, [xsb, psd], [xdb])
        dtb_.free()
        ysb = alloc(6)
        yS = ysb.f32(6, N)
        for m in range(6):
            TS(yS[:, m, :], xs_f[:, m, :], pcol(l, "Dv", m), None, ALU.mult, None, [xsb, par[l]], [ysb])
        xsb.free()
        cpb = alloc(6)
        Cp = cpb.bf(12, N)
        for h in range(12):
            psb_ = PS()
            MM(psb_[:, :N], C["sel12"][0:12, h * 128:(h + 1) * 128], acs[0:12, :], True, True, [C["sel12"], acb], [psb_])
            ea = alloc()
            ACT(ea.f32(N), psb_[:, :N], AF.Exp, [psb_], [ea])
            TT(Cp[:, h, :], BC[:, 2 + h // 6, :], ea.f32(N), ALU.mult, [bcb, ea], [cpb])
            ea.free()
        for c in range(nch):
            slot = cslot(c) % 2
            cols = slice(c * L, (c + 1) * L)
            if grp == "s" and c % cps == 0:
                load_ssm(l, seqs[c // cps][0], slot)
            pt = PS()
            ptb = pt[:].bitcast(BF16)
            for m in range(6):
                TR(ptb[:L, m * 128:(m + 1) * 128], xdt[:, m, cols], ident_bf[:, :], [xdb, ident_bf], [pt])
            for g in range(2):
                TR(ptb[:L, 768 + g * 128:768 + (g + 1) * 128], BC[:, g, cols], ident_bf[:, :], [bcb, ident_bf], [pt])
            tkb = alloc()
            tokc = tkb.bf(1024)
            CP(tokc[:L, :], ptb[:L, :1024], [pt], [tkb])
            pa = PS()
            TR(pa[:L, 0:12], acs[0:12, cols], ident[0:12, 0:12], [acb, ident], [pa])
            atb = alloc()
            at = atb.f32(512)
            CP(at[:L, 0:12], pa[:L, 0:12], [pa], [atb], eng="dve")
            pe_ = PS()
            MM(pe_[:L, 0:12], sell[:L, :L], at[:L, 0:12], True, True, [sell, atb], [pe_])
            MM(pe_[:, 16:28], sell[:L, :], at[:L, 0:12], True, True, [sell, atb], [pe_])
            TT(at[:L, 16:28], pe_[:L, 0:12], at[:L, 0:12], ALU.subtract, [pe_, atb], [atb])
            ACT(at[:L, 32:44], at[:L, 16:28], AF.Exp, [atb], [atb])
            ACT(at[:, 48:60], pe_[:, 16:28], AF.Exp, [pe_], [atb])
            TS(at[:L, 64:76], at[:L, 0:12], -1.0, None, ALU.mult, None, [atb], [atb])
            gtb = alloc()
            GT = gtb.bf(12, L)
            for g in range(2):
                pq = PS()
                for hl in range(6):
                    h = g * 6 + hl
                    MM(pq[:L, hl * L:(hl + 1) * L], C["sel12"][0:12, h * 128:h * 128 + L], acs[0:12, cols], True, True,
                       [C["sel12"], acb], [pq])
                sgb = alloc()
                sg3 = sgb.f32(6, L)
                TT(sg3[:L], pq[:L, :6 * L].rearrange("p (h q) -> p h q", q=L),
                   at[:L, 64 + g * 6:64 + g * 6 + 6].unsqueeze(2).to_broadcast([L, 6, L]), ALU.add, [pq, atb], [sgb])
                TT(sg3[:L], sg3[:L], C["nmle"][:L, :L].unsqueeze(1).to_broadcast([L, 6, L]), ALU.add,
                   [sgb, C["nmle"]], [sgb])
                ACT(sg3[:L], sg3[:L], AF.Exp, [sgb], [sgb])
                pcb = PS()
                MM(pcb[:L, 0:L], BC[:, g, cols], BC[:, 2 + g, cols], True, True, [bcb], [pcb])
                TT(GT[:L, g * 6:(g + 1) * 6, :], sg3[:L], pcb[:L, 0:L].unsqueeze(1).to_broadcast([L, 6, L]), ALU.mult,
                   [sgb, pcb], [gtb])
                sgb.free()
            xeb = alloc()
            xdte = xeb.bf(768)
            TT(xdte[:L, :].rearrange("p (h q) -> p h q", q=64), tokc[:L, 0:768].rearrange("p (h q) -> p h q", q=64),
               at[:L, 32:44].unsqueeze(2).to_broadcast([L, 12, 64]), ALU.mult, [tkb, atb], [xeb])
            psy = PS()
            for m in range(6):
                for hh in range(2):
                    h = m * 2 + hh
                    hc = slice(h * 64, (h + 1) * 64)
                    out = psy[hh * 64:hh * 64 + 64, m * L:(m + 1) * L]
                    MM(out, tokc[:L, hc], GT[:L, h, :], m == 0, False, [tkb, gtb], [psy], skip=True)
                    MM(out, STb_s[:, slot, hc], Cp[:, h, cols], False, True, [(STb_s, slot), cpb], [psy], skip=True)
            TT(yS[:, :, cols], yS[:, :, cols], psy[:, :6 * L].rearrange("p (m q) -> p m q", q=L), ALU.add,
               [psy, ysb], [ysb])
            for g in range(2):
                pst = PS()
                MM(pst[:, :384], tokc[:L, 768 + g * 128:768 + (g + 1) * 128], xdte[:L, g * 384:(g + 1) * 384], True, True,
                   [tkb, xeb], [pst])
                stv = ST_s[:, slot, g * 384:(g + 1) * 384]
                TT(stv.rearrange("p (h q) -> p h q", q=64), stv.rearrange("p (h q) -> p h q", q=64),
                   at[:, 48 + g * 6:48 + g * 6 + 6].unsqueeze(2).to_broadcast([128, 6, 64]), ALU.mult,
                   [(ST_s, slot), atb], [(ST_s, slot)])
                TT(stv, stv, pst[:, :384], ALU.add, [(ST_s, slot), pst], [(ST_s, slot)])
            CP(STb_s[:, slot, :], ST_s[:, slot, :], [(ST_s, slot)], [(STb_s, slot)])
            for t_ in (tkb, atb, gtb, xeb):
                t_.free()
            if last and (c + 1) % cps == 0:
                bi = seqs[c // cps][0]
                sso = alloc(2)
                ssv = sso.f32(6, 128)
                for half in range(2):
                    pt = PS()
                    for q in range(3):
                        mm_ = half * 3 + q
                        TR(pt[:, q * 128:(q + 1) * 128], ST_s[:, slot, mm_ * 128:(mm_ + 1) * 128], ident[:, :],
                           [(ST_s, slot), ident], [pt])
                    CP(ssv[:, half * 3:half * 3 + 3, :], pt[:, :384].rearrange("p (m f) -> p m f", f=128), [pt], [sso])
                S.dma("pool", o_ssm[grp].h[l, bi].rearrange("(m hh) p n -> (hh p) m n", hh=2), ssv,
                      reads=[sso], writes=[o_ssm[grp]])
                sso.free()
                for r_i in range(3):
                    S.dma("pool", o_conv[grp].h[l, bi, r_i].rearrange("(j p) -> p j", p=128), cvc[:, :, cslot(c), r_i],
                          reads=[cvc], writes=[o_conv[grp]], allow_slow_non_contiguous=True)
        for t_ in (acb, xdb, cpb, bcb):
            t_.free()
        for m in range(6):
            TT(yS[:, m, :], yS[:, m, :], szT[:, m, :], ALU.mult, [ysb, szb], [ysb])
        szb.free()
        for g in range(2):
            pss_ = PS()
            for q in range(3):
                m = g * 3 + q
                sqb = alloc()
                TT(sqb.f32(N), yS[:, m, :], yS[:, m, :], ALU.mult, [ysb], [sqb])
                MM(pss_[:, :N], C["ones"][:, :], sqb.f32(N), q == 0, q == 2, [C["ones"], sqb], [pss_])
                sqb.free()
            rsb = alloc()
            TS(rsb.f32(N), pss_[:, :N], 1.0 / 384, 1e-5, ALU.mult, ALU.add, [pss_], [rsb])
            ACT(rsb.f32(N), rsb.f32(N), AF.Sqrt, [rsb], [rsb])
            S.op("dve", lambda e, rsb=rsb: e.reciprocal(rsb.f32(N), rsb.f32(N)), reads=[rsb], writes=[rsb])
            for q in range(3):
                m = g * 3 + q
                STT(oT[:, 6 + m, :N], yS[:, m, :], pcol(l, "snw", m), rsb.f32(N), ALU.mult, ALU.mult,
                    [ysb, par[l], rsb], [(oT, 6 + m)])
            rsb.free()
        ysb.free()
        if cfg.debug and l == 0:
            dbg("ob_%s%d" % (grp, tok0), oT, oT[:, 6:12, :N], [128, 6, N], BF16)
        if cfg.stop == 'SSD':
            raise Stop()
        NQ = ts_ if grp == "s" else N
        blk_len = min(128, ts_)
        nblk = ts_ // blk_len
        qzb = [alloc(2), alloc(2)]
        QTz = [qzb[i].bf(4, N) for i in range(2)]
        for i in range(2):
            S.op("pool", lambda e, i=i: e.memset(qzb[i].bf(4 * N), 0.0), writes=[qzb[i]])

        def headnorm(pt, wname):
            sq_ = alloc()
            ACT(sq_.f32(N), pt[:, :N], AF.Square, [pt], [sq_])
            ss_ = PS()
            MM(ss_[:, :N], C["blk64"][:, :], sq_.f32(N), True, True, [C["blk64"], sq_], [ss_])
            TS(sq_.f32(N), ss_[:, :N], 1.0 / 64, 1e-6, ALU.mult, ALU.add, [ss_], [sq_])
            ACT(sq_.f32(N), sq_.f32(N), AF.Sqrt, [sq_], [sq_])
            S.op("dve", lambda e, sq_=sq_: e.reciprocal(sq_.f32(N), sq_.f32(N)), reads=[sq_], writes=[sq_])
            STT(sq_.f32(N), pt[:, :N], pcol(l, wname), sq_.f32(N), ALU.mult, ALU.mult, [pt, par[l], sq_], [sq_])
            return sq_

        for m in range(4):
            qn = headnorm(proj(CT_Q + m), "qnw")
            for hh in range(2):
                rws = slice(hh * 64, hh * 64 + 64)
                TS(QTz[hh][rws, m, :], qn.f32(N)[rws, :], 0.125, None, ALU.mult, None, [qn], [qzb[hh]])
            qn.free()

        def tok_out(fm_al, m, odram, to_v):
            for si, (bi, slot) in enumerate(seqs):
                for jb in range(nblk):
                    c0 = si * ts_ + jb * blk_len
                    pt = PS()
                    TR(pt[:blk_len, 0:128], fm_al.f32(N)[:, c0:c0 + blk_len], ident[:, :], [fm_al, ident], [pt])
                    tk_ = alloc()
                    CP(tk_.f32(128)[:blk_len, :], pt[:blk_len, 0:128], [pt], [tk_], eng=("dve" if jb % 2 else "act"))
                    t0 = pos0 + jb * blk_len if grp == "p" else jb * blk_len
                    S.dma("pool", odram.h[l, bi, 2 * m:2 * m + 2, t0:t0 + blk_len, :].rearrange("h t d -> t h d"),
                          tk_.f32(128)[:blk_len, :].rearrange("p (h d) -> p h d", d=64), reads=[tk_], writes=[odram])
                    if to_v:
                        kb = (pos0 + jb * blk_len) // 128 if grp == "p" else PAST // 128
                        vb_ = alloc()
                        CP(vb_.bf(128)[:blk_len, :], tk_.f32(128)[:blk_len, :], [tk_], [vb_], eng="dve")
                        S.dma("pool", vscr.h[:blk_len, (si if grp == "s" else 0) * 0 + kb + (si if grp == "s" else 0),
                                             m * 128:(m + 1) * 128],
                              vb_.bf(128)[:blk_len, :], reads=[vb_], writes=[(vscr, ("new", si, m))])
                        vb_.free()
                    tk_.free()

        for m in range(4):
            kn = headnorm(proj(CT_SK + m), "knw")
            tok_out(kn, m, o_k[grp], False)
            kb_ = alloc()
            CP(kb_.bf(N), kn.f32(N), [kn], [kb_], eng="dve")
            for si, (bi, slot) in enumerate(seqs):
                kp0 = pos0 if grp == "p" else PAST + si * 128
                S.dma("pool", ktscr.h[:, m, kp0:kp0 + ts_], kb_.bf(N)[:, si * ts_:(si + 1) * ts_], reads=[kb_],
                      writes=[(ktscr, ("new", si, m))])
            kb_.free()
            kn.free()
        for m in range(4):
            pt = proj(CT_SV + m)
            vf_ = alloc()
            CP(vf_.f32(N), pt[:, :N], [pt], [vf_])
            tok_out(vf_, m, o_v[grp], True)
            vf_.free()
        if cfg.debug and l == 0:
            dbg("qtz0_%s%d" % (grp, tok0), qzb[0], QTz[0], [128, 4, N], BF16)
            dbg("qtz1_%s%d" % (grp, tok0), qzb[1], QTz[1], [128, 4, N], BF16)
        sgb_ = alloc(4)
        sgc = sgb_.f32(4, N)
        for m in range(4):
            pt = proj(CT_GC + m)
            ACT(sgc[:, m, :], pt[:, :N], AF.Silu, [pt], [sgb_])
        mlt_bf, nuinc_bf, nones_bf = Cb["mlt"], Cb["nuinc"], Cb["nones"]
        for si, (bi, slot) in enumerate(seqs):
            qc0 = si * ts_
            if grp == "p":
                npast = pos0 // 128
                blocks = [("d", npast + o, 128, o * 128) for o in range(nblk - 1, -1, -1)] + \
                         [("p", kb, 128, 0) for kb in range(npast - 1, -1, -1)]
                nkb_tot = npast + nblk
                klen_tot = nkb_tot * 128
                vrow = 0
            else:
                load_kv_cache(l, bi)
                npast = PAST // 128
                blocks = [("d", npast + si, ts_, 0)] + [("p", kb, 128, 0) for kb in range(npast - 1, -1, -1)]
                nkb_tot = npast + NSS
                klen_tot = nkb_tot * 128
            for m in range(4):
                nreg_k = (klen_tot * 2 + 2047) // 2048
                ktm_ = alloc(nreg_k)
                KTm = ktm_.bf(klen_tot)
                S.dma("sp", KTm, ktscr.h[:, m, 0:klen_tot], reads=[ktscr], writes=[ktm_])
                nreg_v = (nkb_tot * 128 * 2 + 2047) // 2048
                vm_ = alloc(nreg_v)
                Vm = vm_.bf(nkb_tot, 128)
                S.dma("sp", Vm, vscr.h[:, 0:nkb_tot, m * 128:(m + 1) * 128], reads=[vscr], writes=[vm_])
                ops_ = PSL()
                accs = [alloc(), alloc()]
                for hh in range(2):
                    S.op("pool", lambda e, a_=accs[hh]: e.memset(a_.bf(NQ), 0.0), writes=[accs[hh]])
                def sb_s1(hh, blk):
                    kind, kb, klen, qo = blk
                    nq = NQ - qo
                    qsl = slice(qc0 + qo, qc0 + NQ)
                    ksl = slice(kb * 128, kb * 128 + klen)
                    zps = PS()
                    MM(zps[:klen, :nq], KTm[:, ksl], QTz[hh][:, m, qsl], True, True, [ktm_, qzb[hh]], [zps])
                    e_ = alloc()
                    ACT(e_.f32(NQ)[:klen, :nq], zps[:klen, :nq], AF.Exp, [zps], [e_])
                    sp_ = alloc()
                    SP = sp_.bf(NQ)
                    ACT(SP[:klen, :nq], e_.f32(NQ)[:klen, :nq], AF.Ln, [e_], [sp_], bias=1.0)
                    e_.free()
                    nm_ = min(128, nq)
                    if kind == "d":
                        TT(SP[:klen, :nm_], SP[:klen, :nm_], mlt_bf[:klen, :nm_], ALU.mult, [sp_, mlt_bf], [sp_])
                    return sp_

                def sb_s2(hh, blk, sp_, first):
                    kind, kb, klen, qo = blk
                    acc_ = accs[hh]
                    ACC = acc_.bf(NQ)
                    SP = sp_.bf(NQ)
                    nq = NQ - qo
                    qsl = slice(qc0 + qo, qc0 + NQ)
                    ksl = slice(kb * 128, kb * 128 + klen)
                    nm_ = min(128, nq)
                    tps = PS()
                    MM(tps[:klen, :nq], nuinc_bf[:klen, :klen], SP[:klen, :nq], True, False, [nuinc_bf, sp_], [tps])
                    if not first:
                        MM(tps[:klen, :nq], nones_bf[:, :klen], ACC[:, qo:NQ], False, False, [nones_bf, acc_], [tps])
                    MM(tps[:klen, :nq], KTm[:, ksl], QTz[hh][:, m, qsl], False, True, [ktm_, qzb[hh]], [tps])
                    at_ = alloc()
                    ATT = at_.bf(NQ)
                    ACT(ATT[:klen, :nq], tps[:klen, :nq], AF.Exp, [tps], [at_])
                    if kind == "d":
                        TT(ATT[:klen, :nm_], ATT[:klen, :nm_], mlt_bf[:klen, :nm_], ALU.mult, [at_, mlt_bf], [at_])
                    MM(ops_[hh * 64:hh * 64 + 64, qo:NQ], Vm[:klen, kb, hh * 64:hh * 64 + 64], ATT[:klen, :nq],
                       first, False, [vm_, at_], [ops_], skip=True)
                    TT(ACC[:klen, qo:NQ], ACC[:klen, qo:NQ], SP[:klen, :nq], ALU.add, [acc_, sp_], [acc_], eng="pool")
                    at_.free()
                    sp_.free()

                pend = [sb_s1(0, blocks[0]), sb_s1(1, blocks[0])]
                for k_, blk in enumerate(blocks):
                    for hh in range(2):
                        nxt = sb_s1(hh, blocks[k_ + 1]) if k_ + 1 < len(blocks) else None
                        sb_s2(hh, blk, pend[hh], k_ == 0)
                        pend[hh] = nxt
                for a_ in accs:
                    a_.free()
                if cfg.debug and l == 0 and m == 0 and si == 0:
                    do_ = alloc()
                    CP(do_.f32(NQ), ops_[:, :NQ], [ops_], [do_], eng="dve")
                    dbg("oraw_%s%d" % (grp, tok0), do_, do_.f32(NQ), [128, NQ])
                    do_.free()
                TT(oT[:, 12 + m, qc0:qc0 + NQ], ops_[:, :NQ], sgc[:, m, qc0:qc0 + NQ], ALU.mult, [ops_, sgb_],
                   [(oT, 12 + m)])
                ktm_.free()
                vm_.free()
        for t_ in (qzb[0], qzb[1], sgb_):
            t_.free()
        if cfg.debug and l == 0:
            dbg("oc_%s%d" % (grp, tok0), oT, oT[:, 12:16, :N], [128, 4, N], BF16)
        if cfg.stop == 'SB':
            raise Stop()
        opsB = S.end_capture()
        ctx[0] = "G"
        S.merge([opsA, opsB])
        wo_ = alloc(16)
        wov = wo_.bf(16, D)
        for c4 in range(0, 16, 4):
            S.dma("sp", wov[:, c4:c4 + 4, :], w_out_bf.h[l, :, c4:c4 + 4, :], reads=[w_out_bf], writes=[wo_])
        for b in range(nb):
            xr_ = alloc(2)
            xr = xr_.f32(D)
            S.dma("sp", xr[:PB, :], src.h[tok0 + b * PB:tok0 + (b + 1) * PB, :], reads=[src], writes=[xr_])
            for nh in range(2):
                pso = PS()
                for c in range(16):
                    MM(pso[:PB, :512], oT[:, c, b * PB:(b + 1) * PB], wov[:, c, nh * 512:(nh + 1) * 512], c == 0, c == 15,
                       [oT, wo_], [pso])
                TT(xr[:PB, nh * 512:(nh + 1) * 512], xr[:PB, nh * 512:(nh + 1) * 512], pso[:PB, :512], ALU.add,
                   [xr_, pso], [xr_])
            S.dma("pool", dst.h[tok0 + b * PB:tok0 + (b + 1) * PB, :], xr[:PB, :], reads=[xr_], writes=[dst])
            if cfg.debug and l == 0 and b == 0:
                dbg("y_%s%d" % (grp, tok0), xr_, xr[:PB, :], [PB, D])
            xr_.free()
        wo_.free()
        return N

    def load_ssm(l, bi, slot):
        si_ = alloc(2)
        sv = si_.f32(6, 128)
        S.dma("sp", sv, st_ssm.h[l, bi].rearrange("(m hh) p n -> (hh p) m n", hh=2), reads=[st_ssm], writes=[si_])
        for half in range(2):
            pt = PS()
            for q in range(3):
                m = half * 3 + q
                TR(pt[:, q * 128:(q + 1) * 128], sv[:, m, :], ident[:, :], [si_, ident], [pt])
            CP(ST_s[:, slot, half * 384:(half + 1) * 384], pt[:, :384], [pt], [(ST_s, slot)])
        CP(STb_s[:, slot, :], ST_s[:, slot, :], [(ST_s, slot)], [(STb_s, slot)], eng="dve")
        si_.free()

    def load_rwkv_states(l):
        for slot in range(NSS):
            si_ = alloc(2)
            sv = si_.f32(6, 128)
            S.dma("sp", sv[:64].rearrange("p m (hh j) -> p m hh j", hh=2),
                  st_rwkv.h[l, slot].rearrange("(m hh) i j -> i m hh j", hh=2), reads=[st_rwkv], writes=[si_])
            for half in range(2):
                pt = PS()
                for q in range(3):
                    m = half * 3 + q
                    TR(pt[:, q * 64:(q + 1) * 64], sv[:64, m, :], ident[:64, :64], [si_, ident], [pt])
                CP(ST_r[:, slot, half * 3:half * 3 + 3, :], pt[:, :192].rearrange("p (m i) -> p m i", i=64), [pt],
                   [ST_r])
            si_.free()
            CP(STb_r[:, slot, :, :], ST_r[:, slot, :, :], [ST_r], [STb_r], eng="dve")
            CP(STblk[0:64, slot, :, 0:64], ST_r[0:64, slot, :, :], [ST_r], [STblk], eng="dve")
            CP(STblk[64:128, slot, :, 64:128], ST_r[64:128, slot, :, :], [ST_r], [STblk], eng="pool")
            S.dma("sp", shc[:, :, slot], st_shift.h[l, slot].rearrange("(c p) -> p c", p=128), reads=[st_shift],
                  writes=[shc], allow_slow_non_contiguous=True)
            for r_i in range(3):
                S.dma("sp", cvc[:, :, slot, r_i], st_conv.h[l, slot, r_i].rearrange("(j p) -> p j", p=128),
                      reads=[st_conv], writes=[cvc], allow_slow_non_contiguous=True)

    def load_kv_cache(l, bi):
        npast = PAST // 128
        for h in range(8):
            S.dma("pool", vscr.h[:, 0:npast, h * 64:(h + 1) * 64],
                  cv_d.h[l, bi, h].rearrange("(kb p) d -> p kb d", p=128), reads=[cv_d], writes=[(vscr, "cache")])
        kc_ = alloc(4)
        kct = kc_.bf(npast, 512)
        for h in range(8):
            S.dma("pool", kct[:, :, h * 64:(h + 1) * 64], ck_d.h[l, bi, h].rearrange("(kb p) d -> p kb d", p=128),
                  reads=[ck_d], writes=[kc_])
        for m in range(4):
            for kb0 in range(0, npast, 4):
                pt = PS()
                ptb = pt[:].bitcast(BF16)
                for q in range(4):
                    TR(ptb[:, q * 128:(q + 1) * 128], kct[:, kb0 + q, m * 128:(m + 1) * 128], ident_bf[:, :],
                       [kc_, ident_bf], [pt])
                kt_ = alloc()
                CP(kt_.bf(512), ptb[:, :512], [pt], [kt_], eng=("dve" if (kb0 // 4) % 2 else "act"))
                S.dma("pool", ktscr.h[:, m, kb0 * 128:(kb0 + 4) * 128], kt_.bf(512), reads=[kt_],
                      writes=[(ktscr, "cache")])
                kt_.free()
        kc_.free()

    try:
        for l in range(cfg.nlayers):
            lastl = (l == DEPTH - 1)
            src = xp if l == 0 else xmid
            dst = yp if lastl else xmid
            for s_ in range(NSP):
                for t_ in (ST_r, STb_r, STblk, shc, cvc, ST_s, STb_s):
                    S.op("dve", lambda e, t_=t_: e.memset(t_[:], 0.0), writes=[t_])
                nt = TP // 512
                for ti in range(nt):
                    layer_tile(l, "p", src, dst, s_ * TP + ti * 512, [(s_, 0)], 512, ti * 512, ti == nt - 1)
            if cfg.do_sample:
                load_rwkv_states(l)
                layer_tile(l, "s", xs if l == 0 else xmid_s, ys if lastl else xmid_s, 0,
                           [(i, i) for i in range(NSS)], TSS, PAST, True)
    except Stop:
        pass

    S.finalize(stack)
    stack.close()
    return nc, dbg_outs


OUT_NAMES = ["yp", "ys", "p_rwkv", "p_shift", "p_ssm", "p_conv", "p_k", "p_v",
             "s_rwkv", "s_shift", "s_ssm", "s_conv", "s_k", "s_v"]


def core_inputs(inp, cfg, core, shared):
    NSP, TP, NSS = cfg.nseq_p, cfg.t_p, cfg.nseq_s
    im = dict(shared)
    im["xp"] = np.ascontiguousarray(np.asarray(inp["x_prompt"])[core * NSP:(core + 1) * NSP].reshape(NSP * TP, D))
    im["xs"] = np.ascontiguousarray(np.asarray(inp["x_sample"])[core * NSS:(core + 1) * NSS].reshape(NSS * 16, D))
    sl = slice(core * NSS, (core + 1) * NSS)
    im["st_rwkv"] = np.ascontiguousarray(np.asarray(inp["state_rwkv"])[:, sl])
    im["st_shift"] = np.ascontiguousarray(np.asarray(inp["state_rwkv_shift"])[:, sl, 0])
    im["st_ssm"] = np.ascontiguousarray(np.asarray(inp["state_ssm"])[:, sl])
    im["st_conv"] = np.ascontiguousarray(np.asarray(inp["state_conv"])[:, sl])
    im["ck"] = np.ascontiguousarray(np.asarray(inp["cache_sb_k"])[:, sl])
    im["cv"] = np.ascontiguousarray(np.asarray(inp["cache_sb_v"])[:, sl])
    return im


def shared_inputs(inp):
    cm = col_map()
    w_in = np.asarray(inp["w_in"], np.float32)
    w_in_p = np.zeros((DEPTH, D, NP_), np.float32)
    w_in_p[:, :, cm >= 0] = w_in[:, :, cm[cm >= 0]]
    sh = {}
    sh["w_in"] = np.ascontiguousarray(w_in_p.reshape(DEPTH, 8, 128, NP_).transpose(0, 2, 1, 3))
    sh["w_out"] = np.ascontiguousarray(np.asarray(inp["w_out"], np.float32).reshape(DEPTH, 16, 128, D).transpose(0, 2, 1, 3))
    sh["w2a2"] = np.ascontiguousarray(np.concatenate([np.asarray(inp["rwkv_w2"]), np.asarray(inp["rwkv_a2"])], axis=1))
    sh["params"] = np.stack([build_params(inp, l) for l in range(DEPTH)])
    for k, v in make_consts().items():
        sh["c_" + k] = v
    return sh


def gather(results, cfg, ncores):
    NSP, TP, NSS = cfg.nseq_p, cfg.t_p, cfg.nseq_s
    outs = []
    for nm in OUT_NAMES:
        parts = [np.asarray(results[c][nm]) for c in range(ncores)]
        if nm == "yp":
            o = np.concatenate([p.reshape(NSP, TP, D) for p in parts], axis=0)
        elif nm == "ys":
            o = np.concatenate([p.reshape(NSS, 16, D) for p in parts], axis=0)
        else:
            o = np.concatenate(parts, axis=1)
            if nm.endswith("_shift"):
                o = o[:, :, None, :]
        outs.append(np.ascontiguousarray(o.astype(np.float32)))
    return tuple(outs)


_CACHE = {}


def kernel(**inputs):
    ncores = 8
    B = np.asarray(inputs["x_prompt"]).shape[0]
    T = np.asarray(inputs["x_prompt"]).shape[1]
    BS = np.asarray(inputs["x_sample"]).shape[0]
    cfg = Cfg(nseq_p=B // ncores, t_p=T, nseq_s=BS // ncores)
    key = (cfg.nseq_p, cfg.t_p, cfg.nseq_s)
    if key not in _CACHE:
        _CACHE[key] = build(cfg)[0]
    nc = _CACHE[key]
    shared = shared_inputs(inputs)
    in_maps = [core_inputs(inputs, cfg, c, shared) for c in range(ncores)]
    res = run_bass_kernel_spmd(nc, in_maps, core_ids=list(range(ncores)))
    return gather(res.results, cfg, ncores)
```

```python
import numpy as np
import concourse.bass as bass
import concourse.mybir as mybir
from concourse.bass_utils import run_bass_kernel_spmd
from contextlib import ExitStack

F32 = mybir.dt.float32
BF16 = mybir.dt.bfloat16
AF = mybir.ActivationFunctionType
ALU = mybir.AluOpType
AX = mybir.AxisListType

ENGS = ("pe", "act", "dve", "pool", "sp")
NSLOT = 8


class Res:
    _n = 0

    def __init__(self, name=""):
        Res._n += 1
        self.id = Res._n
        self.name = name


class Tl(Res):
    def __init__(self, name, h):
        super().__init__(name)
        self.h = h

    def __getitem__(self, k):
        return self.h[k]


class Sched:
    def __init__(self, nc):
        self.nc = nc
        self.ops = []
        self.state = {}
        self.ndma = {e: 0 for e in ENGS}
        self._cap = None

    def capture(self):
        self._cap = []

    def end_capture(self):
        c, self._cap = self._cap, None
        return c

    def merge(self, lists):
        ptr = [0] * len(lists)
        while True:
            best, bf = -1, 2.0
            for i, lst in enumerate(lists):
                if ptr[i] < len(lst):
                    f = ptr[i] / len(lst)
                    if f < bf:
                        best, bf = i, f
            if best < 0:
                break
            it = lists[best][ptr[best]]
            ptr[best] += 1
            if it[0] == "op":
                self.op(it[1], it[2], it[3], it[4])
            else:
                self.dma(it[1], it[2], it[3], it[4], it[5], **it[6])

    @staticmethod
    def _flat(lst):
        out = []
        for r in lst:
            if hasattr(r, "regions"):
                out.extend(r.regions)
            else:
                out.append(r)
        return out

    def _deps(self, idx, reads, writes, eng, is_dma):
        deps = set()
        reads = self._flat(reads)
        writes = self._flat(writes)
        rk = "dma%d" % idx if is_dma else eng
        for r in reads:
            rid, key = (r[0].id, r[1]) if isinstance(r, tuple) else (r.id, None)
            st = self.state.setdefault(rid, {})
            ks = list(st.keys()) if key is None else [k for k in (key, None) if k in st]
            for k in ks:
                if st[k][0] is not None:
                    deps.add(st[k][0])
            ent = st.setdefault(key, [None, {}])
            ent[1][rk] = idx
        for w in writes:
            rid, key = (w[0].id, w[1]) if isinstance(w, tuple) else (w.id, None)
            st = self.state.setdefault(rid, {})
            ks = list(st.keys()) if key is None else [k for k in (key, None) if k in st]
            for k in ks:
                if st[k][0] is not None:
                    deps.add(st[k][0])
                deps.update(st[k][1].values())
            if key is None:
                st.clear()
            st[key] = [idx, {}]
        deps.discard(idx)
        return deps

    def op(self, eng, emit, reads=(), writes=()):
        if self._cap is not None:
            self._cap.append(("op", eng, emit, list(reads), list(writes)))
            return -1
        idx = len(self.ops)
        deps = self._deps(idx, reads, writes, eng, False)
        self.ops.append(dict(eng=eng, emit=emit, deps=deps, dma=False, sig=False))
        return idx

    def dma(self, eng, out, in_, reads=(), writes=(), **kw):
        if self._cap is not None:
            self._cap.append(("dma", eng, out, in_, list(reads), list(writes), kw))
            return -1
        idx = len(self.ops)
        deps = self._deps(idx, reads, writes, eng, True)
        n = self.ndma[eng]
        self.ndma[eng] += 1
        self.ops.append(dict(eng=eng, emit=lambda e: e.dma_start(out=out, in_=in_, **kw), deps=deps,
                             dma=True, sig=True, slot=n % NSLOT, val=16 * (n // NSLOT + 1)))
        return idx

    def finalize(self, stack):
        nc = self.nc
        ops = self.ops
        esem = {e: stack.enter_context(nc.semaphore("es_" + e)) for e in ENGS}
        dsem = {e: [stack.enter_context(nc.semaphore("ds_%s%d" % (e, i))) for i in range(NSLOT)]
                for e in ENGS if self.ndma[e] > 0}
        for i, o in enumerate(ops):
            for j in o["deps"]:
                pj = ops[j]
                if pj["dma"]:
                    continue
                if pj["eng"] == "pe" and o["eng"] == "pe":
                    continue
                pj["sig"] = True
        cnt = {e: 0 for e in ENGS}
        for o in ops:
            if o["dma"]:
                o["sem"] = dsem[o["eng"]][o["slot"]]
            elif o["sig"]:
                cnt[o["eng"]] += 1
                o["sem"] = esem[o["eng"]]
                o["val"] = cnt[o["eng"]]
        byeng = {e: [] for e in ENGS}
        for i, o in enumerate(ops):
            byeng[o["eng"]].append(i)

        def run(e, name):
            waited = {}
            for i in byeng[name]:
                o = ops[i]
                need = {}
                for j in o["deps"]:
                    pj = ops[j]
                    if (not pj["dma"]) and pj["eng"] == "pe" and name == "pe":
                        continue
                    s = pj["sem"]
                    if need.get(s, 0) < pj["val"]:
                        need[s] = pj["val"]
                if o["dma"] and o["val"] > 16:
                    s = o["sem"]
                    if need.get(s, 0) < o["val"] - 16:
                        need[s] = o["val"] - 16
                for s, v in need.items():
                    if waited.get(s, 0) < v:
                        e.wait_ge(s, v)
                        waited[s] = v
                ins = o["emit"](e)
                if o["dma"]:
                    ins.then_inc(o["sem"], 16)
                elif o["sig"]:
                    ins.then_inc(o["sem"], 1)
            if name in dsem:
                n = self.ndma[name]
                for s in range(NSLOT):
                    k = (n - s + NSLOT - 1) // NSLOT
                    if k > 0 and waited.get(dsem[name][s], 0) < 16 * k:
                        e.wait_ge(dsem[name][s], 16 * k)

        with nc.Block() as block:
            @block.tensor
            def _(e):
                run(e, "pe")

            @block.scalar
            def _(e):
                run(e, "act")

            @block.vector
            def _(e):
                run(e, "dve")

            @block.gpsimd
            def _(e):
                run(e, "pool")

            @block.sync
            def _(e):
                run(e, "sp")


D = 1024
DEPTH = 2
HD = 64
D_A = 768
D_B = 768
D_C = 512
H_A = 12
H_B = 12
H_C = 8
NST = 128
CONV_DIM = 1280
W_SHIFT = 2432
N_IN = 7308
GN_EPS = 64e-5
PAST = 1024
P_WA = 0
CT_ZB, CT_XS, CT_B, CT_C, CT_DT, CT_Q, CT_SK, CT_SV, CT_GC = 25, 31, 37, 39, 41, 42, 46, 50, 54


def P_R(m):
    return 1 + 4 * m


def P_K(m):
    return 2 + 4 * m


def P_V(m):
    return 3 + 4 * m


def P_GA(m):
    return 4 + 4 * m
NCT = 58
NP_ = NCT * 128


def col_map():
    m = -np.ones(NP_, np.int64)

    def put(pos, c0, n):
        m[pos * 128:pos * 128 + n] = np.arange(c0, c0 + n)
    put(P_WA, 2304, 128)
    for mm in range(6):
        put(P_R(mm), mm * 128, 128)
        put(P_K(mm), 768 + mm * 128, 128)
        put(P_V(mm), 1536 + mm * 128, 128)
        put(P_GA(mm), 2432 + mm * 128, 128)
    put(CT_ZB, 3200, 768)
    put(CT_XS, 3968, 1280)
    put(CT_DT, 5248, 12)
    put(CT_Q, 5260, 2048)
    return m


def param_layout():
    off = {}
    n = 0
    for name, k in (("normw", 8), ("mu", 19), ("w0", 6), ("a0", 6), ("k_k", 6), ("k_a", 6), ("r_k", 6),
                    ("ln_w", 6), ("ln_b", 6), ("cw0", 10), ("cw1", 10), ("cw2", 10), ("cw3", 10), ("cb", 10),
                    ("dtb", 1), ("alog", 1), ("Dv", 6), ("snw", 6), ("qnw", 1), ("knw", 1)):
        off[name] = n
        n += k
    return off, n


POFF, NPAR = param_layout()


def fm(v, ntile):
    return np.ascontiguousarray(np.asarray(v, np.float32).reshape(ntile, 128).T)


def build_params(inp, l):
    P = np.zeros((128, NPAR), np.float32)

    def put(name, arr):
        P[:, POFF[name]:POFF[name] + arr.shape[1]] = arr
    put("normw", fm(inp["norm_w"][l], 8))
    put("mu", fm(inp["rwkv_mu"][l], 19))
    for nm, key in (("w0", "rwkv_w0"), ("a0", "rwkv_a0"), ("k_k", "rwkv_k_k"), ("k_a", "rwkv_k_a"),
                    ("ln_w", "rwkv_ln_w"), ("ln_b", "rwkv_ln_b"), ("snw", "ssm_norm_w")):
        put(nm, fm(inp[key][l], 6))
    put("r_k", fm(np.asarray(inp["rwkv_r_k"][l]).reshape(-1), 6))
    for i in range(4):
        put("cw%d" % i, fm(inp["ssm_conv_w"][l][i], 10))
    put("cb", fm(inp["ssm_conv_b"][l], 10))
    dtb = np.zeros(128, np.float32)
    dtb[:12] = inp["ssm_dt_bias"][l]
    put("dtb", dtb[:, None])
    al = np.zeros(128, np.float32)
    al[:12] = inp["ssm_A_log"][l]
    put("alog", al[:, None])
    put("Dv", fm(np.repeat(np.asarray(inp["ssm_D"][l]), 64), 6))
    put("qnw", np.tile(np.asarray(inp["sb_q_norm_w"][l]), 2)[:, None])
    put("knw", np.tile(np.asarray(inp["sb_k_norm_w"][l]), 2)[:, None])
    return P


def make_consts():
    c = {}
    i = np.arange(128)
    c["ident"] = np.eye(128, dtype=np.float32)
    c["blk64"] = (i[:, None] // 64 == i[None, :] // 64).astype(np.float32)
    c["ones"] = np.ones((128, 128), np.float32)
    c["mlt"] = (i[:, None] < i[None, :]).astype(np.float32)
    c["mle"] = (i[:, None] <= i[None, :]).astype(np.float32)
    c["mgt"] = (i[:, None] > i[None, :]).astype(np.float32)
    c["nmle"] = np.where(i[:, None] <= i[None, :], 0.0, -30000.0).astype(np.float32)
    c["nuinc"] = -(i[:, None] >= i[None, :]).astype(np.float32)
    c["nones"] = -np.ones((128, 128), np.float32)
    e12 = np.zeros((128, 768), np.float32)
    for h in range(12):
        e12[h, h * 64:(h + 1) * 64] = 1.0
    c["e12"] = e12
    sel = np.zeros((128, 12 * 128), np.float32)
    for h in range(12):
        sel[h, h * 128:(h + 1) * 128] = 1.0
    c["sel12"] = sel
    for L in (64, 16):
        sl = np.zeros((128, 128), np.float32)
        sl[L - 1, :] = 1.0
        c["sell%d" % L] = sl
        rm = np.ones((128, 512 // L, L), np.float32)
        rm[:, :, 0] = 0.0
        c["rm%d" % L] = rm.reshape(128, 512)
    return c


CONST_SHAPES = {k: v.shape for k, v in make_consts().items()}


import os
KSEQ = os.environ.get('KSEQ', '')


class Stop(Exception):
    pass


class Cfg:
    def __init__(self, nseq_p=2, t_p=4096, nseq_s=4, debug=False, do_sample=True, nlayers=DEPTH):
        self.nseq_p = nseq_p
        self.t_p = t_p
        self.nseq_s = nseq_s
        self.ts = 16
        self.debug = debug
        self.do_sample = do_sample
        self.nlayers = nlayers
        import os
        self.stop = os.environ.get('KSTOP', '')


NREG = 56


def build(cfg):
    nc = bass.Bass("TRN2", target_bir_lowering=False)
    S = Sched(nc)
    stack = ExitStack()
    dbg_outs = []

    def dram(name, shape, dt, kind):
        return Tl(name, nc.dram_tensor(name, list(shape), dt, kind=kind).ap())

    def sb(name, shape, dt=F32):
        return Tl(name, stack.enter_context(nc.sbuf_tensor("s_" + name, list(shape), dt)))

    NSP, TP, NSS, TSS = cfg.nseq_p, cfg.t_p, cfg.nseq_s, cfg.ts
    NTOK_P = NSP * TP
    NTOK_S = NSS * TSS
    xp = dram("xp", [NTOK_P, D], F32, "ExternalInput")
    xs = dram("xs", [NTOK_S, D], F32, "ExternalInput")
    w_in = dram("w_in", [DEPTH, 128, 8, NP_], F32, "ExternalInput")
    w_out = dram("w_out", [DEPTH, 128, 16, D], F32, "ExternalInput")
    w2a2 = dram("w2a2", [DEPTH, 128, 768], F32, "ExternalInput")
    params = dram("params", [DEPTH, 128, NPAR], F32, "ExternalInput")
    cd = {k: dram("c_" + k, list(shp), F32, "ExternalInput") for k, shp in CONST_SHAPES.items()}
    st_rwkv = dram("st_rwkv", [DEPTH, NSS, 12, 64, 64], F32, "ExternalInput")
    st_shift = dram("st_shift", [DEPTH, NSS, W_SHIFT], F32, "ExternalInput")
    st_ssm = dram("st_ssm", [DEPTH, NSS, 12, 64, 128], F32, "ExternalInput")
    st_conv = dram("st_conv", [DEPTH, NSS, 3, CONV_DIM], F32, "ExternalInput")
    ck_d = dram("ck", [DEPTH, NSS, 8, PAST, 64], F32, "ExternalInput")
    cv_d = dram("cv", [DEPTH, NSS, 8, PAST, 64], F32, "ExternalInput")
    yp = dram("yp", [NTOK_P, D], F32, "ExternalOutput")
    ys = dram("ys", [NTOK_S, D], F32, "ExternalOutput")
    o_rwkv = {"p": dram("p_rwkv", [DEPTH, NSP, 12, 64, 64], F32, "ExternalOutput"),
              "s": dram("s_rwkv", [DEPTH, NSS, 12, 64, 64], F32, "ExternalOutput")}
    o_shift = {"p": dram("p_shift", [DEPTH, NSP, W_SHIFT], F32, "ExternalOutput"),
               "s": dram("s_shift", [DEPTH, NSS, W_SHIFT], F32, "ExternalOutput")}
    o_ssm = {"p": dram("p_ssm", [DEPTH, NSP, 12, 64, 128], F32, "ExternalOutput"),
             "s": dram("s_ssm", [DEPTH, NSS, 12, 64, 128], F32, "ExternalOutput")}
    o_conv = {"p": dram("p_conv", [DEPTH, NSP, 3, CONV_DIM], F32, "ExternalOutput"),
              "s": dram("s_conv", [DEPTH, NSS, 3, CONV_DIM], F32, "ExternalOutput")}
    o_k = {"p": dram("p_k", [DEPTH, NSP, 8, TP, 64], F32, "ExternalOutput"),
           "s": dram("s_k", [DEPTH, NSS, 8, TSS, 64], F32, "ExternalOutput")}
    o_v = {"p": dram("p_v", [DEPTH, NSP, 8, TP, 64], F32, "ExternalOutput"),
           "s": dram("s_v", [DEPTH, NSS, 8, TSS, 64], F32, "ExternalOutput")}
    w_in_bf = dram("w_in_bf", [DEPTH, 128, 8, NP_], BF16, "Internal")
    w_out_bf = dram("w_out_bf", [DEPTH, 128, 16, D], BF16, "Internal")
    xmid = dram("xmid", [NTOK_P, D], F32, "Internal")
    xmid_s = dram("xmid_s", [NTOK_S, D], F32, "Internal")

    def dbg(name, rd, ap, shape, dt=F32):
        if not cfg.debug:
            return
        o = dram("dbg_" + name, shape, dt, "ExternalOutput")
        S.dma("pool", o.h, ap, reads=[rd], writes=[o])
        dbg_outs.append("dbg_" + name)

    def ACT(out, in_, func, reads, writes, bias=None, scale=None, accum=None):
        kw = {}
        if bias is not None:
            kw["bias"] = bias
        if scale is not None:
            kw["scale"] = scale
        if accum is not None:
            kw["accum_out"] = accum
        S.op("act", lambda e: e.activation(out, in_, func, **kw), reads=reads, writes=writes)

    def TS(out, in0, s1, s2, op0, op1, reads, writes, eng="dve"):
        if s2 is None:
            S.op(eng, lambda e: e.tensor_scalar(out, in0, s1, None, op0), reads=reads, writes=writes)
        else:
            S.op(eng, lambda e: e.tensor_scalar(out, in0, s1, s2, op0, op1), reads=reads, writes=writes)

    def TT(out, in0, in1, op, reads, writes, eng="dve"):
        S.op(eng, lambda e: e.tensor_tensor(out, in0, in1, op), reads=reads, writes=writes)

    def STT(out, in0, sc, in1, op0, op1, reads, writes):
        S.op("dve", lambda e: e.scalar_tensor_tensor(out, in0, sc, in1, op0, op1), reads=reads, writes=writes)

    def MM(out, lhsT, rhs, start, stop, reads, writes, skip=False):
        if skip:
            S.op("pe", lambda e: e.matmul(out, lhsT, rhs, start=start, stop=stop, skip_group_check=True),
                 reads=reads, writes=writes)
        else:
            S.op("pe", lambda e: e.matmul(out, lhsT, rhs, start=start, stop=stop), reads=reads, writes=writes)

    def TR(out, in_, idn, reads, writes):
        S.op("pe", lambda e: e.transpose(out, in_, idn), reads=reads, writes=writes)

    def CP(out, in_, reads, writes, eng="act"):
        if eng == "act":
            S.op("act", lambda e: e.copy(out, in_), reads=reads, writes=writes)
        else:
            S.op(eng, lambda e: e.tensor_copy(out, in_), reads=reads, writes=writes)

    C = {}
    for k, shp in CONST_SHAPES.items():
        C[k] = sb("c_" + k, list(shp))
        S.dma("sp", C[k][:], cd[k].h, writes=[C[k]])
    Cb = {}
    for k in ("ident", "mlt", "nuinc", "nones"):
        Cb[k] = sb("cb_" + k, [128, 128], BF16)
        S.op("dve", lambda e, k=k: e.tensor_copy(Cb[k][:], C[k][:]), reads=[C[k]], writes=[Cb[k]])
    ident, ident_bf = C["ident"], Cb["ident"]
    par = [sb("par%d" % l, [128, NPAR]) for l in range(DEPTH)]
    omm = [sb("omm%d" % l, [128, 19]) for l in range(DEPTH)]
    aneg = [sb("aneg%d" % l, [128, 1]) for l in range(DEPTH)]
    w2a2f = sb("w2a2f", [128, 768])
    w2a2b = [sb("w2a2b%d" % l, [128, 768], BF16) for l in range(DEPTH)]
    for l in range(DEPTH):
        S.dma("sp", par[l][:], params.h[l], writes=[par[l]])
        c0 = POFF["mu"]
        TS(omm[l][:], par[l][:, c0:c0 + 19], -1.0, 1.0, ALU.mult, ALU.add, [par[l]], [omm[l]])
        c1 = POFF["alog"]
        ACT(aneg[l][:], par[l][:, c1:c1 + 1], AF.Exp, [par[l]], [aneg[l]])
        TS(aneg[l][:], aneg[l][:], -1.0, None, ALU.mult, None, [aneg[l]], [aneg[l]])
        S.dma("sp", w2a2f[:], w2a2.h[l], writes=[w2a2f])
        CP(w2a2b[l][:], w2a2f[:], [w2a2f], [w2a2b[l]], eng="dve")

    def pcol(l, name, i=0):
        c = POFF[name] + i
        return par[l][:, c:c + 1]

    for l in range(cfg.nlayers):
        for k in range(8):
            S.dma("pool", w_in_bf.h[l, :, k, :], w_in.h[l, :, k, :], reads=[w_in], writes=[(w_in_bf, (l, k))])
        for c in range(0, 16, 4):
            S.dma("pool", w_out_bf.h[l, :, c:c + 4, :], w_out.h[l, :, c:c + 4, :], reads=[w_out],
                  writes=[(w_out_bf, (l, c))])

    psb = [Tl("ps%d" % i, stack.enter_context(nc.psum_tensor("ps%d" % i, [128, 512], F32))) for i in range(8)]
    ps_i = [0]

    CTX = {"G": dict(rot=[0, 1, 2, 3, 4, 5], lng=[6, 7], lo=0, hi=NREG, wb=0),
           "A": dict(rot=[0, 1, 2], lng=[3], lo=0, hi=28, wb=0),
           "B": dict(rot=[4, 5, 6], lng=[7], lo=28, hi=NREG, wb=1)}
    for c_ in CTX.values():
        c_["pi"] = 0
        c_["li"] = 0
        c_["g"] = -1
        c_["wbi"] = 0
    ctx = ["G"]

    def PS():
        c_ = CTX[ctx[0]]
        t = psb[c_["rot"][c_["pi"] % len(c_["rot"])]]
        c_["pi"] += 1
        return t

    def PSL():
        c_ = CTX[ctx[0]]
        t = psb[c_["lng"][c_["li"] % len(c_["lng"])]]
        c_["li"] += 1
        return t

    ssq = sb("ssq", [128, 4])
    rstd = sb("rstd", [128, 4])
    hT = sb("hT", [128, 8, 512], BF16)
    wbuf = [[sb("wbuf%d_%d" % (j, i), [128, 8, 256], BF16) for i in range(2)] for j in range(2)]
    oT = sb("oT", [128, 16, 512], BF16)
    TK = max(TP, PAST + 4 * 128)
    NKB = (TK + 127) // 128
    ktscr = dram("ktscr", [128, 4, NKB * 128], BF16, "Internal")
    vscr = dram("vscr", [128, NKB, 512], BF16, "Internal")
    ST_r = sb("ST_r", [128, 4, 6, 64])
    STb_r = sb("STb_r", [128, 4, 6, 64], BF16)
    STblk = sb("STblk", [128, 4, 6, 128], BF16)
    shc = sb("shc", [128, 19, 4])
    cvc = sb("cvc", [128, 10, 4, 3])
    ST_s = sb("ST_s", [128, 2, 768])
    STb_s = sb("STb_s", [128, 2, 768], BF16)

    arena = stack.enter_context(nc.sbuf_tensor("s_arena", [128, NREG * 512], F32))
    regs = [Res("reg%d" % i) for i in range(NREG)]
    free = [True] * NREG

    class Al:
        def __init__(self, r0, n):
            self.r0, self.n = r0, n
            self.regions = regs[r0:r0 + n]

        def f32(self, *shape):
            n = int(np.prod(shape))
            assert n <= self.n * 512
            ap = arena[:, self.r0 * 512:self.r0 * 512 + n]
            return self._shape(ap, shape)

        def bf(self, *shape):
            n = int(np.prod(shape))
            assert n <= self.n * 1024 and n % 2 == 0
            ap = arena[:, self.r0 * 512:self.r0 * 512 + n // 2].bitcast(BF16)
            return self._shape(ap, shape)

        @staticmethod
        def _shape(ap, shape):
            if len(shape) == 1:
                return ap
            if len(shape) == 2:
                return ap.rearrange("p (a b) -> p a b", a=shape[0])
            return ap.rearrange("p (a b c) -> p a b c", a=shape[0], b=shape[1])

        def free(self):
            for i in range(self.r0, self.r0 + self.n):
                assert not free[i]
                free[i] = True

    def alloc(n=1):
        c_ = CTX[ctx[0]]
        for r0 in range(c_["lo"], c_["hi"] - n + 1):
            if all(free[r0:r0 + n]):
                for i in range(r0, r0 + n):
                    free[i] = False
                return Al(r0, n)
        raise RuntimeError("arena full")

    def layer_tile(l, grp, src, dst, tok0, seqs, ts_, pos0, last):
        nseq = len(seqs)
        N = nseq * ts_
        PB = min(128, N)
        nb = N // PB
        L = min(64, ts_)
        nch = N // L
        nlev = int(np.log2(L))
        cps = ts_ // L
        rm = C["rm%d" % L]
        sell = C["sell%d" % L]

        def cslot(c):
            return seqs[c // cps][1]

        jk = alloc()
        junk = jk.bf(D)
        for b in range(nb):
            xb_ = alloc(2)
            hb_ = alloc()
            xb, hb = xb_.f32(D), hb_.bf(D)
            S.dma("sp", xb[:PB, :], src.h[tok0 + b * PB:tok0 + (b + 1) * PB, :], reads=[src], writes=[xb_])
            ACT(junk[:PB, :], xb[:PB, :], AF.Square, [xb_], [jk, (ssq, b)], accum=ssq[:PB, b:b + 1])
            TS(rstd[:PB, b:b + 1], ssq[:PB, b:b + 1], 1.0 / D, 1e-6, ALU.mult, ALU.add, [(ssq, b)], [(rstd, b)])
            ACT(rstd[:PB, b:b + 1], rstd[:PB, b:b + 1], AF.Sqrt, [(rstd, b)], [(rstd, b)])
            S.op("dve", lambda e, b=b: e.reciprocal(rstd[:PB, b:b + 1], rstd[:PB, b:b + 1]),
                 reads=[(rstd, b)], writes=[(rstd, b)])
            TS(hb[:PB, :], xb[:PB, :], rstd[:PB, b:b + 1], None, ALU.mult, None, [xb_, (rstd, b)], [hb_])
            for half in range(2):
                pt = PS()
                ptb = pt[:].bitcast(BF16)
                for q in range(4):
                    ft = half * 4 + q
                    TR(ptb[:, q * 128:q * 128 + PB], hb[:PB, ft * 128:(ft + 1) * 128], ident_bf[:PB, :PB],
                       [hb_, ident_bf], [pt])
                for q in range(4):
                    ft = half * 4 + q
                    TS(hT[:, ft, b * PB:(b + 1) * PB], ptb[:, q * 128:q * 128 + PB], pcol(l, "normw", ft), None,
                       ALU.mult, None, [pt, par[l]], [(hT, (ft, b))])
            xb_.free()
            hb_.free()
        jk.free()

        for c_ in CTX.values():
            c_["g"] = -1

        def proj(ct):
            c_ = CTX[ctx[0]]
            g = ct // 2
            if g != c_["g"]:
                wb = wbuf[c_["wb"]][c_["wbi"] % 2]
                c_["wbi"] += 1
                S.dma("sp", wb[:, :, :], w_in_bf.h[l, :, :, g * 256:(g + 1) * 256],
                      reads=[(w_in_bf, (l, k_)) for k_ in range(8)], writes=[wb])
                c_["g"], c_["cwb"] = g, wb
            wb = c_["cwb"]
            c = ct % 2
            pt = PS()
            for k in range(8):
                MM(pt[:, :N], wb[:, k, c * 128:(c + 1) * 128], hT[:, k, :N], k == 0, k == 7, [wb, hT], [pt])
            return pt

        def v3(ap):
            return ap.rearrange("p (s t) -> p s t", s=nseq)

        def shifted(ct, pt):
            o = alloc()
            ov = o.f32(N)
            mu = pcol(l, "mu", ct)
            om = omm[l][:, ct:ct + 1]
            TS(ov, pt[:, :N], om, None, ALU.mult, None, [pt, omm[l]], [o])
            if ts_ > 1:
                STT(v3(ov)[:, :, 1:], v3(pt[:, :N])[:, :, 0:ts_ - 1], mu, v3(ov)[:, :, 1:], ALU.mult, ALU.add,
                    [pt, par[l], o], [o])
            for si, (bi, slot) in enumerate(seqs):
                c0 = si * ts_
                STT(ov[:, c0:c0 + 1], shc[:, ct, slot:slot + 1], mu, ov[:, c0:c0 + 1], ALU.mult, ALU.add,
                    [(shc, (ct, slot)), par[l], o], [o])
                CP(shc[:, ct, slot:slot + 1], pt[:, c0 + ts_ - 1:c0 + ts_], [pt], [(shc, (ct, slot))], eng="dve")
            return o

        if cfg.stop == 'A':
            raise Stop()
        ctx[0] = "A"
        S.capture()
        wa_pt = proj(P_WA)
        wa = shifted(18, wa_pt)
        wab = alloc()
        wabv = wab.bf(N)
        ACT(wabv[0:64, :], wa.f32(N)[0:64, :], AF.Tanh, [wa], [wab])
        CP(wabv[64:128, :], wa.f32(N)[64:128, :], [wa], [wab], eng="dve")
        wa.free()
        lev = nlev
        if cfg.stop == 'WA':
            raise Stop()
        for m in range(6):
            r_ = shifted(m, proj(P_R(m)))
            k_ = shifted(6 + m, proj(P_K(m)))
            v_ = shifted(12 + m, proj(P_V(m)))
            gpt = proj(P_GA(m))
            sg = alloc()
            ACT(sg.f32(N), gpt[:, :N], AF.Silu, [gpt], [sg])
            rv, kv, vv = r_.f32(N), k_.f32(N), v_.f32(N)
            wps = PS()
            MM(wps[:, :N], w2a2b[l][0:64, m * 128:(m + 1) * 128], wabv[0:64, :], True, True, [w2a2b[l], wab], [wps])
            lw = alloc()
            ACT(lw.f32(N), wps[:, :N], AF.Sigmoid, [wps, par[l]], [lw], bias=pcol(l, "w0", m))
            TS(lw.f32(N), lw.f32(N), -float(np.exp(-0.5)), None, ALU.mult, None, [lw], [lw])
            aps = PS()
            MM(aps[:, :N], w2a2b[l][64:128, m * 128:(m + 1) * 128], wabv[64:128, :], True, True,
               [w2a2b[l], wab], [aps])
            a_ = alloc()
            ACT(a_.f32(N), aps[:, :N], AF.Sigmoid, [aps, par[l]], [a_], bias=pcol(l, "a0", m))
            kk = alloc()
            TS(kk.f32(N), kv, pcol(l, "k_k", m), None, ALU.mult, None, [k_, par[l]], [kk])
            sq = alloc()
            TT(sq.f32(N), kk.f32(N), kk.f32(N), ALU.mult, [kk], [sq])
            n2 = PS()
            MM(n2[:, :N], C["blk64"][:, :], sq.f32(N), True, True, [C["blk64"], sq], [n2])
            ACT(sq.f32(N), n2[:, :N], AF.Sqrt, [n2], [sq])
            TS(sq.f32(N), sq.f32(N), 1e-12, None, ALU.max, None, [sq], [sq])
            S.op("dve", lambda e, sq=sq: e.reciprocal(sq.f32(N), sq.f32(N)), reads=[sq], writes=[sq])
            TT(kk.f32(N), kk.f32(N), sq.f32(N), ALU.mult, [kk, sq], [kk])
            TS(sq.f32(N), a_.f32(N), -1.0, pcol(l, "k_a", m), ALU.add, ALU.mult, [a_, par[l]], [sq])
            kp = alloc()
            STT(kp.f32(N), sq.f32(N), 1.0, kv, ALU.add, ALU.mult, [sq, k_], [kp])
            STT(sq.f32(N), rv, pcol(l, "r_k", m), kp.f32(N), ALU.mult, ALU.mult, [r_, par[l], kp], [sq])
            rks = PS()
            MM(rks[:, :N], C["blk64"][:, :], sq.f32(N), True, True, [C["blk64"], sq], [rks])
            rkv = alloc()
            TT(rkv.f32(N), rks[:, :N], vv, ALU.mult, [rks, v_], [rkv])
            be = a_
            TT(be.f32(N), kk.f32(N), a_.f32(N), ALU.mult, [kk, a_], [be])
            cs = alloc()
            S.op("dve", lambda e, cs=cs, lw=lw: e.tensor_tensor_scan(cs.f32(N), rm[:, :N], lw.f32(N), 0.0,
                                                                     ALU.mult, ALU.add),
                 reads=[rm, lw], writes=[cs])
            TT(lw.f32(N), cs.f32(N), lw.f32(N), ALU.subtract, [cs, lw], [lw])
            e1 = alloc()
            ACT(e1.f32(N), cs.f32(N), AF.Exp, [cs], [e1])
            ACT(lw.f32(N), lw.f32(N), AF.Exp, [lw], [lw])
            ACT(cs.f32(N), cs.f32(N), AF.Exp, [cs], [cs], scale=-1.0)
            e2, e3 = cs, lw
            opb = alloc(7)
            opv = opb.bf(8, N)
            AbT, RbT, BtT, KtT, BgT, KgT, vbf = (opv[:, i, :] for i in range(7))
            blk_all = arena[:, (opb.r0 + 4) * 512:(opb.r0 + 7) * 512].bitcast(BF16)
            S.op("pool", lambda e, blk_all=blk_all: e.memset(blk_all, 0.0), writes=[opb])

            def blkv(i):
                return blk_all[:, i * 1024:i * 1024 + 2 * N].rearrange("p (c h t) -> p c h t", h=2, t=L)
            Ablk, Rblk, Bblk = blkv(0), blkv(1), blkv(2)

            def c3(ap):
                return ap.rearrange("p (c t) -> p c t", t=L)
            STT(AbT, kk.f32(N), -1.0, e3.f32(N), ALU.mult, ALU.mult, [kk, e3], [opb])
            TT(RbT, rv, e1.f32(N), ALU.mult, [r_, e1], [opb])
            TT(be.f32(N), be.f32(N), e2.f32(N), ALU.mult, [be, e2], [be])
            TT(kp.f32(N), kp.f32(N), e2.f32(N), ALU.mult, [kp, e2], [kp])
            CP(BtT, be.f32(N), [be], [opb])
            CP(KtT, kp.f32(N), [kp], [opb])
            for hh in range(2):
                rws = slice(hh * 64, hh * 64 + 64)
                CP(Ablk[rws, :, hh, :], c3(AbT)[rws], [opb], [opb], eng=("act" if hh else "dve"))
                CP(Rblk[rws, :, hh, :], c3(RbT)[rws], [opb], [opb], eng=("dve" if hh else "act"))
                CP(Bblk[rws, :, hh, :], c3(BtT)[rws], [opb], [opb], eng=("act" if hh else "dve"))
            eend = c3(e1.f32(N))[:, :, L - 1:L].to_broadcast([128, nch, L])
            TT(c3(BgT), c3(be.f32(N)), eend, ALU.mult, [be, e1], [opb])
            TT(c3(KgT), c3(kp.f32(N)), eend, ALU.mult, [kp, e1], [opb])
            CP(vbf, vv, [v_], [opb], eng="dve")
            for t_ in (r_, k_, v_, kk, sq, kp, be, e2, e3):
                t_.free()
            if cfg.stop == 'EW':
                raise Stop()
            yT = alloc()
            ncg = max(1, min(nch, 512 // (2 * L)))
            for ch0 in range(0, nch, ncg):
                ncc = min(ncg, nch - ch0)
                nbb = ncc * 2
                W = nbb * L
                mats = alloc(5)
                mv = mats.bf(10, 512)
                PT, P_, MakT, NrbT, NrkT, XT, P2, PT2 = (mv[:L, i, :W] for i in range(8))
                def first_stage(pi):
                    lh, rh = ((BtT, Ablk), (AbT, Bblk), (KtT, Ablk), (BtT, Rblk), (KtT, Rblk))[pi]
                    ps_ = PS()
                    for cl in range(ncc):
                        c = ch0 + cl
                        cols = slice(c * L, (c + 1) * L)
                        oc_ = slice(cl * 2 * L, (cl + 1) * 2 * L)
                        MM(ps_[:L, oc_], lh[:, cols], rh[:, c, :, :].rearrange("p h t -> p (h t)"), True, True,
                           [opb], [ps_])
                    return ps_

                def bm(name):
                    return C[name][:L, :L].unsqueeze(1).to_broadcast([L, nbb, L])

                def b3(ap):
                    return ap.rearrange("p (b t) -> p b t", t=L)
                ps_ = first_stage(0)
                TT(b3(PT), b3(ps_[:L, :W]), bm("mlt"), ALU.mult, [ps_, C["mlt"]], [mats])
                ps_ = first_stage(1)
                TT(b3(P_), b3(ps_[:L, :W]), bm("mgt"), ALU.mult, [ps_, C["mgt"]], [mats])
                ps_ = first_stage(2)
                TT(b3(MakT), b3(ps_[:L, :W]), bm("mlt"), ALU.mult, [ps_, C["mlt"]], [mats])
                ps_ = first_stage(3)
                TT(b3(NrbT), b3(ps_[:L, :W]), bm("mle"), ALU.mult, [ps_, C["mle"]], [mats])
                ps_ = first_stage(4)
                TT(b3(NrkT), b3(ps_[:L, :W]), bm("mle"), ALU.mult, [ps_, C["mle"]], [mats])
                TT(b3(XT), b3(PT), C["ident"][:L, :L].unsqueeze(1).to_broadcast([L, nbb, L]), ALU.add,
                   [mats, C["ident"]], [mats])
                pa, pta = P_, PT
                pb_, ptb_ = P2, PT2
                for k in range(1, lev):
                    psP, psQ, psX = PS(), PS(), PS()
                    for b in range(nbb):
                        oc_ = slice(b * L, (b + 1) * L)
                        MM(psP[:L, oc_], pta[:, oc_], pa[:, oc_], True, True, [mats], [psP])
                        if k < lev - 1:
                            MM(psQ[:L, oc_], pa[:, oc_], pta[:, oc_], True, True, [mats], [psQ])
                    CP(pb_, psP[:L, :W], [psP], [mats])
                    if k < lev - 1:
                        CP(ptb_, psQ[:L, :W], [psQ], [mats], eng="dve")
                    for b in range(nbb):
                        oc_ = slice(b * L, (b + 1) * L)
                        MM(psX[:L, oc_], pb_[:, oc_], XT[:, oc_], True, True, [mats], [psX])
                    TT(XT, psX[:L, :W], XT, ALU.add, [psX, mats], [mats])
                    pa, pta, pb_, ptb_ = pb_, ptb_, pa, pta
                if cfg.stop == 'LV':
                    raise Stop()
                tok = alloc(2)
                tkv = tok.bf(4, ncc * 128)
                Vtok, Bgtok, Kgtok = (tkv[:L, i, :].rearrange("p (c f) -> p c f", f=128) for i in (1, 2, 3))
                zb_ = alloc(2)
                Zb = zb_.bf(nbb, 128)
                for i, srcop in enumerate((AbT, vbf, BgT, KgT)):
                    pt = PS()
                    ptb = pt[:].bitcast(BF16)
                    for cl in range(ncc):
                        c = ch0 + cl
                        TR(ptb[:L, cl * 128:(cl + 1) * 128], srcop[:, c * L:(c + 1) * L], ident_bf[:, :],
                           [opb, ident_bf], [pt])
                    if i == 0:
                        CP(Zb[:L, :, 0:64], ptb[:L, :ncc * 128].rearrange("p (b j) -> p b j", j=64), [pt], [zb_])
                    else:
                        CP(tkv[:L, i, :], ptb[:L, :ncc * 128], [pt], [tok], eng=("dve" if i % 2 else "act"))
                psW = PS()
                for cl in range(ncc):
                    for hh in range(2):
                        b = cl * 2 + hh
                        MM(psW[:L, b * 64:(b + 1) * 64], MakT[:, b * L:(b + 1) * L], Vtok[:, cl, hh * 64:hh * 64 + 64],
                           True, True, [mats, tok], [psW])
                CP(Zb[:L, :, 64:128], psW[:L, :nbb * 64].rearrange("p (b j) -> p b j", j=64), [psW], [zb_], eng="dve")
                au_ = alloc(2)
                AU = au_.bf(nbb, 128)
                for b0 in range(0, nbb, 4):
                    nb4 = min(4, nbb - b0)
                    psZ = PS()
                    for b in range(b0, b0 + nb4):
                        MM(psZ[:L, (b - b0) * 128:(b - b0 + 1) * 128], XT[:, b * L:(b + 1) * L], Zb[:L, b, :],
                           True, True, [mats, zb_], [psZ])
                    CP(AU[:L, b0:b0 + nb4, :], psZ[:L, :nb4 * 128].rearrange("p (b j) -> p b j", j=128), [psZ], [au_])
                psG, psR = PS(), PS()
                for cl in range(ncc):
                    for hh in range(2):
                        b = cl * 2 + hh
                        rows = slice(hh * 64, hh * 64 + 64)
                        MM(psG[rows, cl * 64:(cl + 1) * 64], AU[:L, b, 0:64], Bgtok[:, cl, hh * 64:hh * 64 + 64],
                           True, True, [au_, tok], [psG])
                        MM(psR[rows, cl * L:(cl + 1) * L], AU[:L, b, 0:64], NrbT[:, b * L:(b + 1) * L],
                           True, True, [au_, mats], [psR])
                gr_ = alloc(2)
                Gblk = gr_.bf(4, 512)[:, 0, :ncc * 128].rearrange("p (c f) -> p c f", f=128)
                RhT = gr_.bf(4, 512)[:, 2, :ncc * L]
                S.op("pool", lambda e, gr_=gr_: e.memset(gr_.bf(4, 512)[:, 0, :], 0.0), writes=[gr_])
                for hh in range(2):
                    rws = slice(hh * 64, hh * 64 + 64)
                    CP(Gblk[rws, :, hh * 64:hh * 64 + 64], psG[rws, :ncc * 64].rearrange("p (c f) -> p c f", f=64),
                       [psG], [gr_], eng=("act" if hh else "dve"))
                TT(RhT, psR[:, :ncc * L], RbT[:, ch0 * L:(ch0 + ncc) * L], ALU.add, [psR, opb], [gr_])
                if cfg.stop == 'GR':
                    raise Stop()
                psY = PSL()
                for cl in range(ncc):
                    c = ch0 + cl
                    slot = cslot(c)
                    psS = PS()
                    for hh in range(2):
                        b = cl * 2 + hh
                        hc = slice(hh * 64, hh * 64 + 64)
                        yo = psY[:L, b * 64:(b + 1) * 64]
                        MM(yo, NrbT[:, b * L:(b + 1) * L], AU[:L, b, 64:128], b == 0, False, [mats, au_], [psY],
                           skip=True)
                        MM(yo, NrkT[:, b * L:(b + 1) * L], Vtok[:, cl, hc], False, False, [mats, tok], [psY],
                           skip=True)
                        so = psS[hc, 0:64]
                        MM(so, Bgtok[:, cl, hc], AU[:L, b, 64:128], True, False, [tok, au_], [psS])
                        MM(so, Kgtok[:, cl, hc], Vtok[:, cl, hc], False, False, [tok], [psS])
                    MM(psY[:L, cl * 128:(cl + 1) * 128], RhT[:, cl * L:(cl + 1) * L], STblk[:, slot, m, :], False, True,
                       [gr_, (STblk, (slot, m))], [psY], skip=True)
                    MM(psS[:, 0:64], Gblk[:, cl, :], STb_r[:, slot, m, :], False, True,
                       [gr_, (STb_r, (slot, m))], [psS])
                    cend = (c + 1) * L - 1
                    STT(ST_r[:, slot, m, :], ST_r[:, slot, m, :], e1.f32(N)[:, cend:cend + 1], psS[:, 0:64],
                        ALU.mult, ALU.add, [(ST_r, (slot, m)), e1, psS], [(ST_r, (slot, m))])
                    CP(STb_r[:, slot, m, :], ST_r[:, slot, m, :], [(ST_r, (slot, m))], [(STb_r, (slot, m))])
                    CP(STblk[0:64, slot, m, 0:64], ST_r[0:64, slot, m, :], [(ST_r, (slot, m))], [(STblk, (slot, m))],
                       eng="dve")
                    CP(STblk[64:128, slot, m, 64:128], ST_r[64:128, slot, m, :], [(ST_r, (slot, m))],
                       [(STblk, (slot, m))], eng="dve")
                if cfg.stop == 'SEQ':
                    raise Stop()
                gn = alloc(2)
                ysq = gn.f32(2, 512)[:L, 0, :nbb * 64]
                st4 = gn.f32(2, 512)[:L, 1, :]
                s1, s2, mean, rs_ = (st4[:, i * 16:i * 16 + nbb] for i in range(4))
                Y3 = psY[:L, :nbb * 64].rearrange("p (b i) -> p b i", i=64)
                S.op("dve", lambda e, s1=s1, Y3=Y3: e.reduce_sum(s1, Y3, AX.X), reads=[psY], writes=[gn])
                ACT(ysq, psY[:L, :nbb * 64], AF.Square, [psY], [gn])
                S.op("dve", lambda e, s2=s2, ysq=ysq: e.reduce_sum(s2, ysq.rearrange("p (b i) -> p b i", i=64), AX.X),
                     reads=[gn], writes=[gn])
                TS(mean, s1, 1.0 / 64, None, ALU.mult, None, [gn], [gn])
                TT(s1, mean, mean, ALU.mult, [gn], [gn])
                STT(s2, s2, 1.0 / 64, s1, ALU.mult, ALU.subtract, [gn], [gn])
                TS(s2, s2, GN_EPS, None, ALU.add, None, [gn], [gn])
                ACT(s2, s2, AF.Sqrt, [gn], [gn])
                S.op("dve", lambda e, rs_=rs_, s2=s2: e.reciprocal(rs_, s2), reads=[gn], writes=[gn])
                ysq3 = ysq.rearrange("p (b i) -> p b i", i=64)
                TT(ysq3, Y3, mean.unsqueeze(2).to_broadcast([L, nbb, 64]), ALU.subtract, [psY, gn], [gn])
                ynb = gn.bf(4, 512)[:L, 3, :nbb * 64]
                TT(ynb.rearrange("p (b i) -> p b i", i=64), ysq3, rs_.unsqueeze(2).to_broadcast([L, nbb, 64]),
                   ALU.mult, [gn], [gn])
                if cfg.debug and l == 0 and m == 0 and ch0 == 0:
                    dy = alloc(1)
                    CP(dy.f32(512)[:L, :nbb * 64], psY[:L, :nbb * 64], [psY], [dy])
                    dbg("yraw_%s%d" % (grp, tok0), dy, dy.f32(512)[:L, :nbb * 64], [L, nbb * 64])
                    dbg("ynb_%s%d" % (grp, tok0), gn, ynb, [L, nbb * 64], BF16)
                    dbg("st4_%s%d" % (grp, tok0), gn, st4[:, :64], [L, 64])
                    dy.free()
                pt = PS()
                ptb = pt[:].bitcast(BF16)
                for cl in range(ncc):
                    TR(ptb[:, cl * L:(cl + 1) * L], ynb[:, cl * 128:(cl + 1) * 128], ident_bf[:L, :L],
                       [gn, ident_bf], [pt])
                ACT(yT.f32(N)[:, ch0 * L:(ch0 + ncc) * L], ptb[:, :ncc * L], AF.Identity, [pt, par[l]], [yT],
                    bias=pcol(l, "ln_b", m), scale=pcol(l, "ln_w", m))
                for t_ in (mats, tok, zb_, au_, gr_, gn):
                    t_.free()
            if cfg.debug and l == 0 and m == 0:
                dbg("yT_%s%d" % (grp, tok0), yT, yT.f32(N), [128, N])
                dbg("rkv_%s%d" % (grp, tok0), rkv, rkv.f32(N), [128, N])
                dbg("sg_%s%d" % (grp, tok0), sg, sg.f32(N), [128, N])
            TT(yT.f32(N), yT.f32(N), rkv.f32(N), ALU.add, [yT, rkv], [yT])
            TT(oT[:, m, :N], yT.f32(N), sg.f32(N), ALU.mult, [yT, sg], [(oT, m)])
            for t_ in (yT, rkv, sg, e1, opb):
                t_.free()
        wab.free()
        if cfg.debug and l == 0:
            dbg("oa_%s%d" % (grp, tok0), oT, oT[:, 0:6, :N], [128, 6, N], BF16)
        if cfg.stop == 'OA':
            raise Stop()
        if last:
            for si, (bi, slot) in enumerate(seqs):
                so_ = alloc(2)
                sov = so_.f32(6, 128)
                for half in range(2):
                    pt = PS()
                    for q in range(3):
                        mm_ = half * 3 + q
                        TR(pt[:64, q * 128:(q + 1) * 128], ST_r[:, slot, mm_, :], ident[:, :],
                           [(ST_r, (slot, mm_)), ident], [pt])
                    CP(sov[:64, half * 3:half * 3 + 3, :], pt[:64, :384].rearrange("p (m f) -> p m f", f=128),
                       [pt], [so_])
                S.dma("pool", o_rwkv[grp].h[l, bi].rearrange("(m hh) i j -> i m hh j", hh=2),
                      sov[:64, :, :].rearrange("p m (hh j) -> p m hh j", hh=2), reads=[so_], writes=[o_rwkv[grp]])
                so_.free()
                S.dma("pool", o_shift[grp].h[l, bi].rearrange("(c p) -> p c", p=128), shc[:, :, slot],
                      reads=[shc], writes=[o_shift[grp]], allow_slow_non_contiguous=True)

        opsA = S.end_capture()
        ctx[0] = "B"
        S.capture()

        def c3(ap):
            return ap.rearrange("p (c t) -> p c t", t=L)
        szb = alloc(3)
        szT = szb.bf(6, N)
        for m in range(6):
            pt = proj(CT_ZB + m)
            ACT(szT[:, m, :], pt[:, :N], AF.Silu, [pt], [szb])
        xsb = alloc(6)
        xs_f = xsb.f32(6, N)
        bcb = alloc(2)
        BC = bcb.bf(4, N)
        for j in range(10):
            pt = proj(CT_XS + j)
            acc = alloc()
            av = acc.f32(N)
            cw = [pcol(l, "cw%d" % i, j) for i in range(4)]
            TS(av, pt[:, :N], cw[3], pcol(l, "cb", j), ALU.mult, ALU.add, [pt, par[l]], [acc])
            for d_ in (1, 2, 3):
                STT(v3(av)[:, :, d_:], v3(pt[:, :N])[:, :, 0:ts_ - d_], cw[3 - d_], v3(av)[:, :, d_:], ALU.mult, ALU.add,
                    [pt, par[l], acc], [acc])
            for si, (bi, slot) in enumerate(seqs):
                c0 = si * ts_
                for d_ in (3, 2, 1):
                    STT(av[:, c0:c0 + d_], cvc[:, j, slot, 3 - d_:3], cw[3 - d_], av[:, c0:c0 + d_], ALU.mult, ALU.add,
                        [(cvc, (j, slot)), par[l], acc], [acc])
                CP(cvc[:, j, slot, :], pt[:, c0 + ts_ - 3:c0 + ts_], [pt], [(cvc, (j, slot))], eng="dve")
            if j < 6:
                ACT(xs_f[:, j, :], av, AF.Silu, [acc], [xsb])
            else:
                ACT(BC[:, j - 6, :], av, AF.Silu, [acc], [bcb])
            acc.free()
        pt = proj(CT_DT)
        dtb_ = alloc()
        dtT = dtb_.f32(N)
        ACT(dtT[0:12, :], pt[0:12, :N], AF.Exp, [pt, par[l]], [dtb_], bias=pcol(l, "dtb")[0:12, :])
        ACT(dtT[0:12, :], dtT[0:12, :], AF.Ln, [dtb_], [dtb_], bias=1.0)
        acb = alloc()
        acs = acb.f32(N)
        dab = alloc()
        TS(dab.f32(N)[0:12, :], dtT[0:12, :], aneg[l][0:12, :], None, ALU.mult, None, [dtb_, aneg[l]], [dab])
        S.op("dve", lambda e: e.tensor_tensor_scan(acs[0:12, :], rm[0:12, :N], dab.f32(N)[0:12, :], 0.0, ALU.mult, ALU.add),
             reads=[rm, dab], writes=[acb])
        dab.free()
        xdb = alloc(3)
        xdt = xdb.bf(6, N)
        for m in range(6):
            psd = PS()
            MM(psd[:, :N], C["e12"][0:12, m * 128:(m + 1) * 128], dtT[0:12, :], True, True, [C["e12"], dtb_], [psd])
            TT(xdt[:, m, :], xs_f[:, m, :], psd[:, :N], ALU.mult, [xsb, psd], [xdb])
        dtb_.free()
        ysb = alloc(6)
        yS = ysb.f32(6, N)
        for m in range(6):
            TS(yS[:, m, :], xs_f[:, m, :], pcol(l, "Dv", m), None, ALU.mult, None, [xsb, par[l]], [ysb])
        xsb.free()
        cpb = alloc(6)
        Cp = cpb.bf(12, N)
        for h in range(12):
            psb_ = PS()
            MM(psb_[:, :N], C["sel12"][0:12, h * 128:(h + 1) * 128], acs[0:12, :], True, True, [C["sel12"], acb], [psb_])
            ea = alloc()
            ACT(ea.f32(N), psb_[:, :N], AF.Exp, [psb_], [ea])
            TT(Cp[:, h, :], BC[:, 2 + h // 6, :], ea.f32(N), ALU.mult, [bcb, ea], [cpb])
            ea.free()
        for c in range(nch):
            slot = cslot(c) % 2
            cols = slice(c * L, (c + 1) * L)
            if grp == "s" and c % cps == 0:
                load_ssm(l, seqs[c // cps][0], slot)
            pt = PS()
            ptb = pt[:].bitcast(BF16)
            for m in range(6):
                TR(ptb[:L, m * 128:(m + 1) * 128], xdt[:, m, cols], ident_bf[:, :], [xdb, ident_bf], [pt])
            for g in range(2):
                TR(ptb[:L, 768 + g * 128:768 + (g + 1) * 128], BC[:, g, cols], ident_bf[:, :], [bcb, ident_bf], [pt])
            tkb = alloc()
            tokc = tkb.bf(1024)
            CP(tokc[:L, :], ptb[:L, :1024], [pt], [tkb])
            pa = PS()
            TR(pa[:L, 0:12], acs[0:12, cols], ident[0:12, 0:12], [acb, ident], [pa])
            atb = alloc()
            at = atb.f32(512)
            CP(at[:L, 0:12], pa[:L, 0:12], [pa], [atb], eng="dve")
            pe_ = PS()
            MM(pe_[:L, 0:12], sell[:L, :L], at[:L, 0:12], True, True, [sell, atb], [pe_])
            MM(pe_[:, 16:28], sell[:L, :], at[:L, 0:12], True, True, [sell, atb], [pe_])
            TT(at[:L, 16:28], pe_[:L, 0:12], at[:L, 0:12], ALU.subtract, [pe_, atb], [atb])
            ACT(at[:L, 32:44], at[:L, 16:28], AF.Exp, [atb], [atb])
            ACT(at[:, 48:60], pe_[:, 16:28], AF.Exp, [pe_], [atb])
            TS(at[:L, 64:76], at[:L, 0:12], -1.0, None, ALU.mult, None, [atb], [atb])
            gtb = alloc()
            GT = gtb.bf(12, L)
            for g in range(2):
                pq = PS()
                for hl in range(6):
                    h = g * 6 + hl
                    MM(pq[:L, hl * L:(hl + 1) * L], C["sel12"][0:12, h * 128:h * 128 + L], acs[0:12, cols], True, True,
                       [C["sel12"], acb], [pq])
                sgb = alloc()
                sg3 = sgb.f32(6, L)
                TT(sg3[:L], pq[:L, :6 * L].rearrange("p (h q) -> p h q", q=L),
                   at[:L, 64 + g * 6:64 + g * 6 + 6].unsqueeze(2).to_broadcast([L, 6, L]), ALU.add, [pq, atb], [sgb])
                TT(sg3[:L], sg3[:L], C["nmle"][:L, :L].unsqueeze(1).to_broadcast([L, 6, L]), ALU.add,
                   [sgb, C["nmle"]], [sgb])
                ACT(sg3[:L], sg3[:L], AF.Exp, [sgb], [sgb])
                pcb = PS()
                MM(pcb[:L, 0:L], BC[:, g, cols], BC[:, 2 + g, cols], True, True, [bcb], [pcb])
                TT(GT[:L, g * 6:(g + 1) * 6, :], sg3[:L], pcb[:L, 0:L].unsqueeze(1).to_broadcast([L, 6, L]), ALU.mult,
                   [sgb, pcb], [gtb])
                sgb.free()
            xeb = alloc()
            xdte = xeb.bf(768)
            TT(xdte[:L, :].rearrange("p (h q) -> p h q", q=64), tokc[:L, 0:768].rearrange("p (h q) -> p h q", q=64),
               at[:L, 32:44].unsqueeze(2).to_broadcast([L, 12, 64]), ALU.mult, [tkb, atb], [xeb])
            psy = PS()
            for m in range(6):
                for hh in range(2):
                    h = m * 2 + hh
                    hc = slice(h * 64, (h + 1) * 64)
                    out = psy[hh * 64:hh * 64 + 64, m * L:(m + 1) * L]
                    MM(out, tokc[:L, hc], GT[:L, h, :], m == 0, False, [tkb, gtb], [psy], skip=True)
                    MM(out, STb_s[:, slot, hc], Cp[:, h, cols], False, True, [(STb_s, slot), cpb], [psy], skip=True)
            TT(yS[:, :, cols], yS[:, :, cols], psy[:, :6 * L].rearrange("p (m q) -> p m q", q=L), ALU.add,
               [psy, ysb], [ysb])
            for g in range(2):
                pst = PS()
                MM(pst[:, :384], tokc[:L, 768 + g * 128:768 + (g + 1) * 128], xdte[:L, g * 384:(g + 1) * 384], True, True,
                   [tkb, xeb], [pst])
                stv = ST_s[:, slot, g * 384:(g + 1) * 384]
                TT(stv.rearrange("p (h q) -> p h q", q=64), stv.rearrange("p (h q) -> p h q", q=64),
                   at[:, 48 + g * 6:48 + g * 6 + 6].unsqueeze(2).to_broadcast([128, 6, 64]), ALU.mult,
                   [(ST_s, slot), atb], [(ST_s, slot)])
                TT(stv, stv, pst[:, :384], ALU.add, [(ST_s, slot), pst], [(ST_s, slot)])
            CP(STb_s[:, slot, :], ST_s[:, slot, :], [(ST_s, slot)], [(STb_s, slot)])
            for t_ in (tkb, atb, gtb, xeb):
                t_.free()
            if last and (c + 1) % cps == 0:
                bi = seqs[c // cps][0]
                sso = alloc(2)
                ssv = sso.f32(6, 128)
                for half in range(2):
                    pt = PS()
                    for q in range(3):
                        mm_ = half * 3 + q
                        TR(pt[:, q * 128:(q + 1) * 128], ST_s[:, slot, mm_ * 128:(mm_ + 1) * 128], ident[:, :],
                           [(ST_s, slot), ident], [pt])
                    CP(ssv[:, half * 3:half * 3 + 3, :], pt[:, :384].rearrange("p (m f) -> p m f", f=128), [pt], [sso])
                S.dma("pool", o_ssm[grp].h[l, bi].rearrange("(m hh) p n -> (hh p) m n", hh=2), ssv,
                      reads=[sso], writes=[o_ssm[grp]])
                sso.free()
                for r_i in range(3):
                    S.dma("pool", o_conv[grp].h[l, bi, r_i].rearrange("(j p) -> p j", p=128), cvc[:, :, cslot(c), r_i],
                          reads=[cvc], writes=[o_conv[grp]], allow_slow_non_contiguous=True)
        for t_ in (acb, xdb, cpb, bcb):
            t_.free()
        for m in range(6):
            TT(yS[:, m, :], yS[:, m, :], szT[:, m, :], ALU.mult, [ysb, szb], [ysb])
        szb.free()
        for g in range(2):
            pss_ = PS()
            for q in range(3):
                m = g * 3 + q
                sqb = alloc()
                TT(sqb.f32(N), yS[:, m, :], yS[:, m, :], ALU.mult, [ysb], [sqb])
                MM(pss_[:, :N], C["ones"][:, :], sqb.f32(N), q == 0, q == 2, [C["ones"], sqb], [pss_])
                sqb.free()
            rsb = alloc()
            TS(rsb.f32(N), pss_[:, :N], 1.0 / 384, 1e-5, ALU.mult, ALU.add, [pss_], [rsb])
            ACT(rsb.f32(N), rsb.f32(N), AF.Sqrt, [rsb], [rsb])
            S.op("dve", lambda e, rsb=rsb: e.reciprocal(rsb.f32(N), rsb.f32(N)), reads=[rsb], writes=[rsb])
            for q in range(3):
                m = g * 3 + q
                STT(oT[:, 6 + m, :N], yS[:, m, :], pcol(l, "snw", m), rsb.f32(N), ALU.mult, ALU.mult,
                    [ysb, par[l], rsb], [(oT, 6 + m)])
            rsb.free()
        ysb.free()
        if cfg.debug and l == 0:
            dbg("ob_%s%d" % (grp, tok0), oT, oT[:, 6:12, :N], [128, 6, N], BF16)
        if cfg.stop == 'SSD':
            raise Stop()
        NQ = ts_ if grp == "s" else N
        blk_len = min(128, ts_)
        nblk = ts_ // blk_len
        qzb = [alloc(2), alloc(2)]
        QTz = [qzb[i].bf(4, N) for i in range(2)]
        for i in range(2):
            S.op("pool", lambda e, i=i: e.memset(qzb[i].bf(4 * N), 0.0), writes=[qzb[i]])

        def headnorm(pt, wname):
            sq_ = alloc()
            ACT(sq_.f32(N), pt[:, :N], AF.Square, [pt], [sq_])
            ss_ = PS()
            MM(ss_[:, :N], C["blk64"][:, :], sq_.f32(N), True, True, [C["blk64"], sq_], [ss_])
            TS(sq_.f32(N), ss_[:, :N], 1.0 / 64, 1e-6, ALU.mult, ALU.add, [ss_], [sq_])
            ACT(sq_.f32(N), sq_.f32(N), AF.Sqrt, [sq_], [sq_])
            S.op("dve", lambda e, sq_=sq_: e.reciprocal(sq_.f32(N), sq_.f32(N)), reads=[sq_], writes=[sq_])
            STT(sq_.f32(N), pt[:, :N], pcol(l, wname), sq_.f32(N), ALU.mult, ALU.mult, [pt, par[l], sq_], [sq_])
            return sq_

        for m in range(4):
            qn = headnorm(proj(CT_Q + m), "qnw")
            for hh in range(2):
                rws = slice(hh * 64, hh * 64 + 64)
                TS(QTz[hh][rws, m, :], qn.f32(N)[rws, :], 0.125, None, ALU.mult, None, [qn], [qzb[hh]])
            qn.free()

        def tok_out(fm_al, m, odram, to_v):
            for si, (bi, slot) in enumerate(seqs):
                for jb in range(nblk):
                    c0 = si * ts_ + jb * blk_len
                    pt = PS()
                    TR(pt[:blk_len, 0:128], fm_al.f32(N)[:, c0:c0 + blk_len], ident[:, :], [fm_al, ident], [pt])
                    tk_ = alloc()
                    CP(tk_.f32(128)[:blk_len, :], pt[:blk_len, 0:128], [pt], [tk_], eng=("dve" if jb % 2 else "act"))
                    t0 = pos0 + jb * blk_len if grp == "p" else jb * blk_len
                    S.dma("pool", odram.h[l, bi, 2 * m:2 * m + 2, t0:t0 + blk_len, :].rearrange("h t d -> t h d"),
                          tk_.f32(128)[:blk_len, :].rearrange("p (h d) -> p h d", d=64), reads=[tk_], writes=[odram])
                    if to_v:
                        kb = (pos0 + jb * blk_len) // 128 if grp == "p" else PAST // 128
                        vb_ = alloc()
                        CP(vb_.bf(128)[:blk_len, :], tk_.f32(128)[:blk_len, :], [tk_], [vb_], eng="dve")
                        S.dma("pool", vscr.h[:blk_len, (si if grp == "s" else 0) * 0 + kb + (si if grp == "s" else 0),
                                             m * 128:(m + 1) * 128],
                              vb_.bf(128)[:blk_len, :], reads=[vb_], writes=[(vscr, ("new", si, m))])
                        vb_.free()
                    tk_.free()

        for m in range(4):
            kn = headnorm(proj(CT_SK + m), "knw")
            tok_out(kn, m, o_k[grp], False)
            kb_ = alloc()
            CP(kb_.bf(N), kn.f32(N), [kn], [kb_], eng="dve")
            for si, (bi, slot) in enumerate(seqs):
                kp0 = pos0 if grp == "p" else PAST + si * 128
                S.dma("pool", ktscr.h[:, m, kp0:kp0 + ts_], kb_.bf(N)[:, si * ts_:(si + 1) * ts_], reads=[kb_],
                      writes=[(ktscr, ("new", si, m))])
            kb_.free()
            kn.free()
        for m in range(4):
            pt = proj(CT_SV + m)
            vf_ = alloc()
            CP(vf_.f32(N), pt[:, :N], [pt], [vf_])
            tok_out(vf_, m, o_v[grp], True)
            vf_.free()
        if cfg.debug and l == 0:
            dbg("qtz0_%s%d" % (grp, tok0), qzb[0], QTz[0], [128, 4, N], BF16)
            dbg("qtz1_%s%d" % (grp, tok0), qzb[1], QTz[1], [128, 4, N], BF16)
        sgb_ = alloc(4)
        sgc = sgb_.f32(4, N)
        for m in range(4):
            pt = proj(CT_GC + m)
            ACT(sgc[:, m, :], pt[:, :N], AF.Silu, [pt], [sgb_])
        mlt_bf, nuinc_bf, nones_bf = Cb["mlt"], Cb["nuinc"], Cb["nones"]
        for si, (bi, slot) in enumerate(seqs):
            qc0 = si * ts_
            if grp == "p":
                npast = pos0 // 128
                blocks = [("d", npast + o, 128, o * 128) for o in range(nblk - 1, -1, -1)] + \
                         [("p", kb, 128, 0) for kb in range(npast - 1, -1, -1)]
                nkb_tot = npast + nblk
                klen_tot = nkb_tot * 128
                vrow = 0
            else:
                load_kv_cache(l, bi)
                npast = PAST // 128
                blocks = [("d", npast + si, ts_, 0)] + [("p", kb, 128, 0) for kb in range(npast - 1, -1, -1)]
                nkb_tot = npast + NSS
                klen_tot = nkb_tot * 128
            for m in range(4):
                nreg_k = (klen_tot * 2 + 2047) // 2048
                ktm_ = alloc(nreg_k)
                KTm = ktm_.bf(klen_tot)
                S.dma("sp", KTm, ktscr.h[:, m, 0:klen_tot], reads=[ktscr], writes=[ktm_])
                nreg_v = (nkb_tot * 128 * 2 + 2047) // 2048
                vm_ = alloc(nreg_v)
                Vm = vm_.bf(nkb_tot, 128)
                S.dma("sp", Vm, vscr.h[:, 0:nkb_tot, m * 128:(m + 1) * 128], reads=[vscr], writes=[vm_])
                ops_ = PSL()
                accs = [alloc(), alloc()]
                for hh in range(2):
                    S.op("pool", lambda e, a_=accs[hh]: e.memset(a_.bf(NQ), 0.0), writes=[accs[hh]])
                def sb_s1(hh, blk):
                    kind, kb, klen, qo = blk
                    nq = NQ - qo
                    qsl = slice(qc0 + qo, qc0 + NQ)
                    ksl = slice(kb * 128, kb * 128 + klen)
                    zps = PS()
                    MM(zps[:klen, :nq], KTm[:, ksl], QTz[hh][:, m, qsl], True, True, [ktm_, qzb[hh]], [zps])
                    e_ = alloc()
                    ACT(e_.f32(NQ)[:klen, :nq], zps[:klen, :nq], AF.Exp, [zps], [e_])
                    sp_ = alloc()
                    SP = sp_.bf(NQ)
                    ACT(SP[:klen, :nq], e_.f32(NQ)[:klen, :nq], AF.Ln, [e_], [sp_], bias=1.0)
                    e_.free()
                    nm_ = min(128, nq)
                    if kind == "d":
                        TT(SP[:klen, :nm_], SP[:klen, :nm_], mlt_bf[:klen, :nm_], ALU.mult, [sp_, mlt_bf], [sp_])
                    return sp_

                def sb_s2(hh, blk, sp_, first):
                    kind, kb, klen, qo = blk
                    acc_ = accs[hh]
                    ACC = acc_.bf(NQ)
                    SP = sp_.bf(NQ)
                    nq = NQ - qo
                    qsl = slice(qc0 + qo, qc0 + NQ)
                    ksl = slice(kb * 128, kb * 128 + klen)
                    nm_ = min(128, nq)
                    tps = PS()
                    MM(tps[:klen, :nq], nuinc_bf[:klen, :klen], SP[:klen, :nq], True, False, [nuinc_bf, sp_], [tps])
                    if not first:
                        MM(tps[:klen, :nq], nones_bf[:, :klen], ACC[:, qo:NQ], False, False, [nones_bf, acc_], [tps])
                    MM(tps[:klen, :nq], KTm[:, ksl], QTz[hh][:, m, qsl], False, True, [ktm_, qzb[hh]], [tps])
                    at_ = alloc()
                    ATT = at_.bf(NQ)
                    ACT(ATT[:klen, :nq], tps[:klen, :nq], AF.Exp, [tps], [at_])
                    if kind == "d":
                        TT(ATT[:klen, :nm_], ATT[:klen, :nm_], mlt_bf[:klen, :nm_], ALU.mult, [at_, mlt_bf], [at_])
                    MM(ops_[hh * 64:hh * 64 + 64, qo:NQ], Vm[:klen, kb, hh * 64:hh * 64 + 64], ATT[:klen, :nq],
                       first, False, [vm_, at_], [ops_], skip=True)
                    TT(ACC[:klen, qo:NQ], ACC[:klen, qo:NQ], SP[:klen, :nq], ALU.add, [acc_, sp_], [acc_], eng="pool")
                    at_.free()
                    sp_.free()

                LOOK = 2
                pend = [[], []]
                for j_ in range(min(LOOK, len(blocks))):
                    for hh in range(2):
                        pend[hh].append(sb_s1(hh, blocks[j_]))
                for k_, blk in enumerate(blocks):
                    for hh in range(2):
                        if k_ + LOOK < len(blocks):
                            pend[hh].append(sb_s1(hh, blocks[k_ + LOOK]))
                        sb_s2(hh, blk, pend[hh].pop(0), k_ == 0)
                for a_ in accs:
                    a_.free()
                if cfg.debug and l == 0 and m == 0 and si == 0:
                    do_ = alloc()
                    CP(do_.f32(NQ), ops_[:, :NQ], [ops_], [do_], eng="dve")
                    dbg("oraw_%s%d" % (grp, tok0), do_, do_.f32(NQ), [128, NQ])
                    do_.free()
                TT(oT[:, 12 + m, qc0:qc0 + NQ], ops_[:, :NQ], sgc[:, m, qc0:qc0 + NQ], ALU.mult, [ops_, sgb_],
                   [(oT, 12 + m)])
                ktm_.free()
                vm_.free()
        for t_ in (qzb[0], qzb[1], sgb_):
            t_.free()
        if cfg.debug and l == 0:
            dbg("oc_%s%d" % (grp, tok0), oT, oT[:, 12:16, :N], [128, 4, N], BF16)
        if cfg.stop == 'SB':
            raise Stop()
        opsB = S.end_capture()
        ctx[0] = "G"
        S.merge([opsA, opsB])
        wo_ = alloc(16)
        wov = wo_.bf(16, D)
        for c4 in range(0, 16, 4):
            S.dma("sp", wov[:, c4:c4 + 4, :], w_out_bf.h[l, :, c4:c4 + 4, :], reads=[(w_out_bf, (l, c4))], writes=[wo_])
        for b in range(nb):
            xr_ = alloc(2)
            xr = xr_.f32(D)
            S.dma("sp", xr[:PB, :], src.h[tok0 + b * PB:tok0 + (b + 1) * PB, :], reads=[src], writes=[xr_])
            for nh in range(2):
                pso = PS()
                for c in range(16):
                    MM(pso[:PB, :512], oT[:, c, b * PB:(b + 1) * PB], wov[:, c, nh * 512:(nh + 1) * 512], c == 0, c == 15,
                       [oT, wo_], [pso])
                TT(xr[:PB, nh * 512:(nh + 1) * 512], xr[:PB, nh * 512:(nh + 1) * 512], pso[:PB, :512], ALU.add,
                   [xr_, pso], [xr_])
            S.dma("pool", dst.h[tok0 + b * PB:tok0 + (b + 1) * PB, :], xr[:PB, :], reads=[xr_], writes=[dst])
            if cfg.debug and l == 0 and b == 0:
                dbg("y_%s%d" % (grp, tok0), xr_, xr[:PB, :], [PB, D])
            xr_.free()
        wo_.free()
        return N

    def load_ssm(l, bi, slot):
        si_ = alloc(2)
        sv = si_.f32(6, 128)
        S.dma("sp", sv, st_ssm.h[l, bi].rearrange("(m hh) p n -> (hh p) m n", hh=2), reads=[st_ssm], writes=[si_])
        for half in range(2):
            pt = PS()
            for q in range(3):
                m = half * 3 + q
                TR(pt[:, q * 128:(q + 1) * 128], sv[:, m, :], ident[:, :], [si_, ident], [pt])
            CP(ST_s[:, slot, half * 384:(half + 1) * 384], pt[:, :384], [pt], [(ST_s, slot)])
        CP(STb_s[:, slot, :], ST_s[:, slot, :], [(ST_s, slot)], [(STb_s, slot)], eng="dve")
        si_.free()

    def load_rwkv_states(l):
        for slot in range(NSS):
            si_ = alloc(2)
            sv = si_.f32(6, 128)
            S.dma("sp", sv[:64].rearrange("p m (hh j) -> p m hh j", hh=2),
                  st_rwkv.h[l, slot].rearrange("(m hh) i j -> i m hh j", hh=2), reads=[st_rwkv], writes=[si_])
            for half in range(2):
                pt = PS()
                for q in range(3):
                    m = half * 3 + q
                    TR(pt[:, q * 64:(q + 1) * 64], sv[:64, m, :], ident[:64, :64], [si_, ident], [pt])
                CP(ST_r[:, slot, half * 3:half * 3 + 3, :], pt[:, :192].rearrange("p (m i) -> p m i", i=64), [pt],
                   [ST_r])
            si_.free()
            CP(STb_r[:, slot, :, :], ST_r[:, slot, :, :], [ST_r], [STb_r], eng="dve")
            CP(STblk[0:64, slot, :, 0:64], ST_r[0:64, slot, :, :], [ST_r], [STblk], eng="dve")
            CP(STblk[64:128, slot, :, 64:128], ST_r[64:128, slot, :, :], [ST_r], [STblk], eng="pool")
            S.dma("sp", shc[:, :, slot], st_shift.h[l, slot].rearrange("(c p) -> p c", p=128), reads=[st_shift],
                  writes=[shc], allow_slow_non_contiguous=True)
            for r_i in range(3):
                S.dma("sp", cvc[:, :, slot, r_i], st_conv.h[l, slot, r_i].rearrange("(j p) -> p j", p=128),
                      reads=[st_conv], writes=[cvc], allow_slow_non_contiguous=True)

    def load_kv_cache(l, bi):
        npast = PAST // 128
        for h in range(8):
            S.dma("pool", vscr.h[:, 0:npast, h * 64:(h + 1) * 64],
                  cv_d.h[l, bi, h].rearrange("(kb p) d -> p kb d", p=128), reads=[cv_d], writes=[(vscr, "cache")])
        kc_ = alloc(4)
        kct = kc_.bf(npast, 512)
        for h in range(8):
            S.dma("pool", kct[:, :, h * 64:(h + 1) * 64], ck_d.h[l, bi, h].rearrange("(kb p) d -> p kb d", p=128),
                  reads=[ck_d], writes=[kc_])
        for m in range(4):
            for kb0 in range(0, npast, 4):
                pt = PS()
                ptb = pt[:].bitcast(BF16)
                for q in range(4):
                    TR(ptb[:, q * 128:(q + 1) * 128], kct[:, kb0 + q, m * 128:(m + 1) * 128], ident_bf[:, :],
                       [kc_, ident_bf], [pt])
                kt_ = alloc()
                CP(kt_.bf(512), ptb[:, :512], [pt], [kt_], eng=("dve" if (kb0 // 4) % 2 else "act"))
                S.dma("pool", ktscr.h[:, m, kb0 * 128:(kb0 + 4) * 128], kt_.bf(512), reads=[kt_],
                      writes=[(ktscr, "cache")])
                kt_.free()
        kc_.free()

    try:
        for l in range(cfg.nlayers):
            lastl = (l == DEPTH - 1)
            src = xp if l == 0 else xmid
            dst = yp if lastl else xmid
            for s_ in range(NSP):
                for t_ in (ST_r, STb_r, STblk, shc, cvc, ST_s, STb_s):
                    S.op("dve", lambda e, t_=t_: e.memset(t_[:], 0.0), writes=[t_])
                nt = TP // 512
                for ti in range(nt):
                    layer_tile(l, "p", src, dst, s_ * TP + ti * 512, [(s_, 0)], 512, ti * 512, ti == nt - 1)
            if cfg.do_sample:
                load_rwkv_states(l)
                layer_tile(l, "s", xs if l == 0 else xmid_s, ys if lastl else xmid_s, 0,
                           [(i, i) for i in range(NSS)], TSS, PAST, True)
    except Stop:
        pass

    S.finalize(stack)
    stack.close()
    return nc, dbg_outs


OUT_NAMES = ["yp", "ys", "p_rwkv", "p_shift", "p_ssm", "p_conv", "p_k", "p_v",
             "s_rwkv", "s_shift", "s_ssm", "s_conv", "s_k", "s_v"]


def core_inputs(inp, cfg, core, shared):
    NSP, TP, NSS = cfg.nseq_p, cfg.t_p, cfg.nseq_s
    im = dict(shared)
    im["xp"] = np.ascontiguousarray(np.asarray(inp["x_prompt"])[core * NSP:(core + 1) * NSP].reshape(NSP * TP, D))
    im["xs"] = np.ascontiguousarray(np.asarray(inp["x_sample"])[core * NSS:(core + 1) * NSS].reshape(NSS * 16, D))
    sl = slice(core * NSS, (core + 1) * NSS)
    im["st_rwkv"] = np.ascontiguousarray(np.asarray(inp["state_rwkv"])[:, sl])
    im["st_shift"] = np.ascontiguousarray(np.asarray(inp["state_rwkv_shift"])[:, sl, 0])
    im["st_ssm"] = np.ascontiguousarray(np.asarray(inp["state_ssm"])[:, sl])
    im["st_conv"] = np.ascontiguousarray(np.asarray(inp["state_conv"])[:, sl])
    im["ck"] = np.ascontiguousarray(np.asarray(inp["cache_sb_k"])[:, sl])
    im["cv"] = np.ascontiguousarray(np.asarray(inp["cache_sb_v"])[:, sl])
    return im


def shared_inputs(inp):
    cm = col_map()
    w_in = np.asarray(inp["w_in"], np.float32)
    w_in_p = np.zeros((DEPTH, D, NP_), np.float32)
    w_in_p[:, :, cm >= 0] = w_in[:, :, cm[cm >= 0]]
    sh = {}
    sh["w_in"] = np.ascontiguousarray(w_in_p.reshape(DEPTH, 8, 128, NP_).transpose(0, 2, 1, 3))
    sh["w_out"] = np.ascontiguousarray(np.asarray(inp["w_out"], np.float32).reshape(DEPTH, 16, 128, D).transpose(0, 2, 1, 3))
    sh["w2a2"] = np.ascontiguousarray(np.concatenate([np.asarray(inp["rwkv_w2"]), np.asarray(inp["rwkv_a2"])], axis=1))
    sh["params"] = np.stack([build_params(inp, l) for l in range(DEPTH)])
    for k, v in make_consts().items():
        sh["c_" + k] = v
    return sh


def gather(results, cfg, ncores):
    NSP, TP, NSS = cfg.nseq_p, cfg.t_p, cfg.nseq_s
    outs = []
    for nm in OUT_NAMES:
        parts = [np.asarray(results[c][nm]) for c in range(ncores)]
        if nm == "yp":
            o = np.concatenate([p.reshape(NSP, TP, D) for p in parts], axis=0)
        elif nm == "ys":
            o = np.concatenate([p.reshape(NSS, 16, D) for p in parts], axis=0)
        else:
            o = np.concatenate(parts, axis=1)
            if nm.endswith("_shift"):
                o = o[:, :, None, :]
        outs.append(np.ascontiguousarray(o.astype(np.float32)))
    return tuple(outs)


_CACHE = {}


def kernel(**inputs):
    ncores = 8
    B = np.asarray(inputs["x_prompt"]).shape[0]
    T = np.asarray(inputs["x_prompt"]).shape[1]
    BS = np.asarray(inputs["x_sample"]).shape[0]
    cfg = Cfg(nseq_p=B // ncores, t_p=T, nseq_s=BS // ncores)
    key = (cfg.nseq_p, cfg.t_p, cfg.nseq_s)
    if key not in _CACHE:
        _CACHE[key] = build(cfg)[0]
    nc = _CACHE[key]
    shared = shared_inputs(inputs)
    in_maps = [core_inputs(inputs, cfg, c, shared) for c in range(ncores)]
    res = run_bass_kernel_spmd(nc, in_maps, core_ids=list(range(ncores)))
    return gather(res.results, cfg, ncores)
```
